# Optimizing a Trainium2 kernel written in Bass

```python
import math
import jax, jax.numpy as jnp
from jax import lax
import numpy as np

D_MODEL = 1024
BATCH = 2
SEQ = 8192
DEPTH = 2

N_MIXERS = 2
N_CONV_LAYERS = (DEPTH + N_MIXERS - 1) // N_MIXERS
N_ATTN_LAYERS = DEPTH // N_MIXERS
CONV_WIDTH = 3
N_HEADS = 16
N_KV_HEADS = 4
HEAD_DIM = D_MODEL // N_HEADS
GROUP = N_HEADS // N_KV_HEADS
WINDOW = 128
BLOCK = 128
NEG_INF = -1e30
N_BUCKETS = 32
MAX_DISTANCE = 128
PEER_HEADS = 8
N_KEYS = 128
N_EXPERTS = N_KEYS * N_KEYS
PEER_TOPK = 16
QUERY_DIM = 256
SUB_DIM = QUERY_DIM // 2
PEER_CHUNK = 128
RMS_EPS = 1e-6

kernel_name = "hybrid_conv_swa_peer_encoder"


def _rmsnorm(x, g):
    xf = x.astype(jnp.float32)
    y = xf * lax.rsqrt(jnp.mean(xf * xf, axis=-1, keepdims=True) + RMS_EPS)
    return (y * g.astype(jnp.float32)).astype(x.dtype)


def _short_conv_mixer(x, w_in, w_conv, w_out):
    gate_b, gate_c, h = jnp.split(x @ w_in, 3, axis=-1)
    y = lax.conv_general_dilated(
        gate_c * h, w_conv[:, None, :].astype(h.dtype),
        window_strides=(1,), padding=((CONV_WIDTH // 2, CONV_WIDTH // 2),),
        dimension_numbers=("NWC", "WIO", "NWC"), feature_group_count=D_MODEL)
    return (gate_b * y) @ w_out


def _t5_bucket(rel):
    half = N_BUCKETS // 2
    max_exact = half // 2
    ret = jnp.where(rel > 0, half, 0)
    n = jnp.abs(rel)
    nf = jnp.maximum(n, 1).astype(jnp.float32)
    large = max_exact + (jnp.log(nf / max_exact) / math.log(MAX_DISTANCE / max_exact)
                         * (half - max_exact)).astype(jnp.int32)
    large = jnp.minimum(large, half - 1)
    return ret + jnp.where(n < max_exact, n, large)


def _windowed_gqa(x, w_qkv, sink, w_o, rel_bias):
    bsz, s, _ = x.shape
    nb = s // BLOCK
    qkv = x @ w_qkv
    q = qkv[..., :N_HEADS * HEAD_DIM].reshape(bsz, nb, BLOCK, N_KV_HEADS, GROUP, HEAD_DIM)
    k = qkv[..., N_HEADS * HEAD_DIM:(N_HEADS + N_KV_HEADS) * HEAD_DIM].reshape(bsz, s, N_KV_HEADS, HEAD_DIM)
    v = qkv[..., (N_HEADS + N_KV_HEADS) * HEAD_DIM:].reshape(bsz, s, N_KV_HEADS, HEAD_DIM)

    def band(t):
        tp = jnp.pad(t, ((0, 0), (BLOCK, BLOCK), (0, 0), (0, 0)))
        tb = tp.reshape(bsz, nb + 2, BLOCK, N_KV_HEADS, HEAD_DIM)
        return jnp.concatenate([tb[:, :-2], tb[:, 1:-1], tb[:, 2:]], axis=2)

    kw, vw = band(k), band(v)
    scores = jnp.einsum("bnqhgd,bnkhd->bnhgqk", q, kw).astype(jnp.float32) / math.sqrt(HEAD_DIM)

    qi = jnp.arange(BLOCK)[:, None]
    kj = jnp.arange(3 * BLOCK)[None, :]
    rel = kj - BLOCK - qi
    bias = rel_bias[_t5_bucket(rel)].astype(jnp.float32)
    bias = jnp.transpose(bias, (2, 0, 1)).reshape(N_KV_HEADS, GROUP, BLOCK, 3 * BLOCK)
    kpos = jnp.arange(nb)[:, None] * BLOCK - BLOCK + jnp.arange(3 * BLOCK)[None, :]
    valid = (jnp.abs(rel) <= WINDOW)[None] & ((kpos >= 0) & (kpos < s))[:, None, :]
    logits = jnp.where(valid[None, :, None, None], scores + bias, NEG_INF)

    sink_logit = jnp.broadcast_to(sink.astype(jnp.float32).reshape(N_KV_HEADS, GROUP, 1, 1),
                                  logits.shape[:-1] + (1,))
    probs = jax.nn.softmax(jnp.concatenate([logits, sink_logit], axis=-1), axis=-1)[..., :-1]
    out = jnp.einsum("bnhgqk,bnkhd->bnqhgd", probs.astype(vw.dtype), vw)
    return out.reshape(bsz, s, N_HEADS * HEAD_DIM) @ w_o


def _peer(xn, w_q, subkeys, u_tab, v_tab):
    bsz, s, d = xn.shape
    t = xn.reshape(-1, d)
    n_tok = t.shape[0]
    q = (t @ w_q).reshape(n_tok, PEER_HEADS, 2, SUB_DIM)
    sc = jnp.einsum("thpd,hpnd->thpn", q, subkeys).astype(jnp.float32)
    sv, si = lax.top_k(sc, PEER_TOPK)
    cand = (sv[:, :, 0, :, None] + sv[:, :, 1, None, :]).reshape(n_tok, PEER_HEADS, PEER_TOPK * PEER_TOPK)
    cidx = (si[:, :, 0, :, None] * N_KEYS + si[:, :, 1, None, :]).reshape(n_tok, PEER_HEADS, PEER_TOPK * PEER_TOPK)
    top_v, pos = lax.top_k(cand, PEER_TOPK)
    idx = jnp.take_along_axis(cidx, pos, axis=-1)
    g = jax.nn.softmax(top_v, axis=-1)

    n_chunks = n_tok // PEER_CHUNK
    n_sel = PEER_HEADS * PEER_TOPK

    def expert_block(args):
        xc, ic, gc = args
        u = jnp.take(u_tab, ic, axis=0)
        h = jnp.einsum("cd,ced->ce", xc, u)
        a = gc.astype(xc.dtype) * jax.nn.gelu(h, approximate=False)
        return jnp.einsum("ce,ced->cd", a, jnp.take(v_tab, ic, axis=0))

    out = lax.map(expert_block, (t.reshape(n_chunks, PEER_CHUNK, d),
                                 idx.reshape(n_chunks, PEER_CHUNK, n_sel),
                                 g.reshape(n_chunks, PEER_CHUNK, n_sel)))
    return out.reshape(bsz, s, d)


def setup_inputs(seed: int = 0) -> dict:
    key = jax.random.key(seed)
    ks = jax.random.split(key, 20)
    D = D_MODEL
    nrm = lambda k, shape, scale: jax.random.normal(k, shape, jnp.float32) * scale
    qkv_cols = (N_HEADS + 2 * N_KV_HEADS) * HEAD_DIM
    return {
        "x": nrm(ks[0], (BATCH, SEQ, D), 1.0),
        "conv_norm_g": 1.0 + nrm(ks[1], (N_CONV_LAYERS, D), 0.02),
        "conv_w_in": nrm(ks[2], (N_CONV_LAYERS, D, 3 * D), D ** -0.5),
        "conv_w": nrm(ks[3], (N_CONV_LAYERS, CONV_WIDTH, D), CONV_WIDTH ** -0.5),
        "conv_w_out": nrm(ks[4], (N_CONV_LAYERS, D, D), D ** -0.5),
        "attn_norm_g": 1.0 + nrm(ks[5], (N_ATTN_LAYERS, D), 0.02),
        "attn_w_qkv": nrm(ks[6], (N_ATTN_LAYERS, D, qkv_cols), D ** -0.5),
        "attn_sink": nrm(ks[7], (N_ATTN_LAYERS, N_HEADS), 0.5),
        "attn_w_o": nrm(ks[8], (N_ATTN_LAYERS, N_HEADS * HEAD_DIM, D), (N_HEADS * HEAD_DIM) ** -0.5),
        "rel_bias": nrm(ks[9], (N_BUCKETS, N_HEADS), 0.1),
        "ffn_norm_g": 1.0 + nrm(ks[10], (DEPTH, D), 0.02),
        "peer_w_q": nrm(ks[11], (DEPTH, D, PEER_HEADS * QUERY_DIM), D ** -0.5),
        "peer_subkeys": nrm(ks[12], (DEPTH, PEER_HEADS, 2, N_KEYS, SUB_DIM), SUB_DIM ** -0.5),
        "peer_u": nrm(ks[13], (DEPTH, N_EXPERTS, D), D ** -0.5),
        "peer_v": nrm(ks[14], (DEPTH, N_EXPERTS, D), D ** -0.5),
        "final_norm_g": 1.0 + nrm(ks[15], (D,), 0.02),
    }


def reference(x, conv_norm_g, conv_w_in, conv_w, conv_w_out, attn_norm_g, attn_w_qkv,
              attn_sink, attn_w_o, rel_bias, ffn_norm_g, peer_w_q, peer_subkeys,
              peer_u, peer_v, final_norm_g):
    for i in range(DEPTH):
        j = i // N_MIXERS
        if i % N_MIXERS == 0:
            x = x + _short_conv_mixer(_rmsnorm(x, conv_norm_g[j]), conv_w_in[j], conv_w[j], conv_w_out[j])
        else:
            x = x + _windowed_gqa(_rmsnorm(x, attn_norm_g[j]), attn_w_qkv[j], attn_sink[j],
                                  attn_w_o[j], rel_bias)
        x = x + _peer(_rmsnorm(x, ffn_norm_g[i]), peer_w_q[i], peer_subkeys[i], peer_u[i], peer_v[i])
    return _rmsnorm(x, final_norm_g)
```

```python
import contextlib
import os
import numpy as np
import concourse.bass as bass
import concourse.mybir as mybir
from concourse.bass_utils import run_bass_kernel_spmd

F32 = mybir.dt.float32
BF16 = mybir.dt.bfloat16
U32 = mybir.dt.uint32
ALU = mybir.AluOpType
AF = mybir.ActivationFunctionType
AX = mybir.AxisListType

D = 1024
EPS = 1e-6
SAME_ENGINE_SYNC = os.environ.get("K_SES", "1") == "1"
NBUF_G = int(os.environ.get("K_NB", "10"))
PE_DOT = int(os.environ.get("K_PEDOT", "0"))
JG = int(os.environ.get("K_JG", "1"))
RG_ENG = os.environ.get("K_RG", "dve")
FRONT_PULL = 3


class Op:
    __slots__ = ("eng", "fn", "deps", "lane", "count", "signaled")

    def __init__(self, eng, fn, deps, lane):
        self.eng = eng
        self.fn = fn
        self.deps = deps
        self.lane = lane
        self.count = None
        self.signaled = False


class Buf:
    __slots__ = ("w", "r")

    def __init__(self):
        self.w = {}
        self.r = {}


class Prog:
    ENGS = ("pe", "act", "dve", "pool", "sp")

    def __init__(self, nc):
        self.nc = nc
        self.ops = []
        self.lane_last = {}
        self.deferred = None

    def call(self, fn):
        if self.deferred is not None:
            self.deferred.append(("none", fn))
        else:
            fn()

    def raw(self, eng, fn, deps=(), lane=None):
        deps = [d for d in deps if d is not None]
        if lane is not None:
            prev = self.lane_last.get(lane)
            if prev is not None:
                deps.append(prev)
        o = Op(eng, fn, deps, lane)
        if lane is not None:
            self.lane_last[lane] = o
        self.ops.append(o)
        return o

    def op(self, eng, fn, R=(), W=(), lane=None, extra=()):
        if self.deferred is not None:
            self.deferred.append((eng, lambda: self._op(eng, fn, R, W, lane, extra)))
            return None
        return self._op(eng, fn, R, W, lane, extra)

    def _op(self, eng, fn, R=(), W=(), lane=None, extra=()):
        deps = list(extra)
        for b in R:
            deps.extend(b.w.values())
        for b in W:
            deps.extend(b.w.values())
            deps.extend(b.r.values())
        o = self.raw(eng, fn, deps, lane)
        key = ("l", lane) if lane is not None else ("e", eng)
        for b in R:
            b.r[key] = o
        for b in W:
            b.w = {key: o}
            b.r = {}
        return o

    def fence(self):
        last = {}
        for o in self.ops:
            key = ("l", o.lane) if o.lane is not None else ("e", o.eng)
            last[key] = o
        return list(last.values())

    def build(self):
        nc = self.nc
        for o in self.ops:
            nd = []
            seen = set()
            for d in o.deps:
                if id(d) in seen:
                    continue
                seen.add(id(d))
                if d.lane is None and d.eng == o.eng and o.lane is None:
                    if d.eng == "pe" or not SAME_ENGINE_SYNC:
                        continue
                nd.append(d)
            o.deps = nd
            for d in nd:
                d.signaled = True
        lanes = sorted({o.lane for o in self.ops if o.lane is not None})
        with contextlib.ExitStack() as es:
            esem = {e: es.enter_context(nc.semaphore("s_" + e)) for e in self.ENGS}
            lsem = {l: es.enter_context(nc.semaphore("l_" + str(l))) for l in lanes}
            ecount = {e: 0 for e in self.ENGS}
            lcount = {l: 0 for l in lanes}
            for o in self.ops:
                if o.lane is not None:
                    lcount[o.lane] += 16
                    o.count = lcount[o.lane]
                elif o.signaled:
                    ecount[o.eng] += 1
                    o.count = ecount[o.eng]
            self.final_counts = (dict(ecount), dict(lcount))
            block = es.enter_context(nc.Block())
            ops = self.ops

            def emit_for(engname):
                def body(eng):
                    waited = {}
                    for o in ops:
                        if o.eng != engname:
                            continue
                        need = {}
                        for d in o.deps:
                            key = ("l", d.lane) if d.lane is not None else ("e", d.eng)
                            if d.count > need.get(key, 0):
                                need[key] = d.count
                        for key, cnt in need.items():
                            if waited.get(key, 0) >= cnt:
                                continue
                            sem = lsem[key[1]] if key[0] == "l" else esem[key[1]]
                            eng.wait_ge(sem, cnt)
                            waited[key] = cnt
                        ins = o.fn(eng)
                        if o.lane is not None:
                            ins.then_inc(lsem[o.lane], 16)
                        elif o.signaled:
                            ins.then_inc(esem[o.eng], 1)
                return body

            block.tensor(emit_for("pe"))
            block.scalar(emit_for("act"))
            block.vector(emit_for("dve"))
            block.gpsimd(emit_for("pool"))
            block.sync(emit_for("sp"))


def _dsize(dt):
    return {F32: 4, BF16: 2, U32: 4}[dt]


class Arena:
    def __init__(self, nc, start, top):
        self.nc = nc
        self.p = start
        self.top = top
        self.n = 0

    def alloc(self, name, shape, dt):
        nbytes = int(np.prod(shape[1:])) * _dsize(dt)
        off = (self.p + 31) // 32 * 32
        self.p = off + nbytes
        assert self.p <= self.top, (name, self.p, self.top)
        self.n += 1
        return self.nc.alloc_sbuf_tensor_at(name, list(shape), dt, offset=off)

    def fork(self):
        return Arena(self.nc, self.p, self.top)


def build(NO=16, upto=99, dbg=False):
    NT1 = NO + 2
    NX = NO + 4
    nc = bass.Bass("TRN2", target_bir_lowering=False)

    def dr(name, shape, dt=F32, kind="ExternalInput"):
        return nc.dram_tensor(name, list(shape), dt, kind=kind).ap()

    x_ext = dr("x_ext", [NX * 128, D])
    pen_d = dr("pen", [1, NT1 * 128])
    w_in_d = dr("w_in", [D, 3 * D])
    cw_d = dr("cw", [128, 24])
    w_out_d = dr("w_out", [D, D])
    w_att_d = dr("w_att", [D, 1792])
    w_o_d = dr("w_o", [D, D])
    sink_d = dr("sink", [1, 16])
    bias_d = dr("bias_tab", [128, 16 * 384])
    wmask_d = dr("wmask", [128, 384])
    gains_d = dr("gains", [5, D])
    w_pq_d = dr("w_pq", [2, D, 2048])
    skT_d = dr("skT", [2, 128, 2048])
    uv_d = [dr("uv0", [16384, 2048]), dr("uv1", [16384, 2048])]
    iota_d = dr("iota", [128, 16])
    uvb_d = [nc.dram_tensor("uvb%d" % l, [16384, 2048], BF16, kind="Internal").ap() for l in range(2)]
    y_d = dr("y", [NO * 128, D], kind="ExternalOutput")
    if dbg:
        dbg_d = dr("dbg", [NT1 * 128, D], kind="ExternalOutput")

    P = Prog(nc)
    op = P.op

    with contextlib.ExitStack() as es:
        pbank = [es.enter_context(nc.psum_tensor("pb%d" % i, [128, 512], F32)) for i in range(8)]
        pbB = [Buf() for _ in range(8)]
        pbB5b = Buf()
        pbank_bf = [p.bitcast(BF16) for p in pbank]

        A0 = Arena(nc, (nc.sbuf_base + 63) // 64 * 64, nc.sbuf_top)
        xres = A0.alloc("xres", [128, NT1, D], F32)
        xresB = [Buf() for _ in range(NT1)]
        ident = A0.alloc("ident", [128, 128], BF16)
        identB = Buf()
        identf = A0.alloc("identf", [128, 128], F32)
        identfB = Buf()
        iota16 = A0.alloc("iota16", [128, 16], F32)
        iotaB = Buf()
        gA = A0.alloc("gA", [128, D], F32)
        gAB = Buf()
        gB = A0.alloc("gB", [128, D], F32)
        gBB = Buf()
        stat = A0.alloc("stat", [128, 96], F32)
        statB = [Buf() for _ in range(96)]
        junk = A0.alloc("junk", [128, D], BF16)
        junkB = Buf()
        junk2_off = (A0.p + 31) // 32 * 32
        junk2 = A0.alloc("junk2", [128, D], BF16)
        junk3 = A0.alloc("junk3", [128, D], BF16)
        junk2s = [junk2, junk3]
        junk2B = [Buf(), Buf()]
        xnb = A0.alloc("xnb", [128, D], BF16)
        xnbB = Buf()
        xnT = A0.alloc("xnT", [128, 8, 128], BF16)
        xnTB = Buf()

        stat_ctr = [0]

        def new_stat():
            k = stat_ctr[0] % 96
            stat_ctr[0] += 1
            return stat[:, k:k + 1], statB[k]

        op("pool", lambda e: e.memset(ident[:], 1.0), W=[identB])
        op("pool", lambda e: e.affine_select(out=ident[:], in_=ident[:], pattern=[[-1, 128]],
                                             compare_op=ALU.is_equal, fill=0.0, base=0,
                                             channel_multiplier=1), R=[identB], W=[identB])
        op("pool", lambda e: e.tensor_copy(out=identf[:], in_=ident[:]), R=[identB], W=[identfB])
        op("sp", lambda e: e.dma_start(out=iota16[:], in_=iota_d[:, :]), W=[iotaB], lane="c_iota")

        def load_gain(dst, dstB, row):
            return op("sp", lambda e: e.dma_start(out=dst[:], in_=gains_d[row:row + 1, :].partition_broadcast(128)),
                      W=[dstB], lane="gain")

        def emit_rstd(src_ap, srcB):
            ss, ssB = new_stat()
            op("act", lambda e: e.activation(out=junk[:], in_=src_ap, func=AF.Square, accum_out=ss),
               R=[srcB], W=[ssB, junkB])
            sd, sdB = new_stat()
            op("act", lambda e: e.activation(out=sd, in_=ss, func=AF.Sqrt, bias=EPS_AP[0], scale=1.0 / D),
               R=[ssB, epsB], W=[sdB])
            r, rB = new_stat()
            op("dve", lambda e: e.reciprocal(out=r, in_=sd), R=[sdB], W=[rB])
            return r, rB

        def emit_transpose8(src_bf, srcB_, dst_ap, dstB_, bank=0):
            pv = pbank_bf[bank][:, 0:1024].rearrange("p (a b) -> p a b", a=8)
            for dc in range(8):
                op("pe", lambda e, dc=dc: e.transpose(out=pv[:, dc, :], in_=src_bf[:, dc * 128:(dc + 1) * 128],
                                                      identity=ident[:]),
                   R=(list(srcB_) if isinstance(srcB_, (list, tuple)) else [srcB_]) + [identB], W=[pbB[bank]])
            op("act", lambda e: e.copy(out=dst_ap, in_=pv), R=[pbB[bank]], W=[dstB_])

        epsT = A0.alloc("epsT", [128, 1], F32)
        epsB = Buf()
        EPS_AP = [epsT[:, 0:1]]
        op("pool", lambda e: e.memset(epsT[:], EPS), W=[epsB])

        AP0 = A0

        NCV = 16
        RCV = 16384 // NCV
        uvbB = [Buf(), Buf()]

        conv_left = {0: list(range(NCV)), 1: list(range(NCV))}

        def emit_convert(layer, nchunks=NCV):
            for _ in range(nchunks):
                if not conv_left[layer]:
                    break
                k = conv_left[layer].pop(0)
                op("pool", lambda e, k=k: e.dma_start(out=uvb_d[layer][k * RCV:(k + 1) * RCV, :],
                                                      in_=uv_d[layer][k * RCV:(k + 1) * RCV, :]),
                   W=[], lane="cv%d" % k)
            if not conv_left[layer]:
                uvbB[layer].w = {("l", "cv%d" % q): P.lane_last["cv%d" % q] for q in range(NCV)}

        if upto >= 1:
            A = AP0.fork()
            xtmp = nc.alloc_sbuf_tensor_at("xtmp", [128, D], F32, offset=junk2_off)
            xtmpB = Buf()
            xnb2 = A.alloc("xnb2", [128, D], BF16)
            xnb2B = Buf()
            win = A.alloc("win", [128, 8, 3 * D], BF16)
            winB = Buf()
            wout = A.alloc("wout", [128, 8, D], BF16)
            woutB = Buf()
            cw = A.alloc("cw", [128, 8, 3], F32)
            cwB = Buf()
            xnT_all = A.alloc("xnT_all", [128, 8, NX * 128], BF16)
            xnT_allB = [Buf() for _ in range(NX)]
            hsb = [A.alloc("hsb%d" % i, [128, 130], F32) for i in range(2)]
            hsbB = [Buf() for _ in range(2)]
            zsb = [A.alloc("zsb%d" % i, [128, 130], F32) for i in range(2)]
            zsbB = [Buf() for _ in range(2)]
            ysb = [A.alloc("ysb%d" % i, [128, 128], F32) for i in range(2)]
            ysbB = [Buf() for _ in range(2)]
            gT = [A.alloc("gT%d" % i, [128, 8, 128], BF16) for i in range(2)]
            gTB = [[Buf() for _ in range(8)] for _ in range(2)]

            for k in range(6):
                op("pool", lambda e, k=k: e.dma_start(
                    out=win[:, :, k * 512:(k + 1) * 512],
                    in_=w_in_d[:, k * 512:(k + 1) * 512].rearrange("(dc dp) n -> dp dc n", dp=128)),
                   W=[winB], lane="w%d" % (k % 4))
            op("pool", lambda e: e.dma_start(out=wout[:], in_=w_out_d.rearrange("(dc dp) n -> dp dc n", dp=128)),
               W=[woutB], lane="w1")
            op("sp", lambda e: e.dma_start(out=cw[:], in_=cw_d.rearrange("p (c k) -> p c k", k=3)), W=[cwB], lane="c_cw")
            load_gain(gA, gAB, 0)

            for e_ in range(NX):
                if e_ >= 4 and e_ % 2 == 0:
                    emit_convert(0, 1)
                if 1 <= e_ <= NT1:
                    dst, dB = xres[:, e_ - 1, :], xresB[e_ - 1]
                else:
                    dst, dB = xtmp[:], xtmpB
                op("sp", lambda e, e_=e_, dst=dst: e.dma_start(out=dst, in_=x_ext[e_ * 128:(e_ + 1) * 128, :]),
                   W=[dB], lane="xl%d" % (e_ % 4))
                r, rB = emit_rstd(dst, dB)
                xb_, xbB_ = (xnb, xnbB) if e_ % 2 == 0 else (xnb2, xnb2B)
                op("dve", lambda e, dst=dst, r=r, xb_=xb_: e.scalar_tensor_tensor(out=xb_[:], in0=dst, scalar=r, in1=gA[:],
                                                                                 op0=ALU.mult, op1=ALU.mult),
                   R=[dB, rB, gAB], W=[xbB_])
                emit_transpose8(xb_, xbB_, xnT_all[:, :, e_ * 128:(e_ + 1) * 128], xnT_allB[e_], bank=(0 if e_ % 2 == 0 else 7))

            for i in range(NT1):
                emit_convert(0, 1)
                e_ = i + 1
                c0 = 128 * e_ - 1
                par = i % 2
                for cc in range(8):
                    q = cc % 2
                    bk = 1 + q
                    psB = pbank[bk][:, 0:130]
                    psC = pbank[bk][:, 130:260]
                    psH = pbank[bk][:, 260:390]
                    for wi, pso in enumerate((psB, psC, psH)):
                        for dc in range(8):
                            op("pe", lambda e, wi=wi, pso=pso, dc=dc, cc=cc, c0=c0: e.matmul(
                                pso, lhsT=win[:, dc, wi * D + cc * 128: wi * D + (cc + 1) * 128],
                                rhs=xnT_all[:, dc, c0:c0 + 130], start=(dc == 0), stop=(dc == 7)),
                               R=[winB, xnT_allB[e_ - 1], xnT_allB[e_], xnT_allB[e_ + 1]], W=[pbB[bk]])
                    op("act", lambda e, q=q, psH=psH: e.copy(out=hsb[q][:], in_=psH), R=[pbB[bk]], W=[hsbB[q]])
                    op("dve", lambda e, q=q, psC=psC: e.tensor_tensor(out=zsb[q][:], in0=psC, in1=hsb[q][:], op=ALU.mult),
                       R=[pbB[bk], hsbB[q]], W=[zsbB[q]])
                    op("dve", lambda e, q=q, cc=cc: e.tensor_scalar(out=ysb[q][:], in0=zsb[q][:, 0:128],
                                                                   scalar1=cw[:, cc, 0:1], scalar2=None, op0=ALU.mult),
                       R=[zsbB[q], cwB], W=[ysbB[q]])
                    for kk in (1, 2):
                        op("dve", lambda e, q=q, cc=cc, kk=kk: e.scalar_tensor_tensor(
                            out=ysb[q][:], in0=zsb[q][:, kk:kk + 128], scalar=cw[:, cc, kk:kk + 1], in1=ysb[q][:],
                            op0=ALU.mult, op1=ALU.add),
                           R=[zsbB[q], cwB, ysbB[q]], W=[ysbB[q]])
                    op("dve", lambda e, q=q, cc=cc, par=par, psB=psB: e.tensor_tensor(
                        out=gT[par][:, cc, :], in0=psB[:, 1:129], in1=ysb[q][:], op=ALU.mult),
                       R=[pbB[bk], ysbB[q]], W=[gTB[par][cc]])
                for half in range(2):
                    bk = 3 + half
                    for cc in range(8):
                        op("pe", lambda e, half=half, cc=cc, par=par, bk=bk: e.matmul(
                            pbank[bk][:, :], lhsT=gT[par][:, cc, :], rhs=wout[:, cc, half * 512:(half + 1) * 512],
                            start=(cc == 0), stop=(cc == 7)),
                           R=[gTB[par][cc], woutB], W=[pbB[bk]])
                    op("dve", lambda e, half=half, i=i, bk=bk: e.tensor_tensor(
                        out=xres[:, i, half * 512:(half + 1) * 512], in0=xres[:, i, half * 512:(half + 1) * 512],
                        in1=pbank[bk][:, :], op=ALU.add),
                       R=[pbB[bk], xresB[i]], W=[xresB[i]])

        def emit_peer(layer, tiles, final=False):
            fence = P.fence()
            A = AP0.fork()
            wq = A.alloc("wq%d" % layer, [128, 8, 2048], BF16)
            wqB = Buf()
            sk = A.alloc("sk%d" % layer, [128, 16, 128], BF16)
            skB = Buf()
            xn = [A.alloc("xn%d_%d" % (layer, i), [128, D], F32) for i in range(2)]
            xnB = [Buf() for _ in range(2)]
            qT = A.alloc("qT%d" % layer, [128, 16, 128], BF16)
            qTBs = [Buf() for _ in range(4)]
            sc = A.alloc("sc%d" % layer, [128, 16, 128], F32)
            scBs = [Buf() for _ in range(4)]
            wk = A.alloc("wk%d" % layer, [128, 128], F32)
            wkB = Buf()
            mx = A.alloc("mx%d" % layer, [128, 16, 16], F32)
            mxB = Buf()
            ixu = A.alloc("ixu%d" % layer, [128, 16, 16], U32)
            ixuB = Buf()
            ixf = A.alloc("ixf%d" % layer, [128, 16, 16], F32)
            ixfB = Buf()
            cand = A.alloc("cand%d" % layer, [128, 256], F32)
            candB = Buf()
            cwk = A.alloc("cwk%d" % layer, [128, 256], F32)
            cwkB = Buf()
            tv = A.alloc("tv%d" % layer, [128, 8, 16], F32)
            tvB = Buf()
            pos = A.alloc("pos%d" % layer, [128, 8, 16], U32)
            posB = Buf()
            pa = A.alloc("pa%d" % layer, [128, 8, 16], U32)
            paB = Buf()
            pb_ = A.alloc("pb%d_" % layer, [128, 8, 16], U32)
            pbB_ = Buf()
            paf = A.alloc("paf%d" % layer, [128, 8, 16], F32)
            pafB = Buf()
            pbf = A.alloc("pbf%d" % layer, [128, 8, 16], F32)
            pbfB = Buf()
            s0 = A.alloc("s0%d" % layer, [128, 8, 16], F32)
            s0B = Buf()
            s1 = A.alloc("s1%d" % layer, [128, 8, 16], F32)
            s1B = Buf()
            idxf = A.alloc("idxf%d" % layer, [128, 128], F32)
            idxfB = Buf()
            idxu = [A.alloc("idxu%d_%d" % (layer, i), [128, 128], U32) for i in range(2)]
            idxuB = [Buf() for _ in range(2)]
            ntv = A.alloc("ntv%d" % layer, [128, 8], F32)
            ntvB = Buf()
            ee = A.alloc("ee%d" % layer, [128, 8, 16], F32)
            eeB = Buf()
            zz = A.alloc("zz%d" % layer, [128, 8], F32)
            zzB = [Buf() for _ in range(8)]
            rz = A.alloc("rz%d" % layer, [128, 8], F32)
            rzB = Buf()
            gg = [A.alloc("gg%d_%d" % (layer, i), [128, 128], F32) for i in range(2)]
            ggB = [Buf() for _ in range(2)]
            hh = A.alloc("hh%d" % layer, [128, 2, 128], F32)
            hhB = [[Buf() for _ in range(128)] for _ in range(2)]
            gl = A.alloc("gl%d" % layer, [128, 2, 128], F32)
            glB = [[Buf() for _ in range(128 // JG)] for _ in range(2)]
            aa = A.alloc("aa%d" % layer, [128, 2, 128], F32)
            aaB = [[Buf() for _ in range(128 // JG)] for _ in range(2)]
            aaB2 = [[Buf() for _ in range(128 // JG)] for _ in range(2)]
            dd = A.alloc("dd%d" % layer, [128, 2 * JG, 128], BF16)
            ddB = [Buf() for _ in range(2 * JG)]
            gbuf = [A.alloc("gb%d_%d" % (layer, i), [128, 2048], BF16) for i in range(NBUF_G)]
            gbufB = [Buf() for _ in range(NBUF_G)]
            yt = None
            xnT2 = A.alloc("xnT2_%d" % layer, [128, 8, 128], BF16)
            xnTp = [xnT, xnT2]
            xnTpB = [xnTB, Buf()]
            ugT = [A.alloc("ugT%d_%d" % (layer, i), [128, 8, 128], BF16) for i in range(2)] if PE_DOT > 0 else None
            ugTB = [Buf() for _ in range(2)]
            pv7 = pbank_bf[7][:, 0:1024].rearrange("p (a b) -> p a b", a=8)
            tb = sc

            for k in range(4):
                op("pool", lambda e, k=k: e.dma_start(
                    out=wq[:, :, k * 512:(k + 1) * 512],
                    in_=w_pq_d[layer, :, k * 512:(k + 1) * 512].rearrange("(dc dp) n -> dp dc n", dp=128)),
                   W=[wqB], lane="w%d" % k, extra=fence)
            op("pool", lambda e: e.dma_start(out=sk[:], in_=skT_d[layer].rearrange("p (c n) -> p c n", n=128)),
               W=[skB], lane="w1", extra=fence)
            load_gain(gB, gBB, 1 if layer == 0 else 3)
            if layer == 0:
                emit_convert(1)
            if final:
                load_gain(gA, gAB, 4)

            scflat = sc[:, :, :].rearrange("p a b -> p (a b)")
            T4 = scflat.rearrange("p (h k a) -> p h k a", h=8, k=16)

            def front(ti, i):
                par = ti % 2
                xr = xres[:, i, :]
                r, rB = emit_rstd(xr, xresB[i])
                op("dve", lambda e: e.scalar_tensor_tensor(out=xn[par][:], in0=xr, scalar=r, in1=gB[:],
                                                           op0=ALU.mult, op1=ALU.mult),
                   R=[xresB[i], rB, gBB], W=[xnB[par]], extra=(fence if ti < 2 else ()))
                op("act", lambda e: e.copy(out=xnb[:], in_=xn[par][:]), R=[xnB[par]], W=[xnbB])
                emit_transpose8(xnb, xnbB, xnTp[par][:], xnTpB[par])
                for c in range(16):
                    bk = 1 + c // 4
                    for dc in range(8):
                        op("pe", lambda e, c=c, dc=dc, bk=bk: e.matmul(
                            pbank[bk][:, (c % 4) * 128:(c % 4 + 1) * 128], lhsT=wq[:, dc, c * 128:(c + 1) * 128],
                            rhs=xnTp[par][:, dc, :], start=(dc == 0), stop=(dc == 7)),
                           R=[wqB, xnTpB[par]], W=[pbB[bk]])
                for b4 in range(4):
                    op("act", lambda e, b4=b4: e.copy(out=qT[:, b4 * 4:(b4 + 1) * 4, :],
                                                      in_=pbank[1 + b4][:, :].rearrange("p (a b) -> p a b", a=4)),
                       R=[pbB[1 + b4]], W=[qTBs[b4]])
                for c in range(16):
                    bk = 1 + c // 4
                    op("pe", lambda e, c=c, bk=bk: e.matmul(
                        pbank[bk][:, (c % 4) * 128:(c % 4 + 1) * 128], lhsT=qT[:, c, :], rhs=sk[:, c, :],
                        start=True, stop=True),
                       R=[qTBs[c // 4], skB], W=[pbB[bk]])
                for b4 in range(4):
                    op("act", lambda e, b4=b4: e.copy(out=sc[:, b4 * 4:(b4 + 1) * 4, :],
                                                      in_=pbank[1 + b4][:, :].rearrange("p (a b) -> p a b", a=4)),
                       R=[pbB[1 + b4]], W=[scBs[b4]])
                for c in range(16):
                    op("dve", lambda e, c=c: e.max(out=mx[:, c, 0:8], in_=sc[:, c, :]), R=[scBs[c // 4]], W=[mxB])
                    op("dve", lambda e, c=c: e.max_index(out=ixu[:, c, 0:8], in_max=mx[:, c, 0:8], in_values=sc[:, c, :]),
                       R=[scBs[c // 4], mxB], W=[ixuB])
                    op("dve", lambda e, c=c: e.match_replace(out=wk[:], in_to_replace=mx[:, c, 0:8], in_values=sc[:, c, :],
                                                             imm_value=-1e30), R=[scBs[c // 4], mxB], W=[wkB])
                    op("dve", lambda e, c=c: e.max(out=mx[:, c, 8:16], in_=wk[:]), R=[wkB], W=[mxB])
                    op("dve", lambda e, c=c: e.max_index(out=ixu[:, c, 8:16], in_max=mx[:, c, 8:16], in_values=wk[:]),
                       R=[wkB, mxB], W=[ixuB])
                op("dve", lambda e: e.tensor_copy(out=ixf[:], in_=ixu[:]), R=[ixuB], W=[ixfB])
                mx4 = mx[:, :, :].rearrange("p (h t) k -> p h t k", t=2)
                ixf4 = ixf[:, :, :].rearrange("p (h t) k -> p h t k", t=2)
                op("dve", lambda e: e.tensor_tensor(
                    out=T4, in0=mx4[:, :, 0, :].unsqueeze(3).broadcast_to([128, 8, 16, 16]),
                    in1=mx4[:, :, 1, :].unsqueeze(2).broadcast_to([128, 8, 16, 16]), op=ALU.add),
                   R=[mxB], W=scBs)
                for h in range(8):
                    cand_h = scflat[:, h * 256:(h + 1) * 256]
                    op("dve", lambda e, h=h, cand_h=cand_h: e.max(out=tv[:, h, 0:8], in_=cand_h), R=scBs, W=[tvB])
                    op("dve", lambda e, h=h, cand_h=cand_h: e.max_index(out=pos[:, h, 0:8], in_max=tv[:, h, 0:8],
                                                                        in_values=cand_h),
                       R=scBs + [tvB], W=[posB])
                    op("dve", lambda e, h=h, cand_h=cand_h: e.match_replace(out=cwk[:], in_to_replace=tv[:, h, 0:8],
                                                                            in_values=cand_h, imm_value=-1e30),
                       R=scBs + [tvB], W=[cwkB])
                    op("dve", lambda e, h=h: e.max(out=tv[:, h, 8:16], in_=cwk[:]), R=[cwkB], W=[tvB])
                    op("dve", lambda e, h=h: e.max_index(out=pos[:, h, 8:16], in_max=tv[:, h, 8:16], in_values=cwk[:]),
                       R=[cwkB, tvB], W=[posB])
                op("dve", lambda e: e.tensor_tensor(out=ee[:], in0=tv[:],
                                                    in1=tv[:, :, 0].unsqueeze(2).broadcast_to([128, 8, 16]),
                                                    op=ALU.subtract), R=[tvB], W=[eeB])
                op("act", lambda e: e.activation(out=ee[:], in_=ee[:], func=AF.Exp), R=[eeB], W=[eeB])
                op("dve", lambda e: e.tensor_single_scalar(out=pa[:], in_=pos[:], scalar=4, op=ALU.logical_shift_right),
                   R=[posB], W=[paB])
                op("dve", lambda e: e.tensor_single_scalar(out=pb_[:], in_=pos[:], scalar=15, op=ALU.bitwise_and),
                   R=[posB], W=[pbB_])
                op("dve", lambda e: e.tensor_copy(out=paf[:], in_=pa[:]), R=[paB], W=[pafB])
                op("dve", lambda e: e.tensor_copy(out=pbf[:], in_=pb_[:]), R=[pbB_], W=[pbfB])
                io4 = iota16[:, :].unsqueeze(1).unsqueeze(1).broadcast_to([128, 8, 16, 16])
                for (rk, rkB, side, dst, dstB) in ((paf, pafB, 0, s0, s0B), (pbf, pbfB, 1, s1, s1B)):
                    op(RG_ENG, lambda e, rk=rk: e.tensor_tensor(
                        out=T4, in0=io4, in1=rk[:, :, :].unsqueeze(3).broadcast_to([128, 8, 16, 16]), op=ALU.is_equal),
                       R=[iotaB, rkB], W=scBs)
                    op(RG_ENG, lambda e, side=side: e.tensor_tensor(
                        out=T4, in0=T4, in1=ixf4[:, :, side, :].unsqueeze(2).broadcast_to([128, 8, 16, 16]), op=ALU.mult),
                       R=scBs + [ixfB], W=scBs)
                    op("dve", lambda e, dst=dst: e.tensor_reduce(out=dst[:], in_=T4, axis=AX.X, op=ALU.add),
                       R=scBs, W=[dstB])
                op("dve", lambda e: e.scalar_tensor_tensor(
                    out=idxf[:, :].rearrange("p (h k) -> p h k", h=8), in0=s0[:], scalar=128.0, in1=s1[:],
                    op0=ALU.mult, op1=ALU.add), R=[s0B, s1B], W=[idxfB])
                op("dve", lambda e: e.tensor_copy(out=idxu[par][:], in_=idxf[:]), R=[idxfB], W=[idxuB[par]])
                op("dve", lambda e: e.tensor_reduce(out=zz[:], in_=ee[:], axis=AX.X, op=ALU.add), R=[eeB], W=[zzB[0]])
                op("dve", lambda e: e.reciprocal(out=rz[:], in_=zz[:]), R=[zzB[0]], W=[rzB])
                op("dve", lambda e: e.tensor_tensor(
                    out=gg[par][:, :].rearrange("p (h k) -> p h k", h=8), in0=ee[:],
                    in1=rz[:, :].unsqueeze(2).broadcast_to([128, 8, 16]), op=ALU.mult),
                   R=[eeB, rzB], W=[ggB[par]])

            def back(ti, i, pending):
                par = ti % 2
                for j in range(128):
                    ndve = 2 if (j % 2 == 1 and j >= 24) else 1
                    nother = 12
                    while pending:
                        en_ = pending[0][0]
                        if en_ == "dve":
                            if ndve == 0:
                                break
                            ndve -= 1
                        else:
                            if nother == 0:
                                break
                            nother -= 1
                        pending.pop(0)[1]()
                    b = (ti * 128 + j) % NBUF_G
                    op("pool", lambda e, b=b, j=j: e.indirect_dma_start(
                        out=gbuf[b][:], out_offset=None, in_=uvb_d[layer][:, :],
                        in_offset=bass.IndirectOffsetOnAxis(ap=idxu[par][:, j:j + 1], axis=0)),
                       R=[idxuB[par], uvbB[layer]], W=[gbufB[b]], lane="g%d" % b)
                    routed = PE_DOT > 0 and (j % PE_DOT == 0) and (PE_DOT % 2 == 0)
                    if routed:
                        kk = (j // PE_DOT) % 2
                        for dc in range(8):
                            op("pe", lambda e, b=b, dc=dc: e.transpose(out=pv7[:, dc, :], in_=gbuf[b][:, dc * 128:(dc + 1) * 128],
                                                                       identity=ident[:]),
                               R=[gbufB[b], identB], W=[pbB[7]])
                        op("act", lambda e, kk=kk: e.copy(out=ugT[kk][:], in_=pv7), R=[pbB[7]], W=[ugTB[kk]])
                    else:
                        op("dve", lambda e, b=b, j=j: e.scalar_tensor_tensor(
                            out=junk2s[j % 2][:], in0=gbuf[b][:, 0:D], scalar=1.0, in1=xn[par][:], op0=ALU.mult,
                            op1=ALU.mult, accum_out=hh[:, par, j:j + 1]),
                           R=[gbufB[b], xnB[par]], W=[hhB[par][j], junk2B[j % 2]])
                    if PE_DOT > 0 and (PE_DOT % 2 == 0) and (j % PE_DOT == 1):
                        j0 = j - 1
                        kk = (j0 // PE_DOT) % 2
                        for dc in range(8):
                            op("pe", lambda e, kk=kk, dc=dc: e.matmul(
                                pbank[0][:, 0:128], lhsT=ugT[kk][:, dc, :], rhs=xnTp[par][:, dc, :],
                                start=(dc == 0), stop=(dc == 7)),
                               R=[ugTB[kk], xnTpB[par]], W=[pbB[0]])
                        op("dve", lambda e, j0=j0: e.scalar_tensor_tensor(
                            out=junk2s[0][:, 0:128], in0=pbank[0][:, 0:128], scalar=1.0, in1=identf[:], op0=ALU.mult,
                            op1=ALU.mult, accum_out=hh[:, par, j0:j0 + 1]),
                           R=[pbB[0], identfB], W=[hhB[par][j0], junk2B[0]])
                    if j % JG == JG - 1:
                        g0 = j - (JG - 1)
                        gi = g0 // JG
                        op("act", lambda e, g0=g0: e.activation(out=gl[:, par, g0:g0 + JG], in_=hh[:, par, g0:g0 + JG],
                                                                func=AF.Gelu),
                           R=[hhB[par][g0 + t] for t in range(JG)], W=[glB[par][gi]])
                        dpar = gi % 2
                        for t in range(JG):
                            jj = g0 + t
                            ds = dpar * JG + t
                            op("act", lambda e, jj=jj: e.activation(
                                out=aa[:, par, jj:jj + 1], in_=gl[:, par, jj:jj + 1], func=AF.Copy,
                                scale=gg[par][:, jj:jj + 1]),
                               R=[glB[par][gi], ggB[par]], W=[aaB[par][gi]] if t == 0 else [aaB2[par][gi]])
                            op("act", lambda e, jj=jj, ds=ds: e.activation(
                                out=dd[:, ds, :], in_=identf[:], func=AF.Copy, scale=aa[:, par, jj:jj + 1]),
                               R=[aaB[par][gi] if t == 0 else aaB2[par][gi], identfB], W=[ddB[ds]])
                        for t in range(JG):
                            jj = g0 + t
                            ds = dpar * JG + t
                            bb = (ti * 128 + jj) % NBUF_G
                            for half in range(2):
                                op("pe", lambda e, ds=ds, bb=bb, half=half, jj=jj: e.matmul(
                                    pbank[5 + half][:, :], lhsT=dd[:, ds, :],
                                    rhs=gbuf[bb][:, D + half * 512: D + (half + 1) * 512],
                                    start=(jj == 0), stop=(jj == 127)),
                                   R=[ddB[ds], gbufB[bb]], W=[pbB[5 + half]])
                while pending:
                    pending.pop(0)[1]()
                for half in range(2):
                    op("dve", lambda e, half=half: e.tensor_tensor(
                        out=xres[:, i, half * 512:(half + 1) * 512], in0=xres[:, i, half * 512:(half + 1) * 512],
                        in1=pbank[5 + half][:, :], op=ALU.add),
                       R=[pbB[5 + half], xresB[i]], W=[xresB[i]])
                if final:
                    xr = xres[:, i, :]
                    r, rB = emit_rstd(xr, xresB[i])
                    ytv = scflat[:, 0:D]
                    op("dve", lambda e: e.scalar_tensor_tensor(out=ytv, in0=xr, scalar=r, in1=gA[:],
                                                               op0=ALU.mult, op1=ALU.mult),
                       R=[xresB[i], rB, gAB], W=scBs)
                    o_ = i - 1
                    op("sp", lambda e: e.dma_start(out=y_d[o_ * 128:(o_ + 1) * 128, :], in_=ytv),
                       R=scBs, lane="yst%d" % (o_ % 2))

            n = len(tiles)
            front(0, tiles[0])
            for ti in range(n):
                pending = []
                if ti + 1 < n:
                    P.deferred = pending
                    front(ti + 1, tiles[ti + 1])
                    P.deferred = None
                back(ti, tiles[ti], pending)

        emit_convert(0)
        if upto >= 2:
            emit_peer(0, list(range(NT1)))

        if upto >= 3:
            fence = P.fence()
            A = AP0.fork()
            watt = A.alloc("watt", [128, 8, 1792], BF16)
            wattB = Buf()
            wo = A.alloc("wo", [128, 8, D], BF16)
            woB = Buf()
            kT = A.alloc("kT", [128, 4, NT1 * 128], BF16)
            kTB = [Buf() for _ in range(NT1)]
            vv = A.alloc("vv", [128, NT1, 256], BF16)
            vvB = [Buf() for _ in range(NT1)]
            biasm = A.alloc("biasm", [128, 16, 384], F32)
            biasmB = Buf()

            sinkbc = A.alloc("sinkbc", [128, 16], F32)
            sinkB = Buf()
            penb = A.alloc("penb", [1, NT1 * 128], BF16)
            penB = Buf()
            ones1 = A.alloc("ones1", [1, 128], BF16)
            ones1B = Buf()
            qTas = [A.alloc("qTa%d" % i, [128, 8, 128], BF16) for i in range(2)]
            qTaBss = [[Buf() for _ in range(2)] for _ in range(2)]
            LL = [A.alloc("LL%d" % i, [128, 384], F32) for i in range(2)]
            LLB = [Buf() for _ in range(2)]
            EE = [A.alloc("EE%d" % i, [128, 384], BF16) for i in range(2)]
            EEB = [Buf() for _ in range(2)]
            ET = [A.alloc("ET%d" % i, [128, 3, 128], BF16) for i in range(2)]
            ETB = [Buf() for _ in range(2)]
            mrow = A.alloc("mrow", [128, 16], F32)
            mrowB = [Buf() for _ in range(16)]
            nmrow = A.alloc("nmrow", [128, 16], F32)
            nmrowB = [Buf() for _ in range(16)]
            rsum = A.alloc("rsum", [128, 16], F32)
            rsumB = [Buf() for _ in range(16)]
            esink = A.alloc("esink", [128, 16], F32)
            esinkB = [Buf() for _ in range(16)]
            den = A.alloc("den", [128, 16], F32)
            denB = Buf()
            rden = A.alloc("rden", [128, 16], F32)
            rdenB = Buf()
            ao_off = (A.p + 31) // 32 * 32
            ao = A.alloc("ao", [128, D], BF16)
            aoBs = [Buf() for _ in range(2)]
            wmask = nc.alloc_sbuf_tensor_at("wmask", [128, 384], F32, offset=ao_off)
            wmaskB = aoBs[0]
            aoT = A.alloc("aoT", [128, 8, 128], BF16)
            aoTB = Buf()

            for k in range(4):
                lo, hi = k * 448, (k + 1) * 448
                op("pool", lambda e, lo=lo, hi=hi: e.dma_start(
                    out=watt[:, :, lo:hi], in_=w_att_d[:, lo:hi].rearrange("(dc dp) n -> dp dc n", dp=128)),
                   W=[wattB], lane="w%d" % k, extra=fence)
            for k in range(2):
                op("pool", lambda e, k=k: e.dma_start(
                    out=wo[:, :, k * 512:(k + 1) * 512],
                    in_=w_o_d[:, k * 512:(k + 1) * 512].rearrange("(dc dp) n -> dp dc n", dp=128)),
                   W=[woB], lane="w%d" % (2 + k), extra=fence)
            op("pool", lambda e: e.dma_start(out=penb[:], in_=pen_d[:, :]), W=[penB], lane="w1", extra=fence)
            op("sp", lambda e: e.dma_start(out=biasm[:], in_=bias_d.rearrange("p (h k) -> p h k", h=16)),
               W=[biasmB], lane="c_bias", extra=fence)
            op("sp", lambda e: e.dma_start(out=wmask[:], in_=wmask_d[:, :]), W=[wmaskB], lane="c_wmask", extra=fence)
            op("sp", lambda e: e.dma_start(out=sinkbc[:], in_=sink_d.partition_broadcast(128)), W=[sinkB], lane="c_sink",
               extra=fence)
            op("dve", lambda e: e.tensor_tensor(out=biasm[:], in0=biasm[:],
                                                in1=wmask[:, :].unsqueeze(1).broadcast_to([128, 16, 384]), op=ALU.add),
               R=[wmaskB, biasmB], W=[biasmB])
            op("dve", lambda e: e.tensor_scalar(out=biasm[:], in0=biasm[:], scalar1=-1.0, scalar2=None, op0=ALU.mult),
               R=[biasmB], W=[biasmB])
            op("dve", lambda e: e.memset(ones1[:], 1.0), W=[ones1B], extra=fence)
            load_gain(gA, gAB, 2)

            def norm_T(i, bank=0):
                xr = xres[:, i, :]
                r, rB = emit_rstd(xr, xresB[i])
                op("dve", lambda e: e.scalar_tensor_tensor(out=xnb[:], in0=xr, scalar=r, in1=gA[:],
                                                           op0=ALU.mult, op1=ALU.mult),
                   R=[xresB[i], rB, gAB], W=[xnbB])
                emit_transpose8(xnb, xnbB, xnT[:], xnTB, bank=bank)

            for i in range(NT1):
                norm_T(i)
                for g in range(4):
                    for dc in range(8):
                        op("pe", lambda e, g=g, dc=dc: e.matmul(
                            pbank[1][:, g * 128:(g + 1) * 128], lhsT=watt[:, dc, 1024 + g * 128: 1024 + (g + 1) * 128],
                            rhs=xnT[:, dc, :], start=(dc == 0), stop=(dc == 7)),
                           R=[wattB, xnTB], W=[pbB[1]])
                op("act", lambda e, i=i: e.copy(out=kT[:, :, i * 128:(i + 1) * 128],
                                                in_=pbank[1][:, :].rearrange("p (g t) -> p g t", g=4)),
                   R=[pbB[1]], W=[kTB[i]])
                for dc in range(8):
                    op("pe", lambda e, dc=dc: e.matmul(pbank[2][:, 0:256], lhsT=xnT[:, dc, :], rhs=watt[:, dc, 1536:1792],
                                                       start=(dc == 0), stop=(dc == 7)),
                       R=[wattB, xnTB], W=[pbB[2]])
                op("act", lambda e, i=i: e.copy(out=vv[:, i, :], in_=pbank[2][:, 0:256]), R=[pbB[2]], W=[vvB[i]])

            def pre(o_):
                i = o_ + 1
                qq = qTas[o_ % 2]
                norm_T(i, bank=1)
                for cq in range(8):
                    bk = 1 + cq // 4
                    for dc in range(8):
                        op("pe", lambda e, cq=cq, dc=dc, bk=bk: e.matmul(
                            pbank[bk][:, (cq % 4) * 128:(cq % 4 + 1) * 128], lhsT=watt[:, dc, cq * 128:(cq + 1) * 128],
                            rhs=xnT[:, dc, :], start=(dc == 0), stop=(dc == 7)),
                           R=[wattB, xnTB], W=[pbB[bk]])
                for b4 in range(2):
                    op("act", lambda e, b4=b4: e.copy(out=qq[:, b4 * 4:(b4 + 1) * 4, :],
                                                      in_=pbank[1 + b4][:, :].rearrange("p (a b) -> p a b", a=4)),
                       R=[pbB[1 + b4]], W=[qTaBss[o_ % 2][b4]])

            def head(o_, pend):
                i = o_ + 1
                qTa = qTas[o_ % 2]
                qTaBs = qTaBss[o_ % 2]
                edge = (o_ == 0) or (o_ == NO - 1)
                npull = (len(pend) + 15) // 16

                def st_S(h):
                    cq, hf, g = h // 2, h % 2, h // 4
                    q = h % 2
                    bk = 3 + q
                    ps_s = pbank[bk][:, 0:384]
                    op("pe", lambda e: e.matmul(
                        ps_s, lhsT=qTa[64 * hf:64 * hf + 64, cq, :],
                        rhs=kT[64 * hf:64 * hf + 64, g, (i - 1) * 128:(i + 2) * 128], start=True, stop=not edge),
                       R=[qTaBs[cq // 4], kTB[i - 1], kTB[i], kTB[i + 1]], W=[pbB[bk]])
                    if edge:
                        op("pe", lambda e: e.matmul(
                            ps_s, lhsT=ones1[0:1, :], rhs=penb[0:1, (i - 1) * 128:(i + 2) * 128], start=False, stop=True),
                           R=[ones1B, penB], W=[pbB[bk]])
                    op("dve", lambda e: e.scalar_tensor_tensor(
                        out=LL[q][:], in0=ps_s, scalar=-0.125, in1=biasm[:, h, :], op0=ALU.mult, op1=ALU.add),
                       R=[pbB[bk], biasmB], W=[LLB[q]])
                    op("dve", lambda e: e.tensor_reduce(out=nmrow[:, h:h + 1], in_=LL[q][:], axis=AX.X, op=ALU.min),
                       R=[LLB[q]], W=[nmrowB[h]])
                    op("act", lambda e: e.activation(out=EE[q][:], in_=LL[q][:], func=AF.Exp, scale=-1.0,
                                                     bias=nmrow[:, h:h + 1], accum_out=rsum[:, h:h + 1]),
                       R=[LLB[q], nmrowB[h]], W=[EEB[q], rsumB[h]])
                    op("act", lambda e: e.activation(out=esink[:, h:h + 1], in_=nmrow[:, h:h + 1], func=AF.Exp,
                                                     bias=sinkbc[:, h:h + 1], scale=1.0),
                       R=[nmrowB[h], sinkB], W=[esinkB[h]])

                def st_T(h):
                    q = h % 2
                    ptbank = 5 if q == 0 else 0
                    pt = pbank_bf[ptbank][:, 0:384].rearrange("p (a b) -> p a b", a=3)
                    ptB = pbB[ptbank]
                    for kb in range(3):
                        op("pe", lambda e, kb=kb: e.transpose(out=pt[:, kb, :], in_=EE[q][:, kb * 128:(kb + 1) * 128],
                                                              identity=ident[:]),
                           R=[EEB[q], identB], W=[ptB])
                    op("act", lambda e: e.copy(out=ET[q][:], in_=pt), R=[ptB], W=[ETB[q]])

                def st_V(h):
                    g = h // 4
                    q = h % 2
                    bko = 6 + h // 8
                    for kb in range(3):
                        op("pe", lambda e, kb=kb: e.matmul(
                            pbank[bko][:, (h % 8) * 64:(h % 8 + 1) * 64], lhsT=ET[q][:, kb, :],
                            rhs=vv[:, i - 1 + kb, g * 64:(g + 1) * 64], start=(kb == 0), stop=(kb == 2)),
                           R=[ETB[q], vvB[i - 1 + kb]], W=[pbB[bko]])

                for k_ in range(16 + 2):
                    if k_ < 16:
                        st_S(k_)
                    if 0 <= k_ - 1 < 16:
                        st_T(k_ - 1)
                    if 0 <= k_ - 2 < 16:
                        st_V(k_ - 2)
                    for _ in range(npull):
                        if pend:
                            pend.pop(0)[1]()
                while pend:
                    pend.pop(0)[1]()

            def tail_now(o_):
                op("dve", lambda e: e.tensor_tensor(out=den[:], in0=rsum[:], in1=esink[:], op=ALU.add),
                   R=rsumB + esinkB, W=[denB])
                op("dve", lambda e: e.reciprocal(out=rden[:], in_=den[:]), R=[denB], W=[rdenB])
                for hb in range(2):
                    op("dve", lambda e, hb=hb: e.tensor_tensor(
                        out=ao[:, hb * 512:(hb + 1) * 512].rearrange("p (h d) -> p h d", h=8),
                        in0=pbank[6 + hb][:, :].rearrange("p (h d) -> p h d", h=8),
                        in1=rden[:, hb * 8:(hb + 1) * 8].unsqueeze(2).broadcast_to([128, 8, 64]), op=ALU.mult),
                       R=[pbB[6 + hb], rdenB], W=[aoBs[hb]])

            def tail_def(o_):
                i = o_ + 1
                emit_transpose8(ao, aoBs, aoT[:], aoTB, bank=2)
                for half in range(2):
                    bk = 1 + half
                    for cc in range(8):
                        op("pe", lambda e, half=half, cc=cc, bk=bk: e.matmul(
                            pbank[bk][:, :], lhsT=aoT[:, cc, :], rhs=wo[:, cc, half * 512:(half + 1) * 512],
                            start=(cc == 0), stop=(cc == 7)),
                           R=[aoTB, woB], W=[pbB[bk]])
                    op("dve", lambda e, half=half, bk=bk: e.tensor_tensor(
                        out=xres[:, i, half * 512:(half + 1) * 512], in0=xres[:, i, half * 512:(half + 1) * 512],
                        in1=pbank[bk][:, :], op=ALU.add),
                       R=[pbB[bk], xresB[i]], W=[xresB[i]])

            pre(0)
            pend_tail = []
            for o_ in range(NO):
                pend = pend_tail
                if o_ + 1 < NO:
                    P.deferred = []
                    pre(o_ + 1)
                    pend = pend + P.deferred
                    P.deferred = None
                head(o_, pend)
                tail_now(o_)
                P.deferred = pend_tail = []
                tail_def(o_)
                P.deferred = None
            for _, th in pend_tail:
                th()

        if upto >= 4:
            emit_peer(1, list(range(1, NO + 1)), final=True)

        if dbg:
            for i in range(NT1):
                op("sp", lambda e, i=i: e.dma_start(out=dbg_d[i * 128:(i + 1) * 128, :], in_=xres[:, i, :]),
                   R=[xresB[i]], lane="dbg")
        P.raw("sp", lambda e: e.nop(), deps=P.fence())
        P.build()
    return nc


def _t5_bucket_np(rel):
    half = 16
    max_exact = 8
    ret = np.where(rel > 0, half, 0)
    n = np.abs(rel)
    nf = np.maximum(n, 1).astype(np.float32)
    large = max_exact + (np.log(nf / max_exact) / np.float32(np.log(128 / max_exact)) * (half - max_exact)).astype(np.int32)
    large = np.minimum(large, half - 1)
    return ret + np.where(n < max_exact, n, large)


def prep_shared(inp):
    f = lambda a: np.ascontiguousarray(np.asarray(a, dtype=np.float32))
    sh = {}
    sh["w_in"] = f(inp["conv_w_in"][0])
    cwv = np.asarray(inp["conv_w"][0], np.float32)
    sh["cw"] = f(cwv.T.reshape(8, 128, 3).transpose(1, 0, 2).reshape(128, 24))
    sh["w_out"] = f(inp["conv_w_out"][0])
    wqkv = np.asarray(inp["attn_w_qkv"][0], np.float32)
    wq_, wk_, wv_ = wqkv[:, :1024], wqkv[:, 1024:1280], wqkv[:, 1280:1536]
    kd = []
    for g in range(4):
        kd += [wk_[:, g * 64:(g + 1) * 64], wk_[:, g * 64:(g + 1) * 64]]
    sh["w_att"] = f(np.concatenate([wq_] + kd + [wv_], axis=1))
    sh["w_o"] = f(inp["attn_w_o"][0])
    sh["sink"] = f(np.asarray(inp["attn_sink"][0]).reshape(1, 16))
    qi = np.arange(128)[:, None]
    kj = np.arange(384)[None, :]
    rel = kj - 128 - qi
    bk = _t5_bucket_np(rel)
    rb = np.asarray(inp["rel_bias"], np.float32)
    sh["bias_tab"] = f(rb[bk].transpose(0, 2, 1).reshape(128, 16 * 384))
    sh["wmask"] = f(np.where(np.abs(rel) <= 128, 0.0, -30000.0))
    sh["gains"] = f(np.stack([inp["conv_norm_g"][0], inp["ffn_norm_g"][0], inp["attn_norm_g"][0],
                              inp["ffn_norm_g"][1], inp["final_norm_g"]], axis=0))
    sh["w_pq"] = f(inp["peer_w_q"])
    sk = np.asarray(inp["peer_subkeys"], np.float32)
    sh["skT"] = f(sk.transpose(0, 4, 1, 2, 3).reshape(2, 128, 2048))
    for l in range(2):
        sh["uv%d" % l] = f(np.concatenate([np.asarray(inp["peer_u"][l], np.float32),
                                            np.asarray(inp["peer_v"][l], np.float32)], axis=1))
    sh["iota"] = f(np.broadcast_to(np.arange(16, dtype=np.float32), (128, 16)))
    return sh


def prep_core(x, b, k, NO, S):
    NT1, NX = NO + 2, NO + 4
    own0 = k * NO * 128
    lo = own0 - 256
    xe = np.zeros((NX * 128, D), np.float32)
    a, e = max(lo, 0), min(lo + NX * 128, S)
    xe[a - lo:e - lo] = x[b, a:e]
    pen = np.zeros((1, NT1 * 128), np.float32)
    t = own0 - 128 + np.arange(NT1 * 128)
    pen[0, (t < 0) | (t >= S)] = -240000.0
    return {"x_ext": xe, "pen": pen}


_NC_CACHE = {}


def kernel(**inputs):
    x = np.asarray(inputs["x"], np.float32)
    B, S, _ = x.shape
    NO = 16
    ncores = 8
    per_b = ncores // B
    sh = prep_shared(inputs)
    in_maps = []
    for c in range(ncores):
        b, k = c // per_b, c % per_b
        m = dict(sh)
        m.update(prep_core(x, b, k, NO, S))
        in_maps.append(m)
    nc = build(NO=NO)
    res = run_bass_kernel_spmd(nc, in_maps, core_ids=list(range(ncores)))
    out = np.zeros((B, S, D), np.float32)
    for c in range(ncores):
        b, k = c // per_b, c % per_b
        out[b, k * NO * 128:(k + 1) * NO * 128] = res.results[c]["y"]
    return out
```

```python
import contextlib
import os
import numpy as np
import concourse.bass as bass
import concourse.mybir as mybir
from concourse.bass_utils import run_bass_kernel_spmd

F32 = mybir.dt.float32
BF16 = mybir.dt.bfloat16
U32 = mybir.dt.uint32
ALU = mybir.AluOpType
AF = mybir.ActivationFunctionType
AX = mybir.AxisListType

D = 1024
EPS = 1e-6
SAME_ENGINE_SYNC = os.environ.get("K_SES", "1") == "1"
NBUF_G = int(os.environ.get("K_NB", "10"))
PE_DOT = int(os.environ.get("K_PEDOT", "0"))
JG = int(os.environ.get("K_JG", "1"))
RG_ENG = os.environ.get("K_RG", "dve")
FRONT_PULL = 3


class Op:
    __slots__ = ("eng", "fn", "deps", "lane", "count", "signaled")

    def __init__(self, eng, fn, deps, lane):
        self.eng = eng
        self.fn = fn
        self.deps = deps
        self.lane = lane
        self.count = None
        self.signaled = False


class Buf:
    __slots__ = ("w", "r")

    def __init__(self):
        self.w = {}
        self.r = {}


class Prog:
    ENGS = ("pe", "act", "dve", "pool", "sp")

    def __init__(self, nc):
        self.nc = nc
        self.ops = []
        self.lane_last = {}
        self.deferred = None

    def call(self, fn):
        if self.deferred is not None:
            self.deferred.append(("none", fn))
        else:
            fn()

    def raw(self, eng, fn, deps=(), lane=None):
        deps = [d for d in deps if d is not None]
        if lane is not None:
            prev = self.lane_last.get(lane)
            if prev is not None:
                deps.append(prev)
        o = Op(eng, fn, deps, lane)
        if lane is not None:
            self.lane_last[lane] = o
        self.ops.append(o)
        return o

    def op(self, eng, fn, R=(), W=(), lane=None, extra=()):
        if self.deferred is not None:
            self.deferred.append((eng, lambda: self._op(eng, fn, R, W, lane, extra)))
            return None
        return self._op(eng, fn, R, W, lane, extra)

    def _op(self, eng, fn, R=(), W=(), lane=None, extra=()):
        deps = list(extra)
        for b in R:
            deps.extend(b.w.values())
        for b in W:
            deps.extend(b.w.values())
            deps.extend(b.r.values())
        o = self.raw(eng, fn, deps, lane)
        key = ("l", lane) if lane is not None else ("e", eng)
        for b in R:
            b.r[key] = o
        for b in W:
            b.w = {key: o}
            b.r = {}
        return o

    def fence(self):
        last = {}
        for o in self.ops:
            key = ("l", o.lane) if o.lane is not None else ("e", o.eng)
            last[key] = o
        return list(last.values())

    def build(self):
        nc = self.nc
        for o in self.ops:
            nd = []
            seen = set()
            for d in o.deps:
                if id(d) in seen:
                    continue
                seen.add(id(d))
                if d.lane is None and d.eng == o.eng and o.lane is None:
                    if d.eng == "pe" or not SAME_ENGINE_SYNC:
                        continue
                nd.append(d)
            o.deps = nd
            for d in nd:
                d.signaled = True
        lanes = sorted({o.lane for o in self.ops if o.lane is not None})
        with contextlib.ExitStack() as es:
            esem = {e: es.enter_context(nc.semaphore("s_" + e)) for e in self.ENGS}
            lsem = {l: es.enter_context(nc.semaphore("l_" + str(l))) for l in lanes}
            ecount = {e: 0 for e in self.ENGS}
            lcount = {l: 0 for l in lanes}
            for o in self.ops:
                if o.lane is not None:
                    lcount[o.lane] += 16
                    o.count = lcount[o.lane]
                elif o.signaled:
                    ecount[o.eng] += 1
                    o.count = ecount[o.eng]
            self.final_counts = (dict(ecount), dict(lcount))
            block = es.enter_context(nc.Block())
            ops = self.ops

            def emit_for(engname):
                def body(eng):
                    waited = {}
                    for o in ops:
                        if o.eng != engname:
                            continue
                        need = {}
                        for d in o.deps:
                            key = ("l", d.lane) if d.lane is not None else ("e", d.eng)
                            if d.count > need.get(key, 0):
                                need[key] = d.count
                        for key, cnt in need.items():
                            if waited.get(key, 0) >= cnt:
                                continue
                            sem = lsem[key[1]] if key[0] == "l" else esem[key[1]]
                            eng.wait_ge(sem, cnt)
                            waited[key] = cnt
                        ins = o.fn(eng)
                        if o.lane is not None:
                            ins.then_inc(lsem[o.lane], 16)
                        elif o.signaled:
                            ins.then_inc(esem[o.eng], 1)
                return body

            block.tensor(emit_for("pe"))
            block.scalar(emit_for("act"))
            block.vector(emit_for("dve"))
            block.gpsimd(emit_for("pool"))
            block.sync(emit_for("sp"))


def _dsize(dt):
    return {F32: 4, BF16: 2, U32: 4}[dt]


class Arena:
    def __init__(self, nc, start, top):
        self.nc = nc
        self.p = start
        self.top = top
        self.n = 0

    def alloc(self, name, shape, dt):
        nbytes = int(np.prod(shape[1:])) * _dsize(dt)
        off = (self.p + 31) // 32 * 32
        self.p = off + nbytes
        assert self.p <= self.top, (name, self.p, self.top)
        self.n += 1
        return self.nc.alloc_sbuf_tensor_at(name, list(shape), dt, offset=off)

    def fork(self):
        return Arena(self.nc, self.p, self.top)


def build(NO=16, upto=99, dbg=False):
    NT1 = NO + 2
    NX = NO + 4
    nc = bass.Bass("TRN2", target_bir_lowering=False)

    def dr(name, shape, dt=F32, kind="ExternalInput"):
        return nc.dram_tensor(name, list(shape), dt, kind=kind).ap()

    x_ext = dr("x_ext", [NX * 128, D])
    pen_d = dr("pen", [1, NT1 * 128])
    w_in_d = dr("w_in", [D, 3 * D])
    cw_d = dr("cw", [128, 24])
    w_out_d = dr("w_out", [D, D])
    w_att_d = dr("w_att", [D, 1792])
    w_o_d = dr("w_o", [D, D])
    sink_d = dr("sink", [1, 16])
    bias_d = dr("bias_tab", [128, 16 * 384])
    wmask_d = dr("wmask", [128, 384])
    gains_d = dr("gains", [5, D])
    w_pq_d = dr("w_pq", [2, D, 2048])
    skT_d = dr("skT", [2, 128, 2048])
    uv_d = [dr("uv0", [16384, 2048]), dr("uv1", [16384, 2048])]
    iota_d = dr("iota", [128, 16])
    uvb_d = [nc.dram_tensor("uvb%d" % l, [16384, 2048], BF16, kind="Internal").ap() for l in range(2)]
    y_d = dr("y", [NO * 128, D], kind="ExternalOutput")
    if dbg:
        dbg_d = dr("dbg", [NT1 * 128, D], kind="ExternalOutput")

    P = Prog(nc)
    op = P.op

    with contextlib.ExitStack() as es:
        pbank = [es.enter_context(nc.psum_tensor("pb%d" % i, [128, 512], F32)) for i in range(8)]
        pbB = [Buf() for _ in range(8)]
        pbB5b = Buf()
        pbank_bf = [p.bitcast(BF16) for p in pbank]

        A0 = Arena(nc, (nc.sbuf_base + 63) // 64 * 64, nc.sbuf_top)
        xres = A0.alloc("xres", [128, NT1, D], F32)
        xresB = [Buf() for _ in range(NT1)]
        ident = A0.alloc("ident", [128, 128], BF16)
        identB = Buf()
        identf = A0.alloc("identf", [128, 128], F32)
        identfB = Buf()
        iota16 = A0.alloc("iota16", [128, 16], F32)
        iotaB = Buf()
        gA = A0.alloc("gA", [128, D], F32)
        gAB = Buf()
        gB = A0.alloc("gB", [128, D], F32)
        gBB = Buf()
        stat = A0.alloc("stat", [128, 96], F32)
        statB = [Buf() for _ in range(96)]
        junk = A0.alloc("junk", [128, D], BF16)
        junkB = Buf()
        junk2_off = (A0.p + 31) // 32 * 32
        junk2 = A0.alloc("junk2", [128, D], BF16)
        junk3 = A0.alloc("junk3", [128, D], BF16)
        junk2s = [junk2, junk3]
        junk2B = [Buf(), Buf()]
        xnb = A0.alloc("xnb", [128, D], BF16)
        xnbB = Buf()
        xnT = A0.alloc("xnT", [128, 8, 128], BF16)
        xnTB = Buf()

        stat_ctr = [0]

        def new_stat():
            k = stat_ctr[0] % 96
            stat_ctr[0] += 1
            return stat[:, k:k + 1], statB[k]

        op("pool", lambda e: e.memset(ident[:], 1.0), W=[identB])
        op("pool", lambda e: e.affine_select(out=ident[:], in_=ident[:], pattern=[[-1, 128]],
                                             compare_op=ALU.is_equal, fill=0.0, base=0,
                                             channel_multiplier=1), R=[identB], W=[identB])
        op("pool", lambda e: e.tensor_copy(out=identf[:], in_=ident[:]), R=[identB], W=[identfB])
        op("sp", lambda e: e.dma_start(out=iota16[:], in_=iota_d[:, :]), W=[iotaB], lane="c_iota")

        def load_gain(dst, dstB, row):
            return op("sp", lambda e: e.dma_start(out=dst[:], in_=gains_d[row:row + 1, :].partition_broadcast(128)),
                      W=[dstB], lane="gain")

        def emit_rstd(src_ap, srcB):
            ss, ssB = new_stat()
            op("act", lambda e: e.activation(out=junk[:], in_=src_ap, func=AF.Square, accum_out=ss),
               R=[srcB], W=[ssB, junkB])
            sd, sdB = new_stat()
            op("act", lambda e: e.activation(out=sd, in_=ss, func=AF.Sqrt, bias=EPS_AP[0], scale=1.0 / D),
               R=[ssB, epsB], W=[sdB])
            r, rB = new_stat()
            op("dve", lambda e: e.reciprocal(out=r, in_=sd), R=[sdB], W=[rB])
            return r, rB

        def emit_transpose8(src_bf, srcB_, dst_ap, dstB_, bank=0):
            pv = pbank_bf[bank][:, 0:1024].rearrange("p (a b) -> p a b", a=8)
            for dc in range(8):
                op("pe", lambda e, dc=dc: e.transpose(out=pv[:, dc, :], in_=src_bf[:, dc * 128:(dc + 1) * 128],
                                                      identity=ident[:]),
                   R=(list(srcB_) if isinstance(srcB_, (list, tuple)) else [srcB_]) + [identB], W=[pbB[bank]])
            op("act", lambda e: e.copy(out=dst_ap, in_=pv), R=[pbB[bank]], W=[dstB_])

        epsT = A0.alloc("epsT", [128, 1], F32)
        epsB = Buf()
        EPS_AP = [epsT[:, 0:1]]
        op("pool", lambda e: e.memset(epsT[:], EPS), W=[epsB])

        AP0 = A0

        NCV = 16
        RCV = 16384 // NCV
        uvbB = [Buf(), Buf()]

        CV_ROWS = {0: RCV, 1: 128}
        CV_LANES = {0: NCV, 1: 8}
        conv_left = {0: list(range(16384 // CV_ROWS[0])), 1: list(range(16384 // CV_ROWS[1]))}

        def emit_convert(layer, nchunks=10 ** 9):
            rows = CV_ROWS[layer]
            for _ in range(nchunks):
                if not conv_left[layer]:
                    break
                k = conv_left[layer].pop(0)
                op("pool", lambda e, k=k, rows=rows: e.dma_start(out=uvb_d[layer][k * rows:(k + 1) * rows, :],
                                                                 in_=uv_d[layer][k * rows:(k + 1) * rows, :]),
                   W=[], lane="cv%d_%d" % (layer, k % CV_LANES[layer]))
            if not conv_left[layer]:
                uvbB[layer].w = {("l", "cv%d_%d" % (layer, q)): P.lane_last["cv%d_%d" % (layer, q)]
                                 for q in range(CV_LANES[layer]) if ("cv%d_%d" % (layer, q)) in P.lane_last}

        if upto >= 1:
            A = AP0.fork()
            xtmp = nc.alloc_sbuf_tensor_at("xtmp", [128, D], F32, offset=junk2_off)
            xtmpB = Buf()
            xnb2 = A.alloc("xnb2", [128, D], BF16)
            xnb2B = Buf()
            win = A.alloc("win", [128, 8, 3 * D], BF16)
            winB = Buf()
            wout = A.alloc("wout", [128, 8, D], BF16)
            woutB = Buf()
            cw = A.alloc("cw", [128, 8, 3], F32)
            cwB = Buf()
            xnT_all = A.alloc("xnT_all", [128, 8, NX * 128], BF16)
            xnT_allB = [Buf() for _ in range(NX)]
            hsb = [A.alloc("hsb%d" % i, [128, 130], F32) for i in range(2)]
            hsbB = [Buf() for _ in range(2)]
            zsb = [A.alloc("zsb%d" % i, [128, 130], F32) for i in range(2)]
            zsbB = [Buf() for _ in range(2)]
            ysb = [A.alloc("ysb%d" % i, [128, 128], F32) for i in range(2)]
            ysbB = [Buf() for _ in range(2)]
            gT = [A.alloc("gT%d" % i, [128, 8, 128], BF16) for i in range(2)]
            gTB = [[Buf() for _ in range(8)] for _ in range(2)]

            for k in range(6):
                op("pool", lambda e, k=k: e.dma_start(
                    out=win[:, :, k * 512:(k + 1) * 512],
                    in_=w_in_d[:, k * 512:(k + 1) * 512].rearrange("(dc dp) n -> dp dc n", dp=128)),
                   W=[winB], lane="w%d" % (k % 4))
            op("pool", lambda e: e.dma_start(out=wout[:], in_=w_out_d.rearrange("(dc dp) n -> dp dc n", dp=128)),
               W=[woutB], lane="w1")
            op("sp", lambda e: e.dma_start(out=cw[:], in_=cw_d.rearrange("p (c k) -> p c k", k=3)), W=[cwB], lane="c_cw")
            load_gain(gA, gAB, 0)

            for e_ in range(NX):
                if e_ >= 4 and e_ % 2 == 0:
                    emit_convert(0, 1)
                if 1 <= e_ <= NT1:
                    dst, dB = xres[:, e_ - 1, :], xresB[e_ - 1]
                else:
                    dst, dB = xtmp[:], xtmpB
                op("sp", lambda e, e_=e_, dst=dst: e.dma_start(out=dst, in_=x_ext[e_ * 128:(e_ + 1) * 128, :]),
                   W=[dB], lane="xl%d" % (e_ % 4))
                r, rB = emit_rstd(dst, dB)
                xb_, xbB_ = (xnb, xnbB) if e_ % 2 == 0 else (xnb2, xnb2B)
                op("dve", lambda e, dst=dst, r=r, xb_=xb_: e.scalar_tensor_tensor(out=xb_[:], in0=dst, scalar=r, in1=gA[:],
                                                                                 op0=ALU.mult, op1=ALU.mult),
                   R=[dB, rB, gAB], W=[xbB_])
                emit_transpose8(xb_, xbB_, xnT_all[:, :, e_ * 128:(e_ + 1) * 128], xnT_allB[e_], bank=(0 if e_ % 2 == 0 else 7))

            for i in range(NT1):
                emit_convert(0, 1)
                e_ = i + 1
                c0 = 128 * e_ - 1
                par = i % 2
                for cc in range(8):
                    q = cc % 2
                    bk = 1 + q
                    psB = pbank[bk][:, 0:130]
                    psC = pbank[bk][:, 130:260]
                    psH = pbank[bk][:, 260:390]
                    for wi, pso in enumerate((psB, psC, psH)):
                        for dc in range(8):
                            op("pe", lambda e, wi=wi, pso=pso, dc=dc, cc=cc, c0=c0: e.matmul(
                                pso, lhsT=win[:, dc, wi * D + cc * 128: wi * D + (cc + 1) * 128],
                                rhs=xnT_all[:, dc, c0:c0 + 130], start=(dc == 0), stop=(dc == 7)),
                               R=[winB, xnT_allB[e_ - 1], xnT_allB[e_], xnT_allB[e_ + 1]], W=[pbB[bk]])
                    op("act", lambda e, q=q, psH=psH: e.copy(out=hsb[q][:], in_=psH), R=[pbB[bk]], W=[hsbB[q]])
                    op("dve", lambda e, q=q, psC=psC: e.tensor_tensor(out=zsb[q][:], in0=psC, in1=hsb[q][:], op=ALU.mult),
                       R=[pbB[bk], hsbB[q]], W=[zsbB[q]])
                    op("dve", lambda e, q=q, cc=cc: e.tensor_scalar(out=ysb[q][:], in0=zsb[q][:, 0:128],
                                                                   scalar1=cw[:, cc, 0:1], scalar2=None, op0=ALU.mult),
                       R=[zsbB[q], cwB], W=[ysbB[q]])
                    for kk in (1, 2):
                        op("dve", lambda e, q=q, cc=cc, kk=kk: e.scalar_tensor_tensor(
                            out=ysb[q][:], in0=zsb[q][:, kk:kk + 128], scalar=cw[:, cc, kk:kk + 1], in1=ysb[q][:],
                            op0=ALU.mult, op1=ALU.add),
                           R=[zsbB[q], cwB, ysbB[q]], W=[ysbB[q]])
                    op("dve", lambda e, q=q, cc=cc, par=par, psB=psB: e.tensor_tensor(
                        out=gT[par][:, cc, :], in0=psB[:, 1:129], in1=ysb[q][:], op=ALU.mult),
                       R=[pbB[bk], ysbB[q]], W=[gTB[par][cc]])
                for half in range(2):
                    bk = 3 + half
                    for cc in range(8):
                        op("pe", lambda e, half=half, cc=cc, par=par, bk=bk: e.matmul(
                            pbank[bk][:, :], lhsT=gT[par][:, cc, :], rhs=wout[:, cc, half * 512:(half + 1) * 512],
                            start=(cc == 0), stop=(cc == 7)),
                           R=[gTB[par][cc], woutB], W=[pbB[bk]])
                    op("dve", lambda e, half=half, i=i, bk=bk: e.tensor_tensor(
                        out=xres[:, i, half * 512:(half + 1) * 512], in0=xres[:, i, half * 512:(half + 1) * 512],
                        in1=pbank[bk][:, :], op=ALU.add),
                       R=[pbB[bk], xresB[i]], W=[xresB[i]])

        def emit_peer(layer, tiles, final=False):
            fence = P.fence()
            A = AP0.fork()
            wq = A.alloc("wq%d" % layer, [128, 8, 2048], BF16)
            wqB = Buf()
            sk = A.alloc("sk%d" % layer, [128, 16, 128], BF16)
            skB = Buf()
            xn = [A.alloc("xn%d_%d" % (layer, i), [128, D], F32) for i in range(2)]
            xnB = [Buf() for _ in range(2)]
            qT = A.alloc("qT%d" % layer, [128, 16, 128], BF16)
            qTBs = [Buf() for _ in range(4)]
            sc = A.alloc("sc%d" % layer, [128, 16, 128], F32)
            scBs = [Buf() for _ in range(4)]
            wk = A.alloc("wk%d" % layer, [128, 128], F32)
            wkB = Buf()
            mx = A.alloc("mx%d" % layer, [128, 16, 16], F32)
            mxB = Buf()
            ixu = A.alloc("ixu%d" % layer, [128, 16, 16], U32)
            ixuB = Buf()
            ixf = A.alloc("ixf%d" % layer, [128, 16, 16], F32)
            ixfB = Buf()
            cand = A.alloc("cand%d" % layer, [128, 256], F32)
            candB = Buf()
            cwk = A.alloc("cwk%d" % layer, [128, 256], F32)
            cwkB = Buf()
            tv = A.alloc("tv%d" % layer, [128, 8, 16], F32)
            tvB = Buf()
            pos = A.alloc("pos%d" % layer, [128, 8, 16], U32)
            posB = Buf()
            pa = A.alloc("pa%d" % layer, [128, 8, 16], U32)
            paB = Buf()
            pb_ = A.alloc("pb%d_" % layer, [128, 8, 16], U32)
            pbB_ = Buf()
            paf = A.alloc("paf%d" % layer, [128, 8, 16], F32)
            pafB = Buf()
            pbf = A.alloc("pbf%d" % layer, [128, 8, 16], F32)
            pbfB = Buf()
            s0 = A.alloc("s0%d" % layer, [128, 8, 16], F32)
            s0B = Buf()
            s1 = A.alloc("s1%d" % layer, [128, 8, 16], F32)
            s1B = Buf()
            idxf = A.alloc("idxf%d" % layer, [128, 128], F32)
            idxfB = Buf()
            idxu = [A.alloc("idxu%d_%d" % (layer, i), [128, 128], U32) for i in range(2)]
            idxuB = [Buf() for _ in range(2)]
            ntv = A.alloc("ntv%d" % layer, [128, 8], F32)
            ntvB = Buf()
            ee = A.alloc("ee%d" % layer, [128, 8, 16], F32)
            eeB = Buf()
            zz = A.alloc("zz%d" % layer, [128, 8], F32)
            zzB = [Buf() for _ in range(8)]
            rz = A.alloc("rz%d" % layer, [128, 8], F32)
            rzB = Buf()
            gg = [A.alloc("gg%d_%d" % (layer, i), [128, 128], F32) for i in range(2)]
            ggB = [Buf() for _ in range(2)]
            hh = A.alloc("hh%d" % layer, [128, 2, 128], F32)
            hhB = [[Buf() for _ in range(128)] for _ in range(2)]
            gl = A.alloc("gl%d" % layer, [128, 2, 128], F32)
            glB = [[Buf() for _ in range(128 // JG)] for _ in range(2)]
            aa = A.alloc("aa%d" % layer, [128, 2, 128], F32)
            aaB = [[Buf() for _ in range(128 // JG)] for _ in range(2)]
            aaB2 = [[Buf() for _ in range(128 // JG)] for _ in range(2)]
            dd = A.alloc("dd%d" % layer, [128, 2 * JG, 128], BF16)
            ddB = [Buf() for _ in range(2 * JG)]
            gbuf = [A.alloc("gb%d_%d" % (layer, i), [128, 2048], BF16) for i in range(NBUF_G)]
            gbufB = [Buf() for _ in range(NBUF_G)]
            yt = None
            xnT2 = A.alloc("xnT2_%d" % layer, [128, 8, 128], BF16)
            xnTp = [xnT, xnT2]
            xnTpB = [xnTB, Buf()]
            ugT = [A.alloc("ugT%d_%d" % (layer, i), [128, 8, 128], BF16) for i in range(2)] if PE_DOT > 0 else None
            ugTB = [Buf() for _ in range(2)]
            pv7 = pbank_bf[7][:, 0:1024].rearrange("p (a b) -> p a b", a=8)
            tb = sc

            for k in range(4):
                op("pool", lambda e, k=k: e.dma_start(
                    out=wq[:, :, k * 512:(k + 1) * 512],
                    in_=w_pq_d[layer, :, k * 512:(k + 1) * 512].rearrange("(dc dp) n -> dp dc n", dp=128)),
                   W=[wqB], lane="w%d" % k, extra=fence)
            op("pool", lambda e: e.dma_start(out=sk[:], in_=skT_d[layer].rearrange("p (c n) -> p c n", n=128)),
               W=[skB], lane="w1", extra=fence)
            load_gain(gB, gBB, 1 if layer == 0 else 3)
            if final:
                load_gain(gA, gAB, 4)

            scflat = sc[:, :, :].rearrange("p a b -> p (a b)")
            T4 = scflat.rearrange("p (h k a) -> p h k a", h=8, k=16)

            def front(ti, i):
                par = ti % 2
                xr = xres[:, i, :]
                r, rB = emit_rstd(xr, xresB[i])
                op("dve", lambda e: e.scalar_tensor_tensor(out=xn[par][:], in0=xr, scalar=r, in1=gB[:],
                                                           op0=ALU.mult, op1=ALU.mult),
                   R=[xresB[i], rB, gBB], W=[xnB[par]], extra=(fence if ti < 2 else ()))
                op("act", lambda e: e.copy(out=xnb[:], in_=xn[par][:]), R=[xnB[par]], W=[xnbB])
                emit_transpose8(xnb, xnbB, xnTp[par][:], xnTpB[par])
                for c in range(16):
                    bk = 1 + c // 4
                    for dc in range(8):
                        op("pe", lambda e, c=c, dc=dc, bk=bk: e.matmul(
                            pbank[bk][:, (c % 4) * 128:(c % 4 + 1) * 128], lhsT=wq[:, dc, c * 128:(c + 1) * 128],
                            rhs=xnTp[par][:, dc, :], start=(dc == 0), stop=(dc == 7)),
                           R=[wqB, xnTpB[par]], W=[pbB[bk]])
                for b4 in range(4):
                    op("act", lambda e, b4=b4: e.copy(out=qT[:, b4 * 4:(b4 + 1) * 4, :],
                                                      in_=pbank[1 + b4][:, :].rearrange("p (a b) -> p a b", a=4)),
                       R=[pbB[1 + b4]], W=[qTBs[b4]])
                for c in range(16):
                    bk = 1 + c // 4
                    op("pe", lambda e, c=c, bk=bk: e.matmul(
                        pbank[bk][:, (c % 4) * 128:(c % 4 + 1) * 128], lhsT=qT[:, c, :], rhs=sk[:, c, :],
                        start=True, stop=True),
                       R=[qTBs[c // 4], skB], W=[pbB[bk]])
                for b4 in range(4):
                    op("act", lambda e, b4=b4: e.copy(out=sc[:, b4 * 4:(b4 + 1) * 4, :],
                                                      in_=pbank[1 + b4][:, :].rearrange("p (a b) -> p a b", a=4)),
                       R=[pbB[1 + b4]], W=[scBs[b4]])
                for c in range(16):
                    op("dve", lambda e, c=c: e.max(out=mx[:, c, 0:8], in_=sc[:, c, :]), R=[scBs[c // 4]], W=[mxB])
                    op("dve", lambda e, c=c: e.max_index(out=ixu[:, c, 0:8], in_max=mx[:, c, 0:8], in_values=sc[:, c, :]),
                       R=[scBs[c // 4], mxB], W=[ixuB])
                    op("dve", lambda e, c=c: e.match_replace(out=wk[:], in_to_replace=mx[:, c, 0:8], in_values=sc[:, c, :],
                                                             imm_value=-1e30), R=[scBs[c // 4], mxB], W=[wkB])
                    op("dve", lambda e, c=c: e.max(out=mx[:, c, 8:16], in_=wk[:]), R=[wkB], W=[mxB])
                    op("dve", lambda e, c=c: e.max_index(out=ixu[:, c, 8:16], in_max=mx[:, c, 8:16], in_values=wk[:]),
                       R=[wkB, mxB], W=[ixuB])
                op("dve", lambda e: e.tensor_copy(out=ixf[:], in_=ixu[:]), R=[ixuB], W=[ixfB])
                mx4 = mx[:, :, :].rearrange("p (h t) k -> p h t k", t=2)
                ixf4 = ixf[:, :, :].rearrange("p (h t) k -> p h t k", t=2)
                op("dve", lambda e: e.tensor_tensor(
                    out=T4, in0=mx4[:, :, 0, :].unsqueeze(3).broadcast_to([128, 8, 16, 16]),
                    in1=mx4[:, :, 1, :].unsqueeze(2).broadcast_to([128, 8, 16, 16]), op=ALU.add),
                   R=[mxB], W=scBs)
                for h in range(8):
                    cand_h = scflat[:, h * 256:(h + 1) * 256]
                    op("dve", lambda e, h=h, cand_h=cand_h: e.max(out=tv[:, h, 0:8], in_=cand_h), R=scBs, W=[tvB])
                    op("dve", lambda e, h=h, cand_h=cand_h: e.max_index(out=pos[:, h, 0:8], in_max=tv[:, h, 0:8],
                                                                        in_values=cand_h),
                       R=scBs + [tvB], W=[posB])
                    op("dve", lambda e, h=h, cand_h=cand_h: e.match_replace(out=cwk[:], in_to_replace=tv[:, h, 0:8],
                                                                            in_values=cand_h, imm_value=-1e30),
                       R=scBs + [tvB], W=[cwkB])
                    op("dve", lambda e, h=h: e.max(out=tv[:, h, 8:16], in_=cwk[:]), R=[cwkB], W=[tvB])
                    op("dve", lambda e, h=h: e.max_index(out=pos[:, h, 8:16], in_max=tv[:, h, 8:16], in_values=cwk[:]),
                       R=[cwkB, tvB], W=[posB])
                op("dve", lambda e: e.tensor_tensor(out=ee[:], in0=tv[:],
                                                    in1=tv[:, :, 0].unsqueeze(2).broadcast_to([128, 8, 16]),
                                                    op=ALU.subtract), R=[tvB], W=[eeB])
                op("act", lambda e: e.activation(out=ee[:], in_=ee[:], func=AF.Exp), R=[eeB], W=[eeB])
                op("dve", lambda e: e.tensor_single_scalar(out=pa[:], in_=pos[:], scalar=4, op=ALU.logical_shift_right),
                   R=[posB], W=[paB])
                op("dve", lambda e: e.tensor_single_scalar(out=pb_[:], in_=pos[:], scalar=15, op=ALU.bitwise_and),
                   R=[posB], W=[pbB_])
                op("dve", lambda e: e.tensor_copy(out=paf[:], in_=pa[:]), R=[paB], W=[pafB])
                op("dve", lambda e: e.tensor_copy(out=pbf[:], in_=pb_[:]), R=[pbB_], W=[pbfB])
                io4 = iota16[:, :].unsqueeze(1).unsqueeze(1).broadcast_to([128, 8, 16, 16])
                for (rk, rkB, side, dst, dstB) in ((paf, pafB, 0, s0, s0B), (pbf, pbfB, 1, s1, s1B)):
                    op(RG_ENG, lambda e, rk=rk: e.tensor_tensor(
                        out=T4, in0=io4, in1=rk[:, :, :].unsqueeze(3).broadcast_to([128, 8, 16, 16]), op=ALU.is_equal),
                       R=[iotaB, rkB], W=scBs)
                    op(RG_ENG, lambda e, side=side: e.tensor_tensor(
                        out=T4, in0=T4, in1=ixf4[:, :, side, :].unsqueeze(2).broadcast_to([128, 8, 16, 16]), op=ALU.mult),
                       R=scBs + [ixfB], W=scBs)
                    op("dve", lambda e, dst=dst: e.tensor_reduce(out=dst[:], in_=T4, axis=AX.X, op=ALU.add),
                       R=scBs, W=[dstB])
                op("dve", lambda e: e.scalar_tensor_tensor(
                    out=idxf[:, :].rearrange("p (h k) -> p h k", h=8), in0=s0[:], scalar=128.0, in1=s1[:],
                    op0=ALU.mult, op1=ALU.add), R=[s0B, s1B], W=[idxfB])
                op("dve", lambda e: e.tensor_copy(out=idxu[par][:], in_=idxf[:]), R=[idxfB], W=[idxuB[par]])
                op("dve", lambda e: e.tensor_reduce(out=zz[:], in_=ee[:], axis=AX.X, op=ALU.add), R=[eeB], W=[zzB[0]])
                op("dve", lambda e: e.reciprocal(out=rz[:], in_=zz[:]), R=[zzB[0]], W=[rzB])
                op("dve", lambda e: e.tensor_tensor(
                    out=gg[par][:, :].rearrange("p (h k) -> p h k", h=8), in0=ee[:],
                    in1=rz[:, :].unsqueeze(2).broadcast_to([128, 8, 16]), op=ALU.mult),
                   R=[eeB, rzB], W=[ggB[par]])

            def back(ti, i, pending):
                par = ti % 2
                for j in range(128):
                    if layer == 0 and j % 16 == 8:
                        emit_convert(1, 1)
                    ndve = 2 if (j % 2 == 1 and j >= 24) else 1
                    nother = 12
                    while pending:
                        en_ = pending[0][0]
                        if en_ == "dve":
                            if ndve == 0:
                                break
                            ndve -= 1
                        else:
                            if nother == 0:
                                break
                            nother -= 1
                        pending.pop(0)[1]()
                    b = (ti * 128 + j) % NBUF_G
                    op("pool", lambda e, b=b, j=j: e.indirect_dma_start(
                        out=gbuf[b][:], out_offset=None, in_=uvb_d[layer][:, :],
                        in_offset=bass.IndirectOffsetOnAxis(ap=idxu[par][:, j:j + 1], axis=0)),
                       R=[idxuB[par], uvbB[layer]], W=[gbufB[b]], lane="g%d" % b)
                    routed = PE_DOT > 0 and (j % PE_DOT == 0) and (PE_DOT % 2 == 0)
                    if routed:
                        kk = (j // PE_DOT) % 2
                        for dc in range(8):
                            op("pe", lambda e, b=b, dc=dc: e.transpose(out=pv7[:, dc, :], in_=gbuf[b][:, dc * 128:(dc + 1) * 128],
                                                                       identity=ident[:]),
                               R=[gbufB[b], identB], W=[pbB[7]])
                        op("act", lambda e, kk=kk: e.copy(out=ugT[kk][:], in_=pv7), R=[pbB[7]], W=[ugTB[kk]])
                    else:
                        op("dve", lambda e, b=b, j=j: e.scalar_tensor_tensor(
                            out=junk2s[j % 2][:], in0=gbuf[b][:, 0:D], scalar=1.0, in1=xn[par][:], op0=ALU.mult,
                            op1=ALU.mult, accum_out=hh[:, par, j:j + 1]),
                           R=[gbufB[b], xnB[par]], W=[hhB[par][j], junk2B[j % 2]])
                    if PE_DOT > 0 and (PE_DOT % 2 == 0) and (j % PE_DOT == 1):
                        j0 = j - 1
                        kk = (j0 // PE_DOT) % 2
                        for dc in range(8):
                            op("pe", lambda e, kk=kk, dc=dc: e.matmul(
                                pbank[0][:, 0:128], lhsT=ugT[kk][:, dc, :], rhs=xnTp[par][:, dc, :],
                                start=(dc == 0), stop=(dc == 7)),
                               R=[ugTB[kk], xnTpB[par]], W=[pbB[0]])
                        op("dve", lambda e, j0=j0: e.scalar_tensor_tensor(
                            out=junk2s[0][:, 0:128], in0=pbank[0][:, 0:128], scalar=1.0, in1=identf[:], op0=ALU.mult,
                            op1=ALU.mult, accum_out=hh[:, par, j0:j0 + 1]),
                           R=[pbB[0], identfB], W=[hhB[par][j0], junk2B[0]])
                    if j % JG == JG - 1:
                        g0 = j - (JG - 1)
                        gi = g0 // JG
                        op("act", lambda e, g0=g0: e.activation(out=gl[:, par, g0:g0 + JG], in_=hh[:, par, g0:g0 + JG],
                                                                func=AF.Gelu),
                           R=[hhB[par][g0 + t] for t in range(JG)], W=[glB[par][gi]])
                        dpar = gi % 2
                        for t in range(JG):
                            jj = g0 + t
                            ds = dpar * JG + t
                            op("act", lambda e, jj=jj: e.activation(
                                out=aa[:, par, jj:jj + 1], in_=gl[:, par, jj:jj + 1], func=AF.Copy,
                                scale=gg[par][:, jj:jj + 1]),
                               R=[glB[par][gi], ggB[par]], W=[aaB[par][gi]] if t == 0 else [aaB2[par][gi]])
                            op("act", lambda e, jj=jj, ds=ds: e.activation(
                                out=dd[:, ds, :], in_=identf[:], func=AF.Copy, scale=aa[:, par, jj:jj + 1]),
                               R=[aaB[par][gi] if t == 0 else aaB2[par][gi], identfB], W=[ddB[ds]])
                        for t in range(JG):
                            jj = g0 + t
                            ds = dpar * JG + t
                            bb = (ti * 128 + jj) % NBUF_G
                            for half in range(2):
                                op("pe", lambda e, ds=ds, bb=bb, half=half, jj=jj: e.matmul(
                                    pbank[5 + half][:, :], lhsT=dd[:, ds, :],
                                    rhs=gbuf[bb][:, D + half * 512: D + (half + 1) * 512],
                                    start=(jj == 0), stop=(jj == 127)),
                                   R=[ddB[ds], gbufB[bb]], W=[pbB[5 + half]])
                while pending:
                    pending.pop(0)[1]()
                for half in range(2):
                    op("dve", lambda e, half=half: e.tensor_tensor(
                        out=xres[:, i, half * 512:(half + 1) * 512], in0=xres[:, i, half * 512:(half + 1) * 512],
                        in1=pbank[5 + half][:, :], op=ALU.add),
                       R=[pbB[5 + half], xresB[i]], W=[xresB[i]])
                if final:
                    xr = xres[:, i, :]
                    r, rB = emit_rstd(xr, xresB[i])
                    ytv = scflat[:, 0:D]
                    op("dve", lambda e: e.scalar_tensor_tensor(out=ytv, in0=xr, scalar=r, in1=gA[:],
                                                               op0=ALU.mult, op1=ALU.mult),
                       R=[xresB[i], rB, gAB], W=scBs)
                    o_ = i - 1
                    op("sp", lambda e: e.dma_start(out=y_d[o_ * 128:(o_ + 1) * 128, :], in_=ytv),
                       R=scBs, lane="yst%d" % (o_ % 2))

            n = len(tiles)
            front(0, tiles[0])
            for ti in range(n):
                pending = []
                if ti + 1 < n:
                    P.deferred = pending
                    front(ti + 1, tiles[ti + 1])
                    P.deferred = None
                back(ti, tiles[ti], pending)

        emit_convert(0)
        if upto >= 2:
            emit_peer(0, list(range(NT1)))

        emit_convert(1)
        if upto >= 3:
            fence = P.fence()
            A = AP0.fork()
            watt = A.alloc("watt", [128, 8, 1792], BF16)
            wattB = Buf()
            wo = A.alloc("wo", [128, 8, D], BF16)
            woB = Buf()
            kT = A.alloc("kT", [128, 4, NT1 * 128], BF16)
            kTB = [Buf() for _ in range(NT1)]
            vv = A.alloc("vv", [128, NT1, 256], BF16)
            vvB = [Buf() for _ in range(NT1)]
            biasm = A.alloc("biasm", [128, 16, 384], F32)
            biasmB = Buf()

            sinkbc = A.alloc("sinkbc", [128, 16], F32)
            sinkB = Buf()
            penb = A.alloc("penb", [1, NT1 * 128], BF16)
            penB = Buf()
            ones1 = A.alloc("ones1", [1, 128], BF16)
            ones1B = Buf()
            qTas = [A.alloc("qTa%d" % i, [128, 8, 128], BF16) for i in range(2)]
            qTaBss = [[Buf() for _ in range(2)] for _ in range(2)]
            LL = [A.alloc("LL%d" % i, [128, 384], F32) for i in range(2)]
            LLB = [Buf() for _ in range(2)]
            EE = [A.alloc("EE%d" % i, [128, 384], BF16) for i in range(2)]
            EEB = [Buf() for _ in range(2)]
            ET = [A.alloc("ET%d" % i, [128, 3, 128], BF16) for i in range(2)]
            ETB = [Buf() for _ in range(2)]
            mrow = A.alloc("mrow", [128, 16], F32)
            mrowB = [Buf() for _ in range(16)]
            nmrow = A.alloc("nmrow", [128, 16], F32)
            nmrowB = [Buf() for _ in range(16)]
            rsum = A.alloc("rsum", [128, 16], F32)
            rsumB = [Buf() for _ in range(16)]
            esink = A.alloc("esink", [128, 16], F32)
            esinkB = [Buf() for _ in range(16)]
            den = A.alloc("den", [128, 16], F32)
            denB = Buf()
            rden = A.alloc("rden", [128, 16], F32)
            rdenB = Buf()
            ao_off = (A.p + 31) // 32 * 32
            ao = A.alloc("ao", [128, D], BF16)
            aoBs = [Buf() for _ in range(2)]
            wmask = nc.alloc_sbuf_tensor_at("wmask", [128, 384], F32, offset=ao_off)
            wmaskB = aoBs[0]
            aoT = A.alloc("aoT", [128, 8, 128], BF16)
            aoTB = Buf()

            for k in range(4):
                lo, hi = k * 448, (k + 1) * 448
                op("pool", lambda e, lo=lo, hi=hi: e.dma_start(
                    out=watt[:, :, lo:hi], in_=w_att_d[:, lo:hi].rearrange("(dc dp) n -> dp dc n", dp=128)),
                   W=[wattB], lane="w%d" % k, extra=fence)
            for k in range(2):
                op("pool", lambda e, k=k: e.dma_start(
                    out=wo[:, :, k * 512:(k + 1) * 512],
                    in_=w_o_d[:, k * 512:(k + 1) * 512].rearrange("(dc dp) n -> dp dc n", dp=128)),
                   W=[woB], lane="w%d" % (2 + k), extra=fence)
            op("pool", lambda e: e.dma_start(out=penb[:], in_=pen_d[:, :]), W=[penB], lane="w1", extra=fence)
            op("sp", lambda e: e.dma_start(out=biasm[:], in_=bias_d.rearrange("p (h k) -> p h k", h=16)),
               W=[biasmB], lane="c_bias", extra=fence)
            op("sp", lambda e: e.dma_start(out=wmask[:], in_=wmask_d[:, :]), W=[wmaskB], lane="c_wmask", extra=fence)
            op("sp", lambda e: e.dma_start(out=sinkbc[:], in_=sink_d.partition_broadcast(128)), W=[sinkB], lane="c_sink",
               extra=fence)
            op("dve", lambda e: e.tensor_tensor(out=biasm[:], in0=biasm[:],
                                                in1=wmask[:, :].unsqueeze(1).broadcast_to([128, 16, 384]), op=ALU.add),
               R=[wmaskB, biasmB], W=[biasmB])
            op("dve", lambda e: e.tensor_scalar(out=biasm[:], in0=biasm[:], scalar1=-1.0, scalar2=None, op0=ALU.mult),
               R=[biasmB], W=[biasmB])
            op("dve", lambda e: e.memset(ones1[:], 1.0), W=[ones1B], extra=fence)
            load_gain(gA, gAB, 2)

            def norm_T(i, bank=0):
                xr = xres[:, i, :]
                r, rB = emit_rstd(xr, xresB[i])
                op("dve", lambda e: e.scalar_tensor_tensor(out=xnb[:], in0=xr, scalar=r, in1=gA[:],
                                                           op0=ALU.mult, op1=ALU.mult),
                   R=[xresB[i], rB, gAB], W=[xnbB])
                emit_transpose8(xnb, xnbB, xnT[:], xnTB, bank=bank)

            for i in range(NT1):
                norm_T(i)
                for g in range(4):
                    for dc in range(8):
                        op("pe", lambda e, g=g, dc=dc: e.matmul(
                            pbank[1][:, g * 128:(g + 1) * 128], lhsT=watt[:, dc, 1024 + g * 128: 1024 + (g + 1) * 128],
                            rhs=xnT[:, dc, :], start=(dc == 0), stop=(dc == 7)),
                           R=[wattB, xnTB], W=[pbB[1]])
                op("act", lambda e, i=i: e.copy(out=kT[:, :, i * 128:(i + 1) * 128],
                                                in_=pbank[1][:, :].rearrange("p (g t) -> p g t", g=4)),
                   R=[pbB[1]], W=[kTB[i]])
                for dc in range(8):
                    op("pe", lambda e, dc=dc: e.matmul(pbank[2][:, 0:256], lhsT=xnT[:, dc, :], rhs=watt[:, dc, 1536:1792],
                                                       start=(dc == 0), stop=(dc == 7)),
                       R=[wattB, xnTB], W=[pbB[2]])
                op("act", lambda e, i=i: e.copy(out=vv[:, i, :], in_=pbank[2][:, 0:256]), R=[pbB[2]], W=[vvB[i]])

            def pre(o_):
                i = o_ + 1
                qq = qTas[o_ % 2]
                norm_T(i, bank=1)
                for cq in range(8):
                    bk = 1 + cq // 4
                    for dc in range(8):
                        op("pe", lambda e, cq=cq, dc=dc, bk=bk: e.matmul(
                            pbank[bk][:, (cq % 4) * 128:(cq % 4 + 1) * 128], lhsT=watt[:, dc, cq * 128:(cq + 1) * 128],
                            rhs=xnT[:, dc, :], start=(dc == 0), stop=(dc == 7)),
                           R=[wattB, xnTB], W=[pbB[bk]])
                for b4 in range(2):
                    op("act", lambda e, b4=b4: e.copy(out=qq[:, b4 * 4:(b4 + 1) * 4, :],
                                                      in_=pbank[1 + b4][:, :].rearrange("p (a b) -> p a b", a=4)),
                       R=[pbB[1 + b4]], W=[qTaBss[o_ % 2][b4]])

            def head(o_, pend):
                i = o_ + 1
                qTa = qTas[o_ % 2]
                qTaBs = qTaBss[o_ % 2]
                edge = (o_ == 0) or (o_ == NO - 1)
                npull = (len(pend) + 15) // 16

                def st_S(h):
                    cq, hf, g = h // 2, h % 2, h // 4
                    q = h % 2
                    bk = 3 + q
                    ps_s = pbank[bk][:, 0:384]
                    op("pe", lambda e: e.matmul(
                        ps_s, lhsT=qTa[64 * hf:64 * hf + 64, cq, :],
                        rhs=kT[64 * hf:64 * hf + 64, g, (i - 1) * 128:(i + 2) * 128], start=True, stop=not edge),
                       R=[qTaBs[cq // 4], kTB[i - 1], kTB[i], kTB[i + 1]], W=[pbB[bk]])
                    if edge:
                        op("pe", lambda e: e.matmul(
                            ps_s, lhsT=ones1[0:1, :], rhs=penb[0:1, (i - 1) * 128:(i + 2) * 128], start=False, stop=True),
                           R=[ones1B, penB], W=[pbB[bk]])
                    op("dve", lambda e: e.scalar_tensor_tensor(
                        out=LL[q][:], in0=ps_s, scalar=-0.125, in1=biasm[:, h, :], op0=ALU.mult, op1=ALU.add),
                       R=[pbB[bk], biasmB], W=[LLB[q]])
                    op("dve", lambda e: e.tensor_reduce(out=nmrow[:, h:h + 1], in_=LL[q][:], axis=AX.X, op=ALU.min),
                       R=[LLB[q]], W=[nmrowB[h]])
                    op("act", lambda e: e.activation(out=EE[q][:], in_=LL[q][:], func=AF.Exp, scale=-1.0,
                                                     bias=nmrow[:, h:h + 1], accum_out=rsum[:, h:h + 1]),
                       R=[LLB[q], nmrowB[h]], W=[EEB[q], rsumB[h]])
                    op("act", lambda e: e.activation(out=esink[:, h:h + 1], in_=nmrow[:, h:h + 1], func=AF.Exp,
                                                     bias=sinkbc[:, h:h + 1], scale=1.0),
                       R=[nmrowB[h], sinkB], W=[esinkB[h]])

                def st_T(h):
                    q = h % 2
                    ptbank = 5 if q == 0 else 0
                    pt = pbank_bf[ptbank][:, 0:384].rearrange("p (a b) -> p a b", a=3)
                    ptB = pbB[ptbank]
                    for kb in range(3):
                        op("pe", lambda e, kb=kb: e.transpose(out=pt[:, kb, :], in_=EE[q][:, kb * 128:(kb + 1) * 128],
                                                              identity=ident[:]),
                           R=[EEB[q], identB], W=[ptB])
                    op("act", lambda e: e.copy(out=ET[q][:], in_=pt), R=[ptB], W=[ETB[q]])

                def st_V(h):
                    g = h // 4
                    q = h % 2
                    bko = 6 + h // 8
                    for kb in range(3):
                        op("pe", lambda e, kb=kb: e.matmul(
                            pbank[bko][:, (h % 8) * 64:(h % 8 + 1) * 64], lhsT=ET[q][:, kb, :],
                            rhs=vv[:, i - 1 + kb, g * 64:(g + 1) * 64], start=(kb == 0), stop=(kb == 2)),
                           R=[ETB[q], vvB[i - 1 + kb]], W=[pbB[bko]])

                for k_ in range(16 + 2):
                    if k_ < 16:
                        st_S(k_)
                    if 0 <= k_ - 1 < 16:
                        st_T(k_ - 1)
                    if 0 <= k_ - 2 < 16:
                        st_V(k_ - 2)
                    for _ in range(npull):
                        if pend:
                            pend.pop(0)[1]()
                while pend:
                    pend.pop(0)[1]()

            def tail_now(o_):
                op("dve", lambda e: e.tensor_tensor(out=den[:], in0=rsum[:], in1=esink[:], op=ALU.add),
                   R=rsumB + esinkB, W=[denB])
                op("dve", lambda e: e.reciprocal(out=rden[:], in_=den[:]), R=[denB], W=[rdenB])
                for hb in range(2):
                    op("dve", lambda e, hb=hb: e.tensor_tensor(
                        out=ao[:, hb * 512:(hb + 1) * 512].rearrange("p (h d) -> p h d", h=8),
                        in0=pbank[6 + hb][:, :].rearrange("p (h d) -> p h d", h=8),
                        in1=rden[:, hb * 8:(hb + 1) * 8].unsqueeze(2).broadcast_to([128, 8, 64]), op=ALU.mult),
                       R=[pbB[6 + hb], rdenB], W=[aoBs[hb]])

            def tail_def(o_):
                i = o_ + 1
                emit_transpose8(ao, aoBs, aoT[:], aoTB, bank=2)
                for half in range(2):
                    bk = 1 + half
                    for cc in range(8):
                        op("pe", lambda e, half=half, cc=cc, bk=bk: e.matmul(
                            pbank[bk][:, :], lhsT=aoT[:, cc, :], rhs=wo[:, cc, half * 512:(half + 1) * 512],
                            start=(cc == 0), stop=(cc == 7)),
                           R=[aoTB, woB], W=[pbB[bk]])
                    op("dve", lambda e, half=half, bk=bk: e.tensor_tensor(
                        out=xres[:, i, half * 512:(half + 1) * 512], in0=xres[:, i, half * 512:(half + 1) * 512],
                        in1=pbank[bk][:, :], op=ALU.add),
                       R=[pbB[bk], xresB[i]], W=[xresB[i]])

            pre(0)
            pend_tail = []
            for o_ in range(NO):
                pend = pend_tail
                if o_ + 1 < NO:
                    P.deferred = []
                    pre(o_ + 1)
                    pend = pend + P.deferred
                    P.deferred = None
                head(o_, pend)
                tail_now(o_)
                P.deferred = pend_tail = []
                tail_def(o_)
                P.deferred = None
            for _, th in pend_tail:
                th()

        if upto >= 4:
            emit_peer(1, list(range(1, NO + 1)), final=True)

        if dbg:
            for i in range(NT1):
                op("sp", lambda e, i=i: e.dma_start(out=dbg_d[i * 128:(i + 1) * 128, :], in_=xres[:, i, :]),
                   R=[xresB[i]], lane="dbg")
        P.raw("sp", lambda e: e.nop(), deps=P.fence())
        P.build()
    return nc


def _t5_bucket_np(rel):
    half = 16
    max_exact = 8
    ret = np.where(rel > 0, half, 0)
    n = np.abs(rel)
    nf = np.maximum(n, 1).astype(np.float32)
    large = max_exact + (np.log(nf / max_exact) / np.float32(np.log(128 / max_exact)) * (half - max_exact)).astype(np.int32)
    large = np.minimum(large, half - 1)
    return ret + np.where(n < max_exact, n, large)


def prep_shared(inp):
    f = lambda a: np.ascontiguousarray(np.asarray(a, dtype=np.float32))
    sh = {}
    sh["w_in"] = f(inp["conv_w_in"][0])
    cwv = np.asarray(inp["conv_w"][0], np.float32)
    sh["cw"] = f(cwv.T.reshape(8, 128, 3).transpose(1, 0, 2).reshape(128, 24))
    sh["w_out"] = f(inp["conv_w_out"][0])
    wqkv = np.asarray(inp["attn_w_qkv"][0], np.float32)
    wq_, wk_, wv_ = wqkv[:, :1024], wqkv[:, 1024:1280], wqkv[:, 1280:1536]
    kd = []
    for g in range(4):
        kd += [wk_[:, g * 64:(g + 1) * 64], wk_[:, g * 64:(g + 1) * 64]]
    sh["w_att"] = f(np.concatenate([wq_] + kd + [wv_], axis=1))
    sh["w_o"] = f(inp["attn_w_o"][0])
    sh["sink"] = f(np.asarray(inp["attn_sink"][0]).reshape(1, 16))
    qi = np.arange(128)[:, None]
    kj = np.arange(384)[None, :]
    rel = kj - 128 - qi
    bk = _t5_bucket_np(rel)
    rb = np.asarray(inp["rel_bias"], np.float32)
    sh["bias_tab"] = f(rb[bk].transpose(0, 2, 1).reshape(128, 16 * 384))
    sh["wmask"] = f(np.where(np.abs(rel) <= 128, 0.0, -30000.0))
    sh["gains"] = f(np.stack([inp["conv_norm_g"][0], inp["ffn_norm_g"][0], inp["attn_norm_g"][0],
                              inp["ffn_norm_g"][1], inp["final_norm_g"]], axis=0))
    sh["w_pq"] = f(inp["peer_w_q"])
    sk = np.asarray(inp["peer_subkeys"], np.float32)
    sh["skT"] = f(sk.transpose(0, 4, 1, 2, 3).reshape(2, 128, 2048))
    for l in range(2):
        sh["uv%d" % l] = f(np.concatenate([np.asarray(inp["peer_u"][l], np.float32),
                                            np.asarray(inp["peer_v"][l], np.float32)], axis=1))
    sh["iota"] = f(np.broadcast_to(np.arange(16, dtype=np.float32), (128, 16)))
    return sh


def prep_core(x, b, k, NO, S):
    NT1, NX = NO + 2, NO + 4
    own0 = k * NO * 128
    lo = own0 - 256
    xe = np.zeros((NX * 128, D), np.float32)
    a, e = max(lo, 0), min(lo + NX * 128, S)
    xe[a - lo:e - lo] = x[b, a:e]
    pen = np.zeros((1, NT1 * 128), np.float32)
    t = own0 - 128 + np.arange(NT1 * 128)
    pen[0, (t < 0) | (t >= S)] = -240000.0
    return {"x_ext": xe, "pen": pen}


_NC_CACHE = {}


def kernel(**inputs):
    x = np.asarray(inputs["x"], np.float32)
    B, S, _ = x.shape
    NO = 16
    ncores = 8
    per_b = ncores // B
    sh = prep_shared(inputs)
    in_maps = []
    for c in range(ncores):
        b, k = c // per_b, c % per_b
        m = dict(sh)
        m.update(prep_core(x, b, k, NO, S))
        in_maps.append(m)
    nc = build(NO=NO)
    res = run_bass_kernel_spmd(nc, in_maps, core_ids=list(range(ncores)))
    out = np.zeros((B, S, D), np.float32)
    for c in range(ncores):
        b, k = c // per_b, c % per_b
        out[b, k * NO * 128:(k + 1) * NO * 128] = res.results[c]["y"]
    return out
```

```python
import contextlib
import os
import numpy as np
import concourse.bass as bass
import concourse.mybir as mybir
from concourse.bass_utils import run_bass_kernel_spmd

F32 = mybir.dt.float32
BF16 = mybir.dt.bfloat16
U32 = mybir.dt.uint32
ALU = mybir.AluOpType
AF = mybir.ActivationFunctionType
AX = mybir.AxisListType

D = 1024
EPS = 1e-6
SAME_ENGINE_SYNC = os.environ.get("K_SES", "1") == "1"
NBUF_G = int(os.environ.get("K_NB", "10"))
PE_DOT = int(os.environ.get("K_PEDOT", "0"))
JG = int(os.environ.get("K_JG", "1"))
RG_ENG = os.environ.get("K_RG", "dve")
FRONT_PULL = 3


class Op:
    __slots__ = ("eng", "fn", "deps", "lane", "count", "signaled")

    def __init__(self, eng, fn, deps, lane):
        self.eng = eng
        self.fn = fn
        self.deps = deps
        self.lane = lane
        self.count = None
        self.signaled = False


class Buf:
    __slots__ = ("w", "r")

    def __init__(self):
        self.w = {}
        self.r = {}


class Prog:
    ENGS = ("pe", "act", "dve", "pool", "sp")

    def __init__(self, nc):
        self.nc = nc
        self.ops = []
        self.lane_last = {}
        self.deferred = None

    def call(self, fn):
        if self.deferred is not None:
            self.deferred.append(("none", fn))
        else:
            fn()

    def raw(self, eng, fn, deps=(), lane=None):
        deps = [d for d in deps if d is not None]
        if lane is not None:
            prev = self.lane_last.get(lane)
            if prev is not None:
                deps.append(prev)
        o = Op(eng, fn, deps, lane)
        if lane is not None:
            self.lane_last[lane] = o
        self.ops.append(o)
        return o

    def op(self, eng, fn, R=(), W=(), lane=None, extra=()):
        if self.deferred is not None:
            self.deferred.append((eng, lambda: self._op(eng, fn, R, W, lane, extra)))
            return None
        return self._op(eng, fn, R, W, lane, extra)

    def _op(self, eng, fn, R=(), W=(), lane=None, extra=()):
        deps = list(extra)
        for b in R:
            deps.extend(b.w.values())
        for b in W:
            deps.extend(b.w.values())
            deps.extend(b.r.values())
        o = self.raw(eng, fn, deps, lane)
        key = ("l", lane) if lane is not None else ("e", eng)
        for b in R:
            b.r[key] = o
        for b in W:
            b.w = {key: o}
            b.r = {}
        return o

    def fence(self):
        last = {}
        for o in self.ops:
            key = ("l", o.lane) if o.lane is not None else ("e", o.eng)
            last[key] = o
        return list(last.values())

    def build(self):
        nc = self.nc
        for o in self.ops:
            nd = []
            seen = set()
            for d in o.deps:
                if id(d) in seen:
                    continue
                seen.add(id(d))
                if d.lane is None and d.eng == o.eng and o.lane is None:
                    if d.eng == "pe" or not SAME_ENGINE_SYNC:
                        continue
                nd.append(d)
            o.deps = nd
            for d in nd:
                d.signaled = True
        lanes = sorted({o.lane for o in self.ops if o.lane is not None})
        with contextlib.ExitStack() as es:
            esem = {e: es.enter_context(nc.semaphore("s_" + e)) for e in self.ENGS}
            lsem = {l: es.enter_context(nc.semaphore("l_" + str(l))) for l in lanes}
            ecount = {e: 0 for e in self.ENGS}
            lcount = {l: 0 for l in lanes}
            for o in self.ops:
                if o.lane is not None:
                    lcount[o.lane] += 16
                    o.count = lcount[o.lane]
                elif o.signaled:
                    ecount[o.eng] += 1
                    o.count = ecount[o.eng]
            self.final_counts = (dict(ecount), dict(lcount))
            block = es.enter_context(nc.Block())
            ops = self.ops

            def emit_for(engname):
                def body(eng):
                    waited = {}
                    for o in ops:
                        if o.eng != engname:
                            continue
                        need = {}
                        for d in o.deps:
                            key = ("l", d.lane) if d.lane is not None else ("e", d.eng)
                            if d.count > need.get(key, 0):
                                need[key] = d.count
                        for key, cnt in need.items():
                            if waited.get(key, 0) >= cnt:
                                continue
                            sem = lsem[key[1]] if key[0] == "l" else esem[key[1]]
                            eng.wait_ge(sem, cnt)
                            waited[key] = cnt
                        ins = o.fn(eng)
                        if o.lane is not None:
                            ins.then_inc(lsem[o.lane], 16)
                        elif o.signaled:
                            ins.then_inc(esem[o.eng], 1)
                return body

            block.tensor(emit_for("pe"))
            block.scalar(emit_for("act"))
            block.vector(emit_for("dve"))
            block.gpsimd(emit_for("pool"))
            block.sync(emit_for("sp"))


def _dsize(dt):
    return {F32: 4, BF16: 2, U32: 4}[dt]


class Arena:
    def __init__(self, nc, start, top):
        self.nc = nc
        self.p = start
        self.top = top
        self.n = 0

    def alloc(self, name, shape, dt):
        nbytes = int(np.prod(shape[1:])) * _dsize(dt)
        off = (self.p + 31) // 32 * 32
        self.p = off + nbytes
        assert self.p <= self.top, (name, self.p, self.top)
        self.n += 1
        return self.nc.alloc_sbuf_tensor_at(name, list(shape), dt, offset=off)

    def fork(self):
        return Arena(self.nc, self.p, self.top)


def build(NO=16, upto=99, dbg=False):
    NT1 = NO + 2
    NX = NO + 4
    nc = bass.Bass("TRN2", target_bir_lowering=False)

    def dr(name, shape, dt=F32, kind="ExternalInput"):
        return nc.dram_tensor(name, list(shape), dt, kind=kind).ap()

    x_ext = dr("x_ext", [NX * 128, D])
    pen_d = dr("pen", [1, NT1 * 128])
    w_in_d = dr("w_in", [D, 3 * D])
    cw_d = dr("cw", [128, 24])
    w_out_d = dr("w_out", [D, D])
    w_att_d = dr("w_att", [D, 1792])
    w_o_d = dr("w_o", [D, D])
    sink_d = dr("sink", [1, 16])
    bias_d = dr("bias_tab", [128, 16 * 384])
    wmask_d = dr("wmask", [128, 384])
    gains_d = dr("gains", [5, D])
    w_pq_d = dr("w_pq", [2, D, 2048])
    skT_d = dr("skT", [2, 128, 2048])
    uv_d = [dr("uv0", [16384, 2048]), dr("uv1", [16384, 2048])]
    iota_d = dr("iota", [128, 16])
    uvb_d = [nc.dram_tensor("uvb%d" % l, [16384, 2048], BF16, kind="Internal").ap() for l in range(2)]
    y_d = dr("y", [NO * 128, D], kind="ExternalOutput")
    if dbg:
        dbg_d = dr("dbg", [NT1 * 128, D], kind="ExternalOutput")

    P = Prog(nc)
    op = P.op

    with contextlib.ExitStack() as es:
        pbank = [es.enter_context(nc.psum_tensor("pb%d" % i, [128, 512], F32)) for i in range(8)]
        pbB = [Buf() for _ in range(8)]
        pbB5b = Buf()
        pbank_bf = [p.bitcast(BF16) for p in pbank]

        A0 = Arena(nc, (nc.sbuf_base + 63) // 64 * 64, nc.sbuf_top)
        xres = A0.alloc("xres", [128, NT1, D], F32)
        xresB = [Buf() for _ in range(NT1)]
        ident = A0.alloc("ident", [128, 128], BF16)
        identB = Buf()
        identf = A0.alloc("identf", [128, 128], F32)
        identfB = Buf()
        iota16 = A0.alloc("iota16", [128, 16], F32)
        iotaB = Buf()
        gA = A0.alloc("gA", [128, D], F32)
        gAB = Buf()
        gB = A0.alloc("gB", [128, D], F32)
        gBB = Buf()
        stat = A0.alloc("stat", [128, 96], F32)
        statB = [Buf() for _ in range(96)]
        junk = A0.alloc("junk", [128, D], BF16)
        junkB = Buf()
        junk2_off = (A0.p + 31) // 32 * 32
        junk2 = A0.alloc("junk2", [128, D], BF16)
        junk3 = A0.alloc("junk3", [128, D], BF16)
        junk2s = [junk2, junk3]
        junk2B = [Buf(), Buf()]
        xnb = A0.alloc("xnb", [128, D], BF16)
        xnbB = Buf()
        xnT = A0.alloc("xnT", [128, 8, 128], BF16)
        xnTB = Buf()

        stat_ctr = [0]

        def new_stat():
            k = stat_ctr[0] % 96
            stat_ctr[0] += 1
            return stat[:, k:k + 1], statB[k]

        op("pool", lambda e: e.memset(ident[:], 1.0), W=[identB])
        op("pool", lambda e: e.affine_select(out=ident[:], in_=ident[:], pattern=[[-1, 128]],
                                             compare_op=ALU.is_equal, fill=0.0, base=0,
                                             channel_multiplier=1), R=[identB], W=[identB])
        op("pool", lambda e: e.tensor_copy(out=identf[:], in_=ident[:]), R=[identB], W=[identfB])
        op("sp", lambda e: e.dma_start(out=iota16[:], in_=iota_d[:, :]), W=[iotaB], lane="c_iota")

        def load_gain(dst, dstB, row):
            return op("sp", lambda e: e.dma_start(out=dst[:], in_=gains_d[row:row + 1, :].partition_broadcast(128)),
                      W=[dstB], lane="gain")

        def emit_rstd(src_ap, srcB):
            ss, ssB = new_stat()
            op("act", lambda e: e.activation(out=junk[:], in_=src_ap, func=AF.Square, accum_out=ss),
               R=[srcB], W=[ssB, junkB])
            sd, sdB = new_stat()
            op("act", lambda e: e.activation(out=sd, in_=ss, func=AF.Sqrt, bias=EPS_AP[0], scale=1.0 / D),
               R=[ssB, epsB], W=[sdB])
            r, rB = new_stat()
            op("dve", lambda e: e.reciprocal(out=r, in_=sd), R=[sdB], W=[rB])
            return r, rB

        def emit_transpose8(src_bf, srcB_, dst_ap, dstB_, bank=0):
            pv = pbank_bf[bank][:, 0:1024].rearrange("p (a b) -> p a b", a=8)
            for dc in range(8):
                op("pe", lambda e, dc=dc: e.transpose(out=pv[:, dc, :], in_=src_bf[:, dc * 128:(dc + 1) * 128],
                                                      identity=ident[:]),
                   R=(list(srcB_) if isinstance(srcB_, (list, tuple)) else [srcB_]) + [identB], W=[pbB[bank]])
            op("act", lambda e: e.copy(out=dst_ap, in_=pv), R=[pbB[bank]], W=[dstB_])

        epsT = A0.alloc("epsT", [128, 1], F32)
        epsB = Buf()
        EPS_AP = [epsT[:, 0:1]]
        op("pool", lambda e: e.memset(epsT[:], EPS), W=[epsB])

        AP0 = A0

        NCV = 16
        RCV = 16384 // NCV
        uvbB = [Buf(), Buf()]

        CV_ROWS = {0: RCV, 1: RCV}
        CV_LANES = {0: NCV, 1: NCV}
        conv_left = {0: list(range(16384 // CV_ROWS[0])), 1: list(range(16384 // CV_ROWS[1]))}

        def emit_convert(layer, nchunks=10 ** 9):
            rows = CV_ROWS[layer]
            for _ in range(nchunks):
                if not conv_left[layer]:
                    break
                k = conv_left[layer].pop(0)
                op("pool", lambda e, k=k, rows=rows: e.dma_start(out=uvb_d[layer][k * rows:(k + 1) * rows, :],
                                                                 in_=uv_d[layer][k * rows:(k + 1) * rows, :]),
                   W=[], lane="cv%d_%d" % (layer, k % CV_LANES[layer]))
            if not conv_left[layer]:
                uvbB[layer].w = {("l", "cv%d_%d" % (layer, q)): P.lane_last["cv%d_%d" % (layer, q)]
                                 for q in range(CV_LANES[layer]) if ("cv%d_%d" % (layer, q)) in P.lane_last}

        if upto >= 1:
            A = AP0.fork()
            xtmp = nc.alloc_sbuf_tensor_at("xtmp", [128, D], F32, offset=junk2_off)
            xtmpB = Buf()
            xnb2 = A.alloc("xnb2", [128, D], BF16)
            xnb2B = Buf()
            win = A.alloc("win", [128, 8, 3 * D], BF16)
            winB = Buf()
            wout = A.alloc("wout", [128, 8, D], BF16)
            woutB = Buf()
            cw = A.alloc("cw", [128, 8, 3], F32)
            cwB = Buf()
            xnT_all = A.alloc("xnT_all", [128, 8, NX * 128], BF16)
            xnT_allB = [Buf() for _ in range(NX)]
            hsb = [A.alloc("hsb%d" % i, [128, 130], F32) for i in range(2)]
            hsbB = [Buf() for _ in range(2)]
            zsb = [A.alloc("zsb%d" % i, [128, 130], F32) for i in range(2)]
            zsbB = [Buf() for _ in range(2)]
            ysb = [A.alloc("ysb%d" % i, [128, 128], F32) for i in range(2)]
            ysbB = [Buf() for _ in range(2)]
            gT = [A.alloc("gT%d" % i, [128, 8, 128], BF16) for i in range(2)]
            gTB = [[Buf() for _ in range(8)] for _ in range(2)]

            for k in range(6):
                op("pool", lambda e, k=k: e.dma_start(
                    out=win[:, :, k * 512:(k + 1) * 512],
                    in_=w_in_d[:, k * 512:(k + 1) * 512].rearrange("(dc dp) n -> dp dc n", dp=128)),
                   W=[winB], lane="w%d" % (k % 4))
            op("pool", lambda e: e.dma_start(out=wout[:], in_=w_out_d.rearrange("(dc dp) n -> dp dc n", dp=128)),
               W=[woutB], lane="w1")
            op("sp", lambda e: e.dma_start(out=cw[:], in_=cw_d.rearrange("p (c k) -> p c k", k=3)), W=[cwB], lane="c_cw")
            load_gain(gA, gAB, 0)

            for e_ in range(NX):
                if e_ >= 4 and e_ % 2 == 0:
                    emit_convert(0, 1)
                if 1 <= e_ <= NT1:
                    dst, dB = xres[:, e_ - 1, :], xresB[e_ - 1]
                else:
                    dst, dB = xtmp[:], xtmpB
                op("sp", lambda e, e_=e_, dst=dst: e.dma_start(out=dst, in_=x_ext[e_ * 128:(e_ + 1) * 128, :]),
                   W=[dB], lane="xl%d" % (e_ % 4))
                r, rB = emit_rstd(dst, dB)
                xb_, xbB_ = (xnb, xnbB) if e_ % 2 == 0 else (xnb2, xnb2B)
                op("dve", lambda e, dst=dst, r=r, xb_=xb_: e.scalar_tensor_tensor(out=xb_[:], in0=dst, scalar=r, in1=gA[:],
                                                                                 op0=ALU.mult, op1=ALU.mult),
                   R=[dB, rB, gAB], W=[xbB_])
                emit_transpose8(xb_, xbB_, xnT_all[:, :, e_ * 128:(e_ + 1) * 128], xnT_allB[e_], bank=(0 if e_ % 2 == 0 else 7))

            for i in range(NT1):
                emit_convert(0, 1)
                e_ = i + 1
                c0 = 128 * e_ - 1
                par = i % 2
                for cc in range(8):
                    q = cc % 2
                    bk = 1 + q
                    psB = pbank[bk][:, 0:130]
                    psC = pbank[bk][:, 130:260]
                    psH = pbank[bk][:, 260:390]
                    for wi, pso in enumerate((psB, psC, psH)):
                        for dc in range(8):
                            op("pe", lambda e, wi=wi, pso=pso, dc=dc, cc=cc, c0=c0: e.matmul(
                                pso, lhsT=win[:, dc, wi * D + cc * 128: wi * D + (cc + 1) * 128],
                                rhs=xnT_all[:, dc, c0:c0 + 130], start=(dc == 0), stop=(dc == 7)),
                               R=[winB, xnT_allB[e_ - 1], xnT_allB[e_], xnT_allB[e_ + 1]], W=[pbB[bk]])
                    op("act", lambda e, q=q, psH=psH: e.copy(out=hsb[q][:], in_=psH), R=[pbB[bk]], W=[hsbB[q]])
                    op("dve", lambda e, q=q, psC=psC: e.tensor_tensor(out=zsb[q][:], in0=psC, in1=hsb[q][:], op=ALU.mult),
                       R=[pbB[bk], hsbB[q]], W=[zsbB[q]])
                    op("dve", lambda e, q=q, cc=cc: e.tensor_scalar(out=ysb[q][:], in0=zsb[q][:, 0:128],
                                                                   scalar1=cw[:, cc, 0:1], scalar2=None, op0=ALU.mult),
                       R=[zsbB[q], cwB], W=[ysbB[q]])
                    for kk in (1, 2):
                        op("dve", lambda e, q=q, cc=cc, kk=kk: e.scalar_tensor_tensor(
                            out=ysb[q][:], in0=zsb[q][:, kk:kk + 128], scalar=cw[:, cc, kk:kk + 1], in1=ysb[q][:],
                            op0=ALU.mult, op1=ALU.add),
                           R=[zsbB[q], cwB, ysbB[q]], W=[ysbB[q]])
                    op("dve", lambda e, q=q, cc=cc, par=par, psB=psB: e.tensor_tensor(
                        out=gT[par][:, cc, :], in0=psB[:, 1:129], in1=ysb[q][:], op=ALU.mult),
                       R=[pbB[bk], ysbB[q]], W=[gTB[par][cc]])
                for half in range(2):
                    bk = 3 + half
                    for cc in range(8):
                        op("pe", lambda e, half=half, cc=cc, par=par, bk=bk: e.matmul(
                            pbank[bk][:, :], lhsT=gT[par][:, cc, :], rhs=wout[:, cc, half * 512:(half + 1) * 512],
                            start=(cc == 0), stop=(cc == 7)),
                           R=[gTB[par][cc], woutB], W=[pbB[bk]])
                    op("dve", lambda e, half=half, i=i, bk=bk: e.tensor_tensor(
                        out=xres[:, i, half * 512:(half + 1) * 512], in0=xres[:, i, half * 512:(half + 1) * 512],
                        in1=pbank[bk][:, :], op=ALU.add),
                       R=[pbB[bk], xresB[i]], W=[xresB[i]])

        def emit_peer(layer, tiles, final=False):
            fence = P.fence()
            A = AP0.fork()
            wq = A.alloc("wq%d" % layer, [128, 8, 2048], BF16)
            wqB = Buf()
            sk = A.alloc("sk%d" % layer, [128, 16, 128], BF16)
            skB = Buf()
            xn = [A.alloc("xn%d_%d" % (layer, i), [128, D], F32) for i in range(2)]
            xnB = [Buf() for _ in range(2)]
            qT = A.alloc("qT%d" % layer, [128, 16, 128], BF16)
            qTBs = [Buf() for _ in range(4)]
            sc = A.alloc("sc%d" % layer, [128, 16, 128], F32)
            scBs = [Buf() for _ in range(4)]
            wk = A.alloc("wk%d" % layer, [128, 128], F32)
            wkB = Buf()
            mx = A.alloc("mx%d" % layer, [128, 16, 16], F32)
            mxB = Buf()
            ixu = A.alloc("ixu%d" % layer, [128, 16, 16], U32)
            ixuB = Buf()
            ixf = A.alloc("ixf%d" % layer, [128, 16, 16], F32)
            ixfB = Buf()
            cand = A.alloc("cand%d" % layer, [128, 256], F32)
            candB = Buf()
            cwk = A.alloc("cwk%d" % layer, [128, 256], F32)
            cwkB = Buf()
            tv = A.alloc("tv%d" % layer, [128, 8, 16], F32)
            tvB = Buf()
            pos = A.alloc("pos%d" % layer, [128, 8, 16], U32)
            posB = Buf()
            pa = A.alloc("pa%d" % layer, [128, 8, 16], U32)
            paB = Buf()
            pb_ = A.alloc("pb%d_" % layer, [128, 8, 16], U32)
            pbB_ = Buf()
            paf = A.alloc("paf%d" % layer, [128, 8, 16], F32)
            pafB = Buf()
            pbf = A.alloc("pbf%d" % layer, [128, 8, 16], F32)
            pbfB = Buf()
            s0 = A.alloc("s0%d" % layer, [128, 8, 16], F32)
            s0B = Buf()
            s1 = A.alloc("s1%d" % layer, [128, 8, 16], F32)
            s1B = Buf()
            idxf = A.alloc("idxf%d" % layer, [128, 128], F32)
            idxfB = Buf()
            idxu = [A.alloc("idxu%d_%d" % (layer, i), [128, 128], U32) for i in range(2)]
            idxuB = [Buf() for _ in range(2)]
            ntv = A.alloc("ntv%d" % layer, [128, 8], F32)
            ntvB = Buf()
            ee = A.alloc("ee%d" % layer, [128, 8, 16], F32)
            eeB = Buf()
            zz = A.alloc("zz%d" % layer, [128, 8], F32)
            zzB = [Buf() for _ in range(8)]
            rz = A.alloc("rz%d" % layer, [128, 8], F32)
            rzB = Buf()
            gg = [A.alloc("gg%d_%d" % (layer, i), [128, 128], F32) for i in range(2)]
            ggB = [Buf() for _ in range(2)]
            hh = A.alloc("hh%d" % layer, [128, 2, 128], F32)
            hhB = [[Buf() for _ in range(128)] for _ in range(2)]
            gl = A.alloc("gl%d" % layer, [128, 2, 128], F32)
            glB = [[Buf() for _ in range(128 // JG)] for _ in range(2)]
            aa = A.alloc("aa%d" % layer, [128, 2, 128], F32)
            aaB = [[Buf() for _ in range(128 // JG)] for _ in range(2)]
            aaB2 = [[Buf() for _ in range(128 // JG)] for _ in range(2)]
            dd = A.alloc("dd%d" % layer, [128, 2 * JG, 128], BF16)
            ddB = [Buf() for _ in range(2 * JG)]
            gbuf = [A.alloc("gb%d_%d" % (layer, i), [128, 2048], BF16) for i in range(NBUF_G)]
            gbufB = [Buf() for _ in range(NBUF_G)]
            yt = None
            xnT2 = A.alloc("xnT2_%d" % layer, [128, 8, 128], BF16)
            xnTp = [xnT, xnT2]
            xnTpB = [xnTB, Buf()]
            ugT = [A.alloc("ugT%d_%d" % (layer, i), [128, 8, 128], BF16) for i in range(2)] if PE_DOT > 0 else None
            ugTB = [Buf() for _ in range(2)]
            pv7 = pbank_bf[7][:, 0:1024].rearrange("p (a b) -> p a b", a=8)
            tb = sc

            for k in range(4):
                op("pool", lambda e, k=k: e.dma_start(
                    out=wq[:, :, k * 512:(k + 1) * 512],
                    in_=w_pq_d[layer, :, k * 512:(k + 1) * 512].rearrange("(dc dp) n -> dp dc n", dp=128)),
                   W=[wqB], lane="w%d" % k, extra=fence)
            op("pool", lambda e: e.dma_start(out=sk[:], in_=skT_d[layer].rearrange("p (c n) -> p c n", n=128)),
               W=[skB], lane="w1", extra=fence)
            load_gain(gB, gBB, 1 if layer == 0 else 3)
            if final:
                load_gain(gA, gAB, 4)

            scflat = sc[:, :, :].rearrange("p a b -> p (a b)")
            T4 = scflat.rearrange("p (h k a) -> p h k a", h=8, k=16)

            def front(ti, i):
                par = ti % 2
                xr = xres[:, i, :]
                r, rB = emit_rstd(xr, xresB[i])
                op("dve", lambda e: e.scalar_tensor_tensor(out=xn[par][:], in0=xr, scalar=r, in1=gB[:],
                                                           op0=ALU.mult, op1=ALU.mult),
                   R=[xresB[i], rB, gBB], W=[xnB[par]], extra=(fence if ti < 2 else ()))
                op("act", lambda e: e.copy(out=xnb[:], in_=xn[par][:]), R=[xnB[par]], W=[xnbB])
                emit_transpose8(xnb, xnbB, xnTp[par][:], xnTpB[par])
                for c in range(16):
                    bk = 1 + c // 4
                    for dc in range(8):
                        op("pe", lambda e, c=c, dc=dc, bk=bk: e.matmul(
                            pbank[bk][:, (c % 4) * 128:(c % 4 + 1) * 128], lhsT=wq[:, dc, c * 128:(c + 1) * 128],
                            rhs=xnTp[par][:, dc, :], start=(dc == 0), stop=(dc == 7)),
                           R=[wqB, xnTpB[par]], W=[pbB[bk]])
                for b4 in range(4):
                    op("act", lambda e, b4=b4: e.copy(out=qT[:, b4 * 4:(b4 + 1) * 4, :],
                                                      in_=pbank[1 + b4][:, :].rearrange("p (a b) -> p a b", a=4)),
                       R=[pbB[1 + b4]], W=[qTBs[b4]])
                for c in range(16):
                    bk = 1 + c // 4
                    op("pe", lambda e, c=c, bk=bk: e.matmul(
                        pbank[bk][:, (c % 4) * 128:(c % 4 + 1) * 128], lhsT=qT[:, c, :], rhs=sk[:, c, :],
                        start=True, stop=True),
                       R=[qTBs[c // 4], skB], W=[pbB[bk]])
                for b4 in range(4):
                    op("act", lambda e, b4=b4: e.copy(out=sc[:, b4 * 4:(b4 + 1) * 4, :],
                                                      in_=pbank[1 + b4][:, :].rearrange("p (a b) -> p a b", a=4)),
                       R=[pbB[1 + b4]], W=[scBs[b4]])
                for c in range(16):
                    op("dve", lambda e, c=c: e.max(out=mx[:, c, 0:8], in_=sc[:, c, :]), R=[scBs[c // 4]], W=[mxB])
                    op("dve", lambda e, c=c: e.max_index(out=ixu[:, c, 0:8], in_max=mx[:, c, 0:8], in_values=sc[:, c, :]),
                       R=[scBs[c // 4], mxB], W=[ixuB])
                    op("dve", lambda e, c=c: e.match_replace(out=wk[:], in_to_replace=mx[:, c, 0:8], in_values=sc[:, c, :],
                                                             imm_value=-1e30), R=[scBs[c // 4], mxB], W=[wkB])
                    op("dve", lambda e, c=c: e.max(out=mx[:, c, 8:16], in_=wk[:]), R=[wkB], W=[mxB])
                    op("dve", lambda e, c=c: e.max_index(out=ixu[:, c, 8:16], in_max=mx[:, c, 8:16], in_values=wk[:]),
                       R=[wkB, mxB], W=[ixuB])
                op("dve", lambda e: e.tensor_copy(out=ixf[:], in_=ixu[:]), R=[ixuB], W=[ixfB])
                mx4 = mx[:, :, :].rearrange("p (h t) k -> p h t k", t=2)
                ixf4 = ixf[:, :, :].rearrange("p (h t) k -> p h t k", t=2)
                op("dve", lambda e: e.tensor_tensor(
                    out=T4, in0=mx4[:, :, 0, :].unsqueeze(3).broadcast_to([128, 8, 16, 16]),
                    in1=mx4[:, :, 1, :].unsqueeze(2).broadcast_to([128, 8, 16, 16]), op=ALU.add),
                   R=[mxB], W=scBs)
                for h in range(8):
                    cand_h = scflat[:, h * 256:(h + 1) * 256]
                    op("dve", lambda e, h=h, cand_h=cand_h: e.max(out=tv[:, h, 0:8], in_=cand_h), R=scBs, W=[tvB])
                    op("dve", lambda e, h=h, cand_h=cand_h: e.max_index(out=pos[:, h, 0:8], in_max=tv[:, h, 0:8],
                                                                        in_values=cand_h),
                       R=scBs + [tvB], W=[posB])
                    op("dve", lambda e, h=h, cand_h=cand_h: e.match_replace(out=cwk[:], in_to_replace=tv[:, h, 0:8],
                                                                            in_values=cand_h, imm_value=-1e30),
                       R=scBs + [tvB], W=[cwkB])
                    op("dve", lambda e, h=h: e.max(out=tv[:, h, 8:16], in_=cwk[:]), R=[cwkB], W=[tvB])
                    op("dve", lambda e, h=h: e.max_index(out=pos[:, h, 8:16], in_max=tv[:, h, 8:16], in_values=cwk[:]),
                       R=[cwkB, tvB], W=[posB])
                op("dve", lambda e: e.tensor_tensor(out=ee[:], in0=tv[:],
                                                    in1=tv[:, :, 0].unsqueeze(2).broadcast_to([128, 8, 16]),
                                                    op=ALU.subtract), R=[tvB], W=[eeB])
                op("act", lambda e: e.activation(out=ee[:], in_=ee[:], func=AF.Exp), R=[eeB], W=[eeB])
                op("dve", lambda e: e.tensor_single_scalar(out=pa[:], in_=pos[:], scalar=4, op=ALU.logical_shift_right),
                   R=[posB], W=[paB])
                op("dve", lambda e: e.tensor_single_scalar(out=pb_[:], in_=pos[:], scalar=15, op=ALU.bitwise_and),
                   R=[posB], W=[pbB_])
                op("dve", lambda e: e.tensor_copy(out=paf[:], in_=pa[:]), R=[paB], W=[pafB])
                op("dve", lambda e: e.tensor_copy(out=pbf[:], in_=pb_[:]), R=[pbB_], W=[pbfB])
                io4 = iota16[:, :].unsqueeze(1).unsqueeze(1).broadcast_to([128, 8, 16, 16])
                for (rk, rkB, side, dst, dstB) in ((paf, pafB, 0, s0, s0B), (pbf, pbfB, 1, s1, s1B)):
                    op(RG_ENG, lambda e, rk=rk: e.tensor_tensor(
                        out=T4, in0=io4, in1=rk[:, :, :].unsqueeze(3).broadcast_to([128, 8, 16, 16]), op=ALU.is_equal),
                       R=[iotaB, rkB], W=scBs)
                    op(RG_ENG, lambda e, side=side: e.tensor_tensor(
                        out=T4, in0=T4, in1=ixf4[:, :, side, :].unsqueeze(2).broadcast_to([128, 8, 16, 16]), op=ALU.mult),
                       R=scBs + [ixfB], W=scBs)
                    op("dve", lambda e, dst=dst: e.tensor_reduce(out=dst[:], in_=T4, axis=AX.X, op=ALU.add),
                       R=scBs, W=[dstB])
                op("dve", lambda e: e.scalar_tensor_tensor(
                    out=idxf[:, :].rearrange("p (h k) -> p h k", h=8), in0=s0[:], scalar=128.0, in1=s1[:],
                    op0=ALU.mult, op1=ALU.add), R=[s0B, s1B], W=[idxfB])
                op("dve", lambda e: e.tensor_copy(out=idxu[par][:], in_=idxf[:]), R=[idxfB], W=[idxuB[par]])
                op("dve", lambda e: e.tensor_reduce(out=zz[:], in_=ee[:], axis=AX.X, op=ALU.add), R=[eeB], W=[zzB[0]])
                op("dve", lambda e: e.reciprocal(out=rz[:], in_=zz[:]), R=[zzB[0]], W=[rzB])
                op("dve", lambda e: e.tensor_tensor(
                    out=gg[par][:, :].rearrange("p (h k) -> p h k", h=8), in0=ee[:],
                    in1=rz[:, :].unsqueeze(2).broadcast_to([128, 8, 16]), op=ALU.mult),
                   R=[eeB, rzB], W=[ggB[par]])

            def back(ti, i, pending):
                par = ti % 2
                for j in range(128):
                    ndve = 2 if (j % 2 == 1 and j >= 24) else 1
                    nother = 12
                    while pending:
                        en_ = pending[0][0]
                        if en_ == "dve":
                            if ndve == 0:
                                break
                            ndve -= 1
                        else:
                            if nother == 0:
                                break
                            nother -= 1
                        pending.pop(0)[1]()
                    b = (ti * 128 + j) % NBUF_G
                    op("pool", lambda e, b=b, j=j: e.indirect_dma_start(
                        out=gbuf[b][:], out_offset=None, in_=uvb_d[layer][:, :],
                        in_offset=bass.IndirectOffsetOnAxis(ap=idxu[par][:, j:j + 1], axis=0)),
                       R=[idxuB[par], uvbB[layer]], W=[gbufB[b]], lane="g%d" % b)
                    routed = PE_DOT > 0 and (j % PE_DOT == 0) and (PE_DOT % 2 == 0)
                    if routed:
                        kk = (j // PE_DOT) % 2
                        for dc in range(8):
                            op("pe", lambda e, b=b, dc=dc: e.transpose(out=pv7[:, dc, :], in_=gbuf[b][:, dc * 128:(dc + 1) * 128],
                                                                       identity=ident[:]),
                               R=[gbufB[b], identB], W=[pbB[7]])
                        op("act", lambda e, kk=kk: e.copy(out=ugT[kk][:], in_=pv7), R=[pbB[7]], W=[ugTB[kk]])
                    else:
                        op("dve", lambda e, b=b, j=j: e.scalar_tensor_tensor(
                            out=junk2s[j % 2][:], in0=gbuf[b][:, 0:D], scalar=1.0, in1=xn[par][:], op0=ALU.mult,
                            op1=ALU.mult, accum_out=hh[:, par, j:j + 1]),
                           R=[gbufB[b], xnB[par]], W=[hhB[par][j], junk2B[j % 2]])
                    if PE_DOT > 0 and (PE_DOT % 2 == 0) and (j % PE_DOT == 1):
                        j0 = j - 1
                        kk = (j0 // PE_DOT) % 2
                        for dc in range(8):
                            op("pe", lambda e, kk=kk, dc=dc: e.matmul(
                                pbank[0][:, 0:128], lhsT=ugT[kk][:, dc, :], rhs=xnTp[par][:, dc, :],
                                start=(dc == 0), stop=(dc == 7)),
                               R=[ugTB[kk], xnTpB[par]], W=[pbB[0]])
                        op("dve", lambda e, j0=j0: e.scalar_tensor_tensor(
                            out=junk2s[0][:, 0:128], in0=pbank[0][:, 0:128], scalar=1.0, in1=identf[:], op0=ALU.mult,
                            op1=ALU.mult, accum_out=hh[:, par, j0:j0 + 1]),
                           R=[pbB[0], identfB], W=[hhB[par][j0], junk2B[0]])
                    if j % JG == JG - 1:
                        g0 = j - (JG - 1)
                        gi = g0 // JG
                        op("act", lambda e, g0=g0: e.activation(out=gl[:, par, g0:g0 + JG], in_=hh[:, par, g0:g0 + JG],
                                                                func=AF.Gelu),
                           R=[hhB[par][g0 + t] for t in range(JG)], W=[glB[par][gi]])
                        dpar = gi % 2
                        for t in range(JG):
                            jj = g0 + t
                            ds = dpar * JG + t
                            op("act", lambda e, jj=jj: e.activation(
                                out=aa[:, par, jj:jj + 1], in_=gl[:, par, jj:jj + 1], func=AF.Copy,
                                scale=gg[par][:, jj:jj + 1]),
                               R=[glB[par][gi], ggB[par]], W=[aaB[par][gi]] if t == 0 else [aaB2[par][gi]])
                            op("act", lambda e, jj=jj, ds=ds: e.activation(
                                out=dd[:, ds, :], in_=identf[:], func=AF.Copy, scale=aa[:, par, jj:jj + 1]),
                               R=[aaB[par][gi] if t == 0 else aaB2[par][gi], identfB], W=[ddB[ds]])
                        for t in range(JG):
                            jj = g0 + t
                            ds = dpar * JG + t
                            bb = (ti * 128 + jj) % NBUF_G
                            for half in range(2):
                                op("pe", lambda e, ds=ds, bb=bb, half=half, jj=jj: e.matmul(
                                    pbank[5 + half][:, :], lhsT=dd[:, ds, :],
                                    rhs=gbuf[bb][:, D + half * 512: D + (half + 1) * 512],
                                    start=(jj == 0), stop=(jj == 127)),
                                   R=[ddB[ds], gbufB[bb]], W=[pbB[5 + half]])
                while pending:
                    pending.pop(0)[1]()
                for half in range(2):
                    op("dve", lambda e, half=half: e.tensor_tensor(
                        out=xres[:, i, half * 512:(half + 1) * 512], in0=xres[:, i, half * 512:(half + 1) * 512],
                        in1=pbank[5 + half][:, :], op=ALU.add),
                       R=[pbB[5 + half], xresB[i]], W=[xresB[i]])
                if final:
                    xr = xres[:, i, :]
                    r, rB = emit_rstd(xr, xresB[i])
                    ytv = scflat[:, 0:D]
                    op("dve", lambda e: e.scalar_tensor_tensor(out=ytv, in0=xr, scalar=r, in1=gA[:],
                                                               op0=ALU.mult, op1=ALU.mult),
                       R=[xresB[i], rB, gAB], W=scBs)
                    o_ = i - 1
                    op("sp", lambda e: e.dma_start(out=y_d[o_ * 128:(o_ + 1) * 128, :], in_=ytv),
                       R=scBs, lane="yst%d" % (o_ % 2))

            n = len(tiles)
            front(0, tiles[0])
            for ti in range(n):
                pending = []
                if ti + 1 < n:
                    P.deferred = pending
                    front(ti + 1, tiles[ti + 1])
                    P.deferred = None
                back(ti, tiles[ti], pending)

        emit_convert(0)
        if upto >= 2:
            emit_peer(0, list(range(NT1)))

        if upto >= 3:
            fence = P.fence()
            A = AP0.fork()
            watt = A.alloc("watt", [128, 8, 1792], BF16)
            wattB = Buf()
            wo = A.alloc("wo", [128, 8, D], BF16)
            woB = Buf()
            kT = A.alloc("kT", [128, 4, NT1 * 128], BF16)
            kTB = [Buf() for _ in range(NT1)]
            vv = A.alloc("vv", [128, NT1, 256], BF16)
            vvB = [Buf() for _ in range(NT1)]
            biasm = A.alloc("biasm", [128, 16, 384], F32)
            biasmB = Buf()

            sinkbc = A.alloc("sinkbc", [128, 16], F32)
            sinkB = Buf()
            penb = A.alloc("penb", [1, NT1 * 128], BF16)
            penB = Buf()
            ones1 = A.alloc("ones1", [1, 128], BF16)
            ones1B = Buf()
            qTas = [A.alloc("qTa%d" % i, [128, 8, 128], BF16) for i in range(2)]
            qTaBss = [[Buf() for _ in range(2)] for _ in range(2)]
            LL = [A.alloc("LL%d" % i, [128, 384], F32) for i in range(2)]
            LLB = [Buf() for _ in range(2)]
            EE = [A.alloc("EE%d" % i, [128, 384], BF16) for i in range(2)]
            EEB = [Buf() for _ in range(2)]
            ET = [A.alloc("ET%d" % i, [128, 3, 128], BF16) for i in range(2)]
            ETB = [Buf() for _ in range(2)]
            mrow = A.alloc("mrow", [128, 16], F32)
            mrowB = [Buf() for _ in range(16)]
            nmrow = A.alloc("nmrow", [128, 16], F32)
            nmrowB = [Buf() for _ in range(16)]
            rsum = A.alloc("rsum", [128, 16], F32)
            rsumB = [Buf() for _ in range(16)]
            esink = A.alloc("esink", [128, 16], F32)
            esinkB = [Buf() for _ in range(16)]
            den = A.alloc("den", [128, 16], F32)
            denB = Buf()
            rden = A.alloc("rden", [128, 16], F32)
            rdenB = Buf()
            ao_off = (A.p + 31) // 32 * 32
            ao = A.alloc("ao", [128, D], BF16)
            aoBs = [Buf() for _ in range(2)]
            wmask = nc.alloc_sbuf_tensor_at("wmask", [128, 384], F32, offset=ao_off)
            wmaskB = aoBs[0]
            aoT = A.alloc("aoT", [128, 8, 128], BF16)
            aoTB = Buf()

            for k in range(4):
                lo, hi = k * 448, (k + 1) * 448
                op("pool", lambda e, lo=lo, hi=hi: e.dma_start(
                    out=watt[:, :, lo:hi], in_=w_att_d[:, lo:hi].rearrange("(dc dp) n -> dp dc n", dp=128)),
                   W=[wattB], lane="w%d" % k, extra=fence)
            for k in range(2):
                op("pool", lambda e, k=k: e.dma_start(
                    out=wo[:, :, k * 512:(k + 1) * 512],
                    in_=w_o_d[:, k * 512:(k + 1) * 512].rearrange("(dc dp) n -> dp dc n", dp=128)),
                   W=[woB], lane="w%d" % (2 + k), extra=fence)
            op("pool", lambda e: e.dma_start(out=penb[:], in_=pen_d[:, :]), W=[penB], lane="w1", extra=fence)
            op("sp", lambda e: e.dma_start(out=biasm[:], in_=bias_d.rearrange("p (h k) -> p h k", h=16)),
               W=[biasmB], lane="c_bias", extra=fence)
            op("sp", lambda e: e.dma_start(out=wmask[:], in_=wmask_d[:, :]), W=[wmaskB], lane="c_wmask", extra=fence)
            op("sp", lambda e: e.dma_start(out=sinkbc[:], in_=sink_d.partition_broadcast(128)), W=[sinkB], lane="c_sink",
               extra=fence)
            op("dve", lambda e: e.tensor_tensor(out=biasm[:], in0=biasm[:],
                                                in1=wmask[:, :].unsqueeze(1).broadcast_to([128, 16, 384]), op=ALU.add),
               R=[wmaskB, biasmB], W=[biasmB])
            op("dve", lambda e: e.tensor_scalar(out=biasm[:], in0=biasm[:], scalar1=-1.0, scalar2=None, op0=ALU.mult),
               R=[biasmB], W=[biasmB])
            op("dve", lambda e: e.memset(ones1[:], 1.0), W=[ones1B], extra=fence)
            load_gain(gA, gAB, 2)

            def norm_T(i, bank=0):
                xr = xres[:, i, :]
                r, rB = emit_rstd(xr, xresB[i])
                op("dve", lambda e: e.scalar_tensor_tensor(out=xnb[:], in0=xr, scalar=r, in1=gA[:],
                                                           op0=ALU.mult, op1=ALU.mult),
                   R=[xresB[i], rB, gAB], W=[xnbB])
                emit_transpose8(xnb, xnbB, xnT[:], xnTB, bank=bank)

            for i in range(NT1):
                emit_convert(1, 1)
                norm_T(i)
                for g in range(4):
                    for dc in range(8):
                        op("pe", lambda e, g=g, dc=dc: e.matmul(
                            pbank[1][:, g * 128:(g + 1) * 128], lhsT=watt[:, dc, 1024 + g * 128: 1024 + (g + 1) * 128],
                            rhs=xnT[:, dc, :], start=(dc == 0), stop=(dc == 7)),
                           R=[wattB, xnTB], W=[pbB[1]])
                op("act", lambda e, i=i: e.copy(out=kT[:, :, i * 128:(i + 1) * 128],
                                                in_=pbank[1][:, :].rearrange("p (g t) -> p g t", g=4)),
                   R=[pbB[1]], W=[kTB[i]])
                for dc in range(8):
                    op("pe", lambda e, dc=dc: e.matmul(pbank[2][:, 0:256], lhsT=xnT[:, dc, :], rhs=watt[:, dc, 1536:1792],
                                                       start=(dc == 0), stop=(dc == 7)),
                       R=[wattB, xnTB], W=[pbB[2]])
                op("act", lambda e, i=i: e.copy(out=vv[:, i, :], in_=pbank[2][:, 0:256]), R=[pbB[2]], W=[vvB[i]])

            def pre(o_):
                i = o_ + 1
                qq = qTas[o_ % 2]
                norm_T(i, bank=1)
                for cq in range(8):
                    bk = 1 + cq // 4
                    for dc in range(8):
                        op("pe", lambda e, cq=cq, dc=dc, bk=bk: e.matmul(
                            pbank[bk][:, (cq % 4) * 128:(cq % 4 + 1) * 128], lhsT=watt[:, dc, cq * 128:(cq + 1) * 128],
                            rhs=xnT[:, dc, :], start=(dc == 0), stop=(dc == 7)),
                           R=[wattB, xnTB], W=[pbB[bk]])
                for b4 in range(2):
                    op("act", lambda e, b4=b4: e.copy(out=qq[:, b4 * 4:(b4 + 1) * 4, :],
                                                      in_=pbank[1 + b4][:, :].rearrange("p (a b) -> p a b", a=4)),
                       R=[pbB[1 + b4]], W=[qTaBss[o_ % 2][b4]])

            def head(o_, pend):
                i = o_ + 1
                qTa = qTas[o_ % 2]
                qTaBs = qTaBss[o_ % 2]
                edge = (o_ == 0) or (o_ == NO - 1)
                npull = (len(pend) + 15) // 16

                def st_S(h):
                    cq, hf, g = h // 2, h % 2, h // 4
                    q = h % 2
                    bk = 3 + q
                    ps_s = pbank[bk][:, 0:384]
                    op("pe", lambda e: e.matmul(
                        ps_s, lhsT=qTa[64 * hf:64 * hf + 64, cq, :],
                        rhs=kT[64 * hf:64 * hf + 64, g, (i - 1) * 128:(i + 2) * 128], start=True, stop=not edge),
                       R=[qTaBs[cq // 4], kTB[i - 1], kTB[i], kTB[i + 1]], W=[pbB[bk]])
                    if edge:
                        op("pe", lambda e: e.matmul(
                            ps_s, lhsT=ones1[0:1, :], rhs=penb[0:1, (i - 1) * 128:(i + 2) * 128], start=False, stop=True),
                           R=[ones1B, penB], W=[pbB[bk]])
                    op("dve", lambda e: e.scalar_tensor_tensor(
                        out=LL[q][:], in0=ps_s, scalar=-0.125, in1=biasm[:, h, :], op0=ALU.mult, op1=ALU.add),
                       R=[pbB[bk], biasmB], W=[LLB[q]])
                    op("dve", lambda e: e.tensor_reduce(out=nmrow[:, h:h + 1], in_=LL[q][:], axis=AX.X, op=ALU.min),
                       R=[LLB[q]], W=[nmrowB[h]])
                    op("act", lambda e: e.activation(out=EE[q][:], in_=LL[q][:], func=AF.Exp, scale=-1.0,
                                                     bias=nmrow[:, h:h + 1], accum_out=rsum[:, h:h + 1]),
                       R=[LLB[q], nmrowB[h]], W=[EEB[q], rsumB[h]])
                    op("act", lambda e: e.activation(out=esink[:, h:h + 1], in_=nmrow[:, h:h + 1], func=AF.Exp,
                                                     bias=sinkbc[:, h:h + 1], scale=1.0),
                       R=[nmrowB[h], sinkB], W=[esinkB[h]])

                def st_T(h):
                    q = h % 2
                    ptbank = 5 if q == 0 else 0
                    pt = pbank_bf[ptbank][:, 0:384].rearrange("p (a b) -> p a b", a=3)
                    ptB = pbB[ptbank]
                    for kb in range(3):
                        op("pe", lambda e, kb=kb: e.transpose(out=pt[:, kb, :], in_=EE[q][:, kb * 128:(kb + 1) * 128],
                                                              identity=ident[:]),
                           R=[EEB[q], identB], W=[ptB])
                    op("act", lambda e: e.copy(out=ET[q][:], in_=pt), R=[ptB], W=[ETB[q]])

                def st_V(h):
                    g = h // 4
                    q = h % 2
                    bko = 6 + h // 8
                    for kb in range(3):
                        op("pe", lambda e, kb=kb: e.matmul(
                            pbank[bko][:, (h % 8) * 64:(h % 8 + 1) * 64], lhsT=ET[q][:, kb, :],
                            rhs=vv[:, i - 1 + kb, g * 64:(g + 1) * 64], start=(kb == 0), stop=(kb == 2)),
                           R=[ETB[q], vvB[i - 1 + kb]], W=[pbB[bko]])

                for k_ in range(16 + 2):
                    if k_ < 16:
                        st_S(k_)
                    if 0 <= k_ - 1 < 16:
                        st_T(k_ - 1)
                    if 0 <= k_ - 2 < 16:
                        st_V(k_ - 2)
                    for _ in range(npull):
                        if pend:
                            pend.pop(0)[1]()
                while pend:
                    pend.pop(0)[1]()

            def tail_now(o_):
                op("dve", lambda e: e.tensor_tensor(out=den[:], in0=rsum[:], in1=esink[:], op=ALU.add),
                   R=rsumB + esinkB, W=[denB])
                op("dve", lambda e: e.reciprocal(out=rden[:], in_=den[:]), R=[denB], W=[rdenB])
                for hb in range(2):
                    op("dve", lambda e, hb=hb: e.tensor_tensor(
                        out=ao[:, hb * 512:(hb + 1) * 512].rearrange("p (h d) -> p h d", h=8),
                        in0=pbank[6 + hb][:, :].rearrange("p (h d) -> p h d", h=8),
                        in1=rden[:, hb * 8:(hb + 1) * 8].unsqueeze(2).broadcast_to([128, 8, 64]), op=ALU.mult),
                       R=[pbB[6 + hb], rdenB], W=[aoBs[hb]])

            def tail_def(o_):
                i = o_ + 1
                emit_transpose8(ao, aoBs, aoT[:], aoTB, bank=2)
                for half in range(2):
                    bk = 1 + half
                    for cc in range(8):
                        op("pe", lambda e, half=half, cc=cc, bk=bk: e.matmul(
                            pbank[bk][:, :], lhsT=aoT[:, cc, :], rhs=wo[:, cc, half * 512:(half + 1) * 512],
                            start=(cc == 0), stop=(cc == 7)),
                           R=[aoTB, woB], W=[pbB[bk]])
                    op("dve", lambda e, half=half, bk=bk: e.tensor_tensor(
                        out=xres[:, i, half * 512:(half + 1) * 512], in0=xres[:, i, half * 512:(half + 1) * 512],
                        in1=pbank[bk][:, :], op=ALU.add),
                       R=[pbB[bk], xresB[i]], W=[xresB[i]])

            pre(0)
            pend_tail = []
            for o_ in range(NO):
                pend = pend_tail
                if o_ + 1 < NO:
                    P.deferred = []
                    pre(o_ + 1)
                    pend = pend + P.deferred
                    P.deferred = None
                head(o_, pend)
                tail_now(o_)
                P.deferred = pend_tail = []
                tail_def(o_)
                P.deferred = None
            for _, th in pend_tail:
                th()

        emit_convert(1)
        if upto >= 4:
            emit_peer(1, list(range(1, NO + 1)), final=True)

        if dbg:
            for i in range(NT1):
                op("sp", lambda e, i=i: e.dma_start(out=dbg_d[i * 128:(i + 1) * 128, :], in_=xres[:, i, :]),
                   R=[xresB[i]], lane="dbg")
        P.raw("sp", lambda e: e.nop(), deps=P.fence())
        P.build()
    return nc


def _t5_bucket_np(rel):
    half = 16
    max_exact = 8
    ret = np.where(rel > 0, half, 0)
    n = np.abs(rel)
    nf = np.maximum(n, 1).astype(np.float32)
    large = max_exact + (np.log(nf / max_exact) / np.float32(np.log(128 / max_exact)) * (half - max_exact)).astype(np.int32)
    large = np.minimum(large, half - 1)
    return ret + np.where(n < max_exact, n, large)


def prep_shared(inp):
    f = lambda a: np.ascontiguousarray(np.asarray(a, dtype=np.float32))
    sh = {}
    sh["w_in"] = f(inp["conv_w_in"][0])
    cwv = np.asarray(inp["conv_w"][0], np.float32)
    sh["cw"] = f(cwv.T.reshape(8, 128, 3).transpose(1, 0, 2).reshape(128, 24))
    sh["w_out"] = f(inp["conv_w_out"][0])
    wqkv = np.asarray(inp["attn_w_qkv"][0], np.float32)
    wq_, wk_, wv_ = wqkv[:, :1024], wqkv[:, 1024:1280], wqkv[:, 1280:1536]
    kd = []
    for g in range(4):
        kd += [wk_[:, g * 64:(g + 1) * 64], wk_[:, g * 64:(g + 1) * 64]]
    sh["w_att"] = f(np.concatenate([wq_] + kd + [wv_], axis=1))
    sh["w_o"] = f(inp["attn_w_o"][0])
    sh["sink"] = f(np.asarray(inp["attn_sink"][0]).reshape(1, 16))
    qi = np.arange(128)[:, None]
    kj = np.arange(384)[None, :]
    rel = kj - 128 - qi
    bk = _t5_bucket_np(rel)
    rb = np.asarray(inp["rel_bias"], np.float32)
    sh["bias_tab"] = f(rb[bk].transpose(0, 2, 1).reshape(128, 16 * 384))
    sh["wmask"] = f(np.where(np.abs(rel) <= 128, 0.0, -30000.0))
    sh["gains"] = f(np.stack([inp["conv_norm_g"][0], inp["ffn_norm_g"][0], inp["attn_norm_g"][0],
                              inp["ffn_norm_g"][1], inp["final_norm_g"]], axis=0))
    sh["w_pq"] = f(inp["peer_w_q"])
    sk = np.asarray(inp["peer_subkeys"], np.float32)
    sh["skT"] = f(sk.transpose(0, 4, 1, 2, 3).reshape(2, 128, 2048))
    for l in range(2):
        sh["uv%d" % l] = f(np.concatenate([np.asarray(inp["peer_u"][l], np.float32),
                                            np.asarray(inp["peer_v"][l], np.float32)], axis=1))
    sh["iota"] = f(np.broadcast_to(np.arange(16, dtype=np.float32), (128, 16)))
    return sh


def prep_core(x, b, k, NO, S):
    NT1, NX = NO + 2, NO + 4
    own0 = k * NO * 128
    lo = own0 - 256
    xe = np.zeros((NX * 128, D), np.float32)
    a, e = max(lo, 0), min(lo + NX * 128, S)
    xe[a - lo:e - lo] = x[b, a:e]
    pen = np.zeros((1, NT1 * 128), np.float32)
    t = own0 - 128 + np.arange(NT1 * 128)
    pen[0, (t < 0) | (t >= S)] = -240000.0
    return {"x_ext": xe, "pen": pen}


_NC_CACHE = {}


def kernel(**inputs):
    x = np.asarray(inputs["x"], np.float32)
    B, S, _ = x.shape
    NO = 16
    ncores = 8
    per_b = ncores // B
    sh = prep_shared(inputs)
    in_maps = []
    for c in range(ncores):
        b, k = c // per_b, c % per_b
        m = dict(sh)
        m.update(prep_core(x, b, k, NO, S))
        in_maps.append(m)
    nc = build(NO=NO)
    res = run_bass_kernel_spmd(nc, in_maps, core_ids=list(range(ncores)))
    out = np.zeros((B, S, D), np.float32)
    for c in range(ncores):
        b, k = c // per_b, c % per_b
        out[b, k * NO * 128:(k + 1) * NO * 128] = res.results[c]["y"]
    return out
```

```python
import contextlib
import os
import numpy as np
import concourse.bass as bass
import concourse.mybir as mybir
from concourse.bass_utils import run_bass_kernel_spmd

F32 = mybir.dt.float32
BF16 = mybir.dt.bfloat16
U32 = mybir.dt.uint32
ALU = mybir.AluOpType
AF = mybir.ActivationFunctionType
AX = mybir.AxisListType

D = 1024
EPS = 1e-6
SAME_ENGINE_SYNC = os.environ.get("K_SES", "1") == "1"
NBUF_G = int(os.environ.get("K_NB", "11"))
PE_DOT = int(os.environ.get("K_PEDOT", "0"))
JG = int(os.environ.get("K_JG", "1"))
RG_ENG = os.environ.get("K_RG", "dve")
FRONT_PULL = 3


class Op:
    __slots__ = ("eng", "fn", "deps", "lane", "count", "signaled")

    def __init__(self, eng, fn, deps, lane):
        self.eng = eng
        self.fn = fn
        self.deps = deps
        self.lane = lane
        self.count = None
        self.signaled = False


class Buf:
    __slots__ = ("w", "r")

    def __init__(self):
        self.w = {}
        self.r = {}


class Prog:
    ENGS = ("pe", "act", "dve", "pool", "sp")

    def __init__(self, nc):
        self.nc = nc
        self.ops = []
        self.lane_last = {}
        self.deferred = None

    def call(self, fn):
        if self.deferred is not None:
            self.deferred.append(("none", fn))
        else:
            fn()

    def raw(self, eng, fn, deps=(), lane=None):
        deps = [d for d in deps if d is not None]
        if lane is not None:
            prev = self.lane_last.get(lane)
            if prev is not None:
                deps.append(prev)
        o = Op(eng, fn, deps, lane)
        if lane is not None:
            self.lane_last[lane] = o
        self.ops.append(o)
        return o

    def op(self, eng, fn, R=(), W=(), lane=None, extra=()):
        if self.deferred is not None:
            self.deferred.append((eng, lambda: self._op(eng, fn, R, W, lane, extra)))
            return None
        return self._op(eng, fn, R, W, lane, extra)

    def _op(self, eng, fn, R=(), W=(), lane=None, extra=()):
        deps = list(extra)
        for b in R:
            deps.extend(b.w.values())
        for b in W:
            deps.extend(b.w.values())
            deps.extend(b.r.values())
        o = self.raw(eng, fn, deps, lane)
        key = ("l", lane) if lane is not None else ("e", eng)
        for b in R:
            b.r[key] = o
        for b in W:
            b.w = {key: o}
            b.r = {}
        return o

    def fence(self):
        last = {}
        for o in self.ops:
            key = ("l", o.lane) if o.lane is not None else ("e", o.eng)
            last[key] = o
        return list(last.values())

    def build(self):
        nc = self.nc
        for o in self.ops:
            nd = []
            seen = set()
            for d in o.deps:
                if id(d) in seen:
                    continue
                seen.add(id(d))
                if d.lane is None and d.eng == o.eng and o.lane is None:
                    if d.eng == "pe" or not SAME_ENGINE_SYNC:
                        continue
                nd.append(d)
            o.deps = nd
            for d in nd:
                d.signaled = True
        lanes = sorted({o.lane for o in self.ops if o.lane is not None})
        with contextlib.ExitStack() as es:
            esem = {e: es.enter_context(nc.semaphore("s_" + e)) for e in self.ENGS}
            lsem = {l: es.enter_context(nc.semaphore("l_" + str(l))) for l in lanes}
            ecount = {e: 0 for e in self.ENGS}
            lcount = {l: 0 for l in lanes}
            for o in self.ops:
                if o.lane is not None:
                    lcount[o.lane] += 16
                    o.count = lcount[o.lane]
                elif o.signaled:
                    ecount[o.eng] += 1
                    o.count = ecount[o.eng]
            self.final_counts = (dict(ecount), dict(lcount))
            block = es.enter_context(nc.Block())
            ops = self.ops

            def emit_for(engname):
                def body(eng):
                    waited = {}
                    for o in ops:
                        if o.eng != engname:
                            continue
                        need = {}
                        for d in o.deps:
                            key = ("l", d.lane) if d.lane is not None else ("e", d.eng)
                            if d.count > need.get(key, 0):
                                need[key] = d.count
                        for key, cnt in need.items():
                            if waited.get(key, 0) >= cnt:
                                continue
                            sem = lsem[key[1]] if key[0] == "l" else esem[key[1]]
                            eng.wait_ge(sem, cnt)
                            waited[key] = cnt
                        ins = o.fn(eng)
                        if o.lane is not None:
                            ins.then_inc(lsem[o.lane], 16)
                        elif o.signaled:
                            ins.then_inc(esem[o.eng], 1)
                return body

            block.tensor(emit_for("pe"))
            block.scalar(emit_for("act"))
            block.vector(emit_for("dve"))
            block.gpsimd(emit_for("pool"))
            block.sync(emit_for("sp"))


def _dsize(dt):
    return {F32: 4, BF16: 2, U32: 4}[dt]


class Arena:
    def __init__(self, nc, start, top):
        self.nc = nc
        self.p = start
        self.top = top
        self.n = 0

    def alloc(self, name, shape, dt):
        nbytes = int(np.prod(shape[1:])) * _dsize(dt)
        off = (self.p + 31) // 32 * 32
        self.p = off + nbytes
        assert self.p <= self.top, (name, self.p, self.top)
        self.n += 1
        return self.nc.alloc_sbuf_tensor_at(name, list(shape), dt, offset=off)

    def fork(self):
        return Arena(self.nc, self.p, self.top)


def build(NO=16, upto=99, dbg=False):
    NT1 = NO + 2
    NX = NO + 4
    nc = bass.Bass("TRN2", target_bir_lowering=False)

    def dr(name, shape, dt=F32, kind="ExternalInput"):
        return nc.dram_tensor(name, list(shape), dt, kind=kind).ap()

    x_ext = dr("x_ext", [NX * 128, D])
    pen_d = dr("pen", [1, NT1 * 128])
    w_in_d = dr("w_in", [D, 3 * D])
    cw_d = dr("cw", [128, 24])
    w_out_d = dr("w_out", [D, D])
    w_att_d = dr("w_att", [D, 1792])
    w_o_d = dr("w_o", [D, D])
    sink_d = dr("sink", [1, 16])
    bias_d = dr("bias_tab", [128, 16 * 384])
    wmask_d = dr("wmask", [128, 384])
    gains_d = dr("gains", [5, D])
    w_pq_d = dr("w_pq", [2, D, 2048])
    skT_d = dr("skT", [2, 128, 2048])
    uv_d = [dr("uv0", [16384, 2048]), dr("uv1", [16384, 2048])]
    iota_d = dr("iota", [128, 16])
    uvb_d = [nc.dram_tensor("uvb%d" % l, [16384, 2048], BF16, kind="Internal").ap() for l in range(2)]
    y_d = dr("y", [NO * 128, D], kind="ExternalOutput")
    if dbg:
        dbg_d = dr("dbg", [NT1 * 128, D], kind="ExternalOutput")

    P = Prog(nc)
    op = P.op

    with contextlib.ExitStack() as es:
        pbank = [es.enter_context(nc.psum_tensor("pb%d" % i, [128, 512], F32)) for i in range(8)]
        pbB = [Buf() for _ in range(8)]
        pbB5b = Buf()
        pbank_bf = [p.bitcast(BF16) for p in pbank]

        A0 = Arena(nc, (nc.sbuf_base + 63) // 64 * 64, nc.sbuf_top)
        xres = A0.alloc("xres", [128, NT1, D], F32)
        xresB = [Buf() for _ in range(NT1)]
        ident = A0.alloc("ident", [128, 128], BF16)
        identB = Buf()
        identf = A0.alloc("identf", [128, 128], F32)
        identfB = Buf()
        iota16 = A0.alloc("iota16", [128, 16], F32)
        iotaB = Buf()
        gA = A0.alloc("gA", [128, D], F32)
        gAB = Buf()
        gB = A0.alloc("gB", [128, D], F32)
        gBB = Buf()
        stat = A0.alloc("stat", [128, 96], F32)
        statB = [Buf() for _ in range(96)]
        junk = A0.alloc("junk", [128, D], BF16)
        junkB = Buf()
        junk2_off = (A0.p + 31) // 32 * 32
        junk2 = A0.alloc("junk2", [128, D], BF16)
        junk3 = A0.alloc("junk3", [128, D], BF16)
        junk2s = [junk2, junk3]
        junk2B = [Buf(), Buf()]
        xnb = A0.alloc("xnb", [128, D], BF16)
        xnbB = Buf()
        xnT = A0.alloc("xnT", [128, 8, 128], BF16)
        xnTB = Buf()

        stat_ctr = [0]

        def new_stat():
            k = stat_ctr[0] % 96
            stat_ctr[0] += 1
            return stat[:, k:k + 1], statB[k]

        op("pool", lambda e: e.memset(ident[:], 1.0), W=[identB])
        op("pool", lambda e: e.affine_select(out=ident[:], in_=ident[:], pattern=[[-1, 128]],
                                             compare_op=ALU.is_equal, fill=0.0, base=0,
                                             channel_multiplier=1), R=[identB], W=[identB])
        op("pool", lambda e: e.tensor_copy(out=identf[:], in_=ident[:]), R=[identB], W=[identfB])
        op("sp", lambda e: e.dma_start(out=iota16[:], in_=iota_d[:, :]), W=[iotaB], lane="c_iota")

        def load_gain(dst, dstB, row):
            return op("sp", lambda e: e.dma_start(out=dst[:], in_=gains_d[row:row + 1, :].partition_broadcast(128)),
                      W=[dstB], lane="gain")

        def emit_rstd(src_ap, srcB):
            ss, ssB = new_stat()
            op("act", lambda e: e.activation(out=junk[:], in_=src_ap, func=AF.Square, accum_out=ss),
               R=[srcB], W=[ssB, junkB])
            sd, sdB = new_stat()
            op("act", lambda e: e.activation(out=sd, in_=ss, func=AF.Sqrt, bias=EPS_AP[0], scale=1.0 / D),
               R=[ssB, epsB], W=[sdB])
            r, rB = new_stat()
            op("dve", lambda e: e.reciprocal(out=r, in_=sd), R=[sdB], W=[rB])
            return r, rB

        def emit_transpose8(src_bf, srcB_, dst_ap, dstB_, bank=0):
            pv = pbank_bf[bank][:, 0:1024].rearrange("p (a b) -> p a b", a=8)
            for dc in range(8):
                op("pe", lambda e, dc=dc: e.transpose(out=pv[:, dc, :], in_=src_bf[:, dc * 128:(dc + 1) * 128],
                                                      identity=ident[:]),
                   R=(list(srcB_) if isinstance(srcB_, (list, tuple)) else [srcB_]) + [identB], W=[pbB[bank]])
            op("act", lambda e: e.copy(out=dst_ap, in_=pv), R=[pbB[bank]], W=[dstB_])

        epsT = A0.alloc("epsT", [128, 1], F32)
        epsB = Buf()
        EPS_AP = [epsT[:, 0:1]]
        op("pool", lambda e: e.memset(epsT[:], EPS), W=[epsB])

        AP0 = A0

        NCV = 16
        RCV = 16384 // NCV
        uvbB = [Buf(), Buf()]

        CV_ROWS = {0: RCV, 1: RCV}
        CV_LANES = {0: NCV, 1: NCV}
        conv_left = {0: list(range(16384 // CV_ROWS[0])), 1: list(range(16384 // CV_ROWS[1]))}

        def emit_convert(layer, nchunks=10 ** 9):
            rows = CV_ROWS[layer]
            for _ in range(nchunks):
                if not conv_left[layer]:
                    break
                k = conv_left[layer].pop(0)
                op("pool", lambda e, k=k, rows=rows: e.dma_start(out=uvb_d[layer][k * rows:(k + 1) * rows, :],
                                                                 in_=uv_d[layer][k * rows:(k + 1) * rows, :]),
                   W=[], lane="cv%d_%d" % (layer, k % CV_LANES[layer]))
            if not conv_left[layer]:
                uvbB[layer].w = {("l", "cv%d_%d" % (layer, q)): P.lane_last["cv%d_%d" % (layer, q)]
                                 for q in range(CV_LANES[layer]) if ("cv%d_%d" % (layer, q)) in P.lane_last}

        if upto >= 1:
            A = AP0.fork()
            xtmp = nc.alloc_sbuf_tensor_at("xtmp", [128, D], F32, offset=junk2_off)
            xtmpB = Buf()
            xnb2 = A.alloc("xnb2", [128, D], BF16)
            xnb2B = Buf()
            win = A.alloc("win", [128, 8, 3 * D], BF16)
            winB = Buf()
            wout = A.alloc("wout", [128, 8, D], BF16)
            woutB = Buf()
            cw = A.alloc("cw", [128, 8, 3], F32)
            cwB = Buf()
            xnT_all = A.alloc("xnT_all", [128, 8, NX * 128], BF16)
            xnT_allB = [Buf() for _ in range(NX)]
            hsb = [A.alloc("hsb%d" % i, [128, 130], F32) for i in range(2)]
            hsbB = [Buf() for _ in range(2)]
            zsb = [A.alloc("zsb%d" % i, [128, 130], F32) for i in range(2)]
            zsbB = [Buf() for _ in range(2)]
            ysb = [A.alloc("ysb%d" % i, [128, 128], F32) for i in range(2)]
            ysbB = [Buf() for _ in range(2)]
            gT = [A.alloc("gT%d" % i, [128, 8, 128], BF16) for i in range(2)]
            gTB = [[Buf() for _ in range(8)] for _ in range(2)]

            for k in range(6):
                op("pool", lambda e, k=k: e.dma_start(
                    out=win[:, :, k * 512:(k + 1) * 512],
                    in_=w_in_d[:, k * 512:(k + 1) * 512].rearrange("(dc dp) n -> dp dc n", dp=128)),
                   W=[winB], lane="w%d" % (k % 4))
            op("pool", lambda e: e.dma_start(out=wout[:], in_=w_out_d.rearrange("(dc dp) n -> dp dc n", dp=128)),
               W=[woutB], lane="w1")
            op("sp", lambda e: e.dma_start(out=cw[:], in_=cw_d.rearrange("p (c k) -> p c k", k=3)), W=[cwB], lane="c_cw")
            load_gain(gA, gAB, 0)

            for e_ in range(NX):
                if e_ >= 4 and e_ % 2 == 0:
                    emit_convert(0, 1)
                if 1 <= e_ <= NT1:
                    dst, dB = xres[:, e_ - 1, :], xresB[e_ - 1]
                else:
                    dst, dB = xtmp[:], xtmpB
                op("sp", lambda e, e_=e_, dst=dst: e.dma_start(out=dst, in_=x_ext[e_ * 128:(e_ + 1) * 128, :]),
                   W=[dB], lane="xl%d" % (e_ % 4))
                r, rB = emit_rstd(dst, dB)
                xb_, xbB_ = (xnb, xnbB) if e_ % 2 == 0 else (xnb2, xnb2B)
                op("dve", lambda e, dst=dst, r=r, xb_=xb_: e.scalar_tensor_tensor(out=xb_[:], in0=dst, scalar=r, in1=gA[:],
                                                                                 op0=ALU.mult, op1=ALU.mult),
                   R=[dB, rB, gAB], W=[xbB_])
                emit_transpose8(xb_, xbB_, xnT_all[:, :, e_ * 128:(e_ + 1) * 128], xnT_allB[e_], bank=(0 if e_ % 2 == 0 else 7))

            for i in range(NT1):
                emit_convert(0, 1)
                e_ = i + 1
                c0 = 128 * e_ - 1
                par = i % 2
                for cc in range(8):
                    q = cc % 2
                    bk = 1 + q
                    psB = pbank[bk][:, 0:130]
                    psC = pbank[bk][:, 130:260]
                    psH = pbank[bk][:, 260:390]
                    for wi, pso in enumerate((psB, psC, psH)):
                        for dc in range(8):
                            op("pe", lambda e, wi=wi, pso=pso, dc=dc, cc=cc, c0=c0: e.matmul(
                                pso, lhsT=win[:, dc, wi * D + cc * 128: wi * D + (cc + 1) * 128],
                                rhs=xnT_all[:, dc, c0:c0 + 130], start=(dc == 0), stop=(dc == 7)),
                               R=[winB, xnT_allB[e_ - 1], xnT_allB[e_], xnT_allB[e_ + 1]], W=[pbB[bk]])
                    op("act", lambda e, q=q, psH=psH: e.copy(out=hsb[q][:], in_=psH), R=[pbB[bk]], W=[hsbB[q]])
                    op("dve", lambda e, q=q, psC=psC: e.tensor_tensor(out=zsb[q][:], in0=psC, in1=hsb[q][:], op=ALU.mult),
                       R=[pbB[bk], hsbB[q]], W=[zsbB[q]])
                    op("dve", lambda e, q=q, cc=cc: e.tensor_scalar(out=ysb[q][:], in0=zsb[q][:, 0:128],
                                                                   scalar1=cw[:, cc, 0:1], scalar2=None, op0=ALU.mult),
                       R=[zsbB[q], cwB], W=[ysbB[q]])
                    for kk in (1, 2):
                        op("dve", lambda e, q=q, cc=cc, kk=kk: e.scalar_tensor_tensor(
                            out=ysb[q][:], in0=zsb[q][:, kk:kk + 128], scalar=cw[:, cc, kk:kk + 1], in1=ysb[q][:],
                            op0=ALU.mult, op1=ALU.add),
                           R=[zsbB[q], cwB, ysbB[q]], W=[ysbB[q]])
                    op("dve", lambda e, q=q, cc=cc, par=par, psB=psB: e.tensor_tensor(
                        out=gT[par][:, cc, :], in0=psB[:, 1:129], in1=ysb[q][:], op=ALU.mult),
                       R=[pbB[bk], ysbB[q]], W=[gTB[par][cc]])
                for half in range(2):
                    bk = 3 + half
                    for cc in range(8):
                        op("pe", lambda e, half=half, cc=cc, par=par, bk=bk: e.matmul(
                            pbank[bk][:, :], lhsT=gT[par][:, cc, :], rhs=wout[:, cc, half * 512:(half + 1) * 512],
                            start=(cc == 0), stop=(cc == 7)),
                           R=[gTB[par][cc], woutB], W=[pbB[bk]])
                    op("dve", lambda e, half=half, i=i, bk=bk: e.tensor_tensor(
                        out=xres[:, i, half * 512:(half + 1) * 512], in0=xres[:, i, half * 512:(half + 1) * 512],
                        in1=pbank[bk][:, :], op=ALU.add),
                       R=[pbB[bk], xresB[i]], W=[xresB[i]])

        def emit_peer(layer, tiles, final=False):
            fence = P.fence()
            A = AP0.fork()
            wq = A.alloc("wq%d" % layer, [128, 8, 2048], BF16)
            wqB = Buf()
            sk = A.alloc("sk%d" % layer, [128, 16, 128], BF16)
            skB = Buf()
            xn = [A.alloc("xn%d_%d" % (layer, i), [128, D], F32) for i in range(2)]
            xnB = [Buf() for _ in range(2)]
            qT = A.alloc("qT%d" % layer, [128, 16, 128], BF16)
            qTBs = [Buf() for _ in range(4)]
            sc = A.alloc("sc%d" % layer, [128, 16, 128], F32)
            scBs = [Buf() for _ in range(4)]
            wk = A.alloc("wk%d" % layer, [128, 128], F32)
            wkB = Buf()
            mx = A.alloc("mx%d" % layer, [128, 16, 16], F32)
            mxB = Buf()
            ixu = A.alloc("ixu%d" % layer, [128, 16, 16], U32)
            ixuB = Buf()
            ixf = A.alloc("ixf%d" % layer, [128, 16, 16], F32)
            ixfB = Buf()
            cwk = A.alloc("cwk%d" % layer, [128, 256], F32)
            cwkB = Buf()
            tv = A.alloc("tv%d" % layer, [128, 8, 16], F32)
            tvB = Buf()
            pos = A.alloc("pos%d" % layer, [128, 8, 16], U32)
            posB = Buf()
            pa = A.alloc("pa%d" % layer, [128, 8, 16], U32)
            paB = Buf()
            pb_ = A.alloc("pb%d_" % layer, [128, 8, 16], U32)
            pbB_ = Buf()
            paf = A.alloc("paf%d" % layer, [128, 8, 16], F32)
            pafB = Buf()
            pbf = A.alloc("pbf%d" % layer, [128, 8, 16], F32)
            pbfB = Buf()
            s0 = A.alloc("s0%d" % layer, [128, 8, 16], F32)
            s0B = Buf()
            s1 = A.alloc("s1%d" % layer, [128, 8, 16], F32)
            s1B = Buf()
            idxf = A.alloc("idxf%d" % layer, [128, 128], F32)
            idxfB = Buf()
            idxu = [A.alloc("idxu%d_%d" % (layer, i), [128, 128], U32) for i in range(2)]
            idxuB = [Buf() for _ in range(2)]
            ntv = A.alloc("ntv%d" % layer, [128, 8], F32)
            ntvB = Buf()
            ee = A.alloc("ee%d" % layer, [128, 8, 16], F32)
            eeB = Buf()
            zz = A.alloc("zz%d" % layer, [128, 8], F32)
            zzB = [Buf() for _ in range(8)]
            rz = A.alloc("rz%d" % layer, [128, 8], F32)
            rzB = Buf()
            gg = [A.alloc("gg%d_%d" % (layer, i), [128, 128], F32) for i in range(2)]
            ggB = [Buf() for _ in range(2)]
            hh = A.alloc("hh%d" % layer, [128, 2, 128], F32)
            hhB = [[Buf() for _ in range(128)] for _ in range(2)]
            gl = A.alloc("gl%d" % layer, [128, 2, 128], F32)
            glB = [[Buf() for _ in range(128 // JG)] for _ in range(2)]
            aa = A.alloc("aa%d" % layer, [128, 2, 128], F32)
            aaB = [[Buf() for _ in range(128 // JG)] for _ in range(2)]
            aaB2 = [[Buf() for _ in range(128 // JG)] for _ in range(2)]
            dd = A.alloc("dd%d" % layer, [128, 2 * JG, 128], BF16)
            ddB = [Buf() for _ in range(2 * JG)]
            gbuf = [A.alloc("gb%d_%d" % (layer, i), [128, 2048], BF16) for i in range(NBUF_G)]
            gbufB = [Buf() for _ in range(NBUF_G)]
            yt = None
            if PE_DOT > 0:
                xnT2 = A.alloc("xnT2_%d" % layer, [128, 8, 128], BF16)
                xnTp = [xnT, xnT2]
                xnTpB = [xnTB, Buf()]
            else:
                xnTp = [xnT, xnT]
                xnTpB = [xnTB, xnTB]
            ugT = [A.alloc("ugT%d_%d" % (layer, i), [128, 8, 128], BF16) for i in range(2)] if PE_DOT > 0 else None
            ugTB = [Buf() for _ in range(2)]
            pv7 = pbank_bf[7][:, 0:1024].rearrange("p (a b) -> p a b", a=8)
            tb = sc

            for k in range(4):
                op("pool", lambda e, k=k: e.dma_start(
                    out=wq[:, :, k * 512:(k + 1) * 512],
                    in_=w_pq_d[layer, :, k * 512:(k + 1) * 512].rearrange("(dc dp) n -> dp dc n", dp=128)),
                   W=[wqB], lane="w%d" % k, extra=fence)
            op("pool", lambda e: e.dma_start(out=sk[:], in_=skT_d[layer].rearrange("p (c n) -> p c n", n=128)),
               W=[skB], lane="w1", extra=fence)
            load_gain(gB, gBB, 1 if layer == 0 else 3)
            if final:
                load_gain(gA, gAB, 4)

            scflat = sc[:, :, :].rearrange("p a b -> p (a b)")
            T4 = scflat.rearrange("p (h k a) -> p h k a", h=8, k=16)

            def front(ti, i):
                par = ti % 2
                xr = xres[:, i, :]
                r, rB = emit_rstd(xr, xresB[i])
                op("dve", lambda e: e.scalar_tensor_tensor(out=xn[par][:], in0=xr, scalar=r, in1=gB[:],
                                                           op0=ALU.mult, op1=ALU.mult),
                   R=[xresB[i], rB, gBB], W=[xnB[par]], extra=(fence if ti < 2 else ()))
                op("act", lambda e: e.copy(out=xnb[:], in_=xn[par][:]), R=[xnB[par]], W=[xnbB])
                emit_transpose8(xnb, xnbB, xnTp[par][:], xnTpB[par])
                for c in range(16):
                    bk = 1 + c // 4
                    for dc in range(8):
                        op("pe", lambda e, c=c, dc=dc, bk=bk: e.matmul(
                            pbank[bk][:, (c % 4) * 128:(c % 4 + 1) * 128], lhsT=wq[:, dc, c * 128:(c + 1) * 128],
                            rhs=xnTp[par][:, dc, :], start=(dc == 0), stop=(dc == 7)),
                           R=[wqB, xnTpB[par]], W=[pbB[bk]])
                for b4 in range(4):
                    op("act", lambda e, b4=b4: e.copy(out=qT[:, b4 * 4:(b4 + 1) * 4, :],
                                                      in_=pbank[1 + b4][:, :].rearrange("p (a b) -> p a b", a=4)),
                       R=[pbB[1 + b4]], W=[qTBs[b4]])
                for c in range(16):
                    bk = 1 + c // 4
                    op("pe", lambda e, c=c, bk=bk: e.matmul(
                        pbank[bk][:, (c % 4) * 128:(c % 4 + 1) * 128], lhsT=qT[:, c, :], rhs=sk[:, c, :],
                        start=True, stop=True),
                       R=[qTBs[c // 4], skB], W=[pbB[bk]])
                for b4 in range(4):
                    op("act", lambda e, b4=b4: e.copy(out=sc[:, b4 * 4:(b4 + 1) * 4, :],
                                                      in_=pbank[1 + b4][:, :].rearrange("p (a b) -> p a b", a=4)),
                       R=[pbB[1 + b4]], W=[scBs[b4]])
                for c in range(16):
                    op("dve", lambda e, c=c: e.max(out=mx[:, c, 0:8], in_=sc[:, c, :]), R=[scBs[c // 4]], W=[mxB])
                    op("dve", lambda e, c=c: e.max_index(out=ixu[:, c, 0:8], in_max=mx[:, c, 0:8], in_values=sc[:, c, :]),
                       R=[scBs[c // 4], mxB], W=[ixuB])
                    op("dve", lambda e, c=c: e.match_replace(out=wk[:], in_to_replace=mx[:, c, 0:8], in_values=sc[:, c, :],
                                                             imm_value=-1e30), R=[scBs[c // 4], mxB], W=[wkB])
                    op("dve", lambda e, c=c: e.max(out=mx[:, c, 8:16], in_=wk[:]), R=[wkB], W=[mxB])
                    op("dve", lambda e, c=c: e.max_index(out=ixu[:, c, 8:16], in_max=mx[:, c, 8:16], in_values=wk[:]),
                       R=[wkB, mxB], W=[ixuB])
                op("dve", lambda e: e.tensor_copy(out=ixf[:], in_=ixu[:]), R=[ixuB], W=[ixfB])
                mx4 = mx[:, :, :].rearrange("p (h t) k -> p h t k", t=2)
                ixf4 = ixf[:, :, :].rearrange("p (h t) k -> p h t k", t=2)
                op("dve", lambda e: e.tensor_tensor(
                    out=T4, in0=mx4[:, :, 0, :].unsqueeze(3).broadcast_to([128, 8, 16, 16]),
                    in1=mx4[:, :, 1, :].unsqueeze(2).broadcast_to([128, 8, 16, 16]), op=ALU.add),
                   R=[mxB], W=scBs)
                for h in range(8):
                    cand_h = scflat[:, h * 256:(h + 1) * 256]
                    op("dve", lambda e, h=h, cand_h=cand_h: e.max(out=tv[:, h, 0:8], in_=cand_h), R=scBs, W=[tvB])
                    op("dve", lambda e, h=h, cand_h=cand_h: e.max_index(out=pos[:, h, 0:8], in_max=tv[:, h, 0:8],
                                                                        in_values=cand_h),
                       R=scBs + [tvB], W=[posB])
                    op("dve", lambda e, h=h, cand_h=cand_h: e.match_replace(out=cwk[:], in_to_replace=tv[:, h, 0:8],
                                                                            in_values=cand_h, imm_value=-1e30),
                       R=scBs + [tvB], W=[cwkB])
                    op("dve", lambda e, h=h: e.max(out=tv[:, h, 8:16], in_=cwk[:]), R=[cwkB], W=[tvB])
                    op("dve", lambda e, h=h: e.max_index(out=pos[:, h, 8:16], in_max=tv[:, h, 8:16], in_values=cwk[:]),
                       R=[cwkB, tvB], W=[posB])
                op("dve", lambda e: e.tensor_tensor(out=ee[:], in0=tv[:],
                                                    in1=tv[:, :, 0].unsqueeze(2).broadcast_to([128, 8, 16]),
                                                    op=ALU.subtract), R=[tvB], W=[eeB])
                op("act", lambda e: e.activation(out=ee[:], in_=ee[:], func=AF.Exp), R=[eeB], W=[eeB])
                op("dve", lambda e: e.tensor_single_scalar(out=pa[:], in_=pos[:], scalar=4, op=ALU.logical_shift_right),
                   R=[posB], W=[paB])
                op("dve", lambda e: e.tensor_single_scalar(out=pb_[:], in_=pos[:], scalar=15, op=ALU.bitwise_and),
                   R=[posB], W=[pbB_])
                op("dve", lambda e: e.tensor_copy(out=paf[:], in_=pa[:]), R=[paB], W=[pafB])
                op("dve", lambda e: e.tensor_copy(out=pbf[:], in_=pb_[:]), R=[pbB_], W=[pbfB])
                io4 = iota16[:, :].unsqueeze(1).unsqueeze(1).broadcast_to([128, 8, 16, 16])
                for (rk, rkB, side, dst, dstB) in ((paf, pafB, 0, s0, s0B), (pbf, pbfB, 1, s1, s1B)):
                    op(RG_ENG, lambda e, rk=rk: e.tensor_tensor(
                        out=T4, in0=io4, in1=rk[:, :, :].unsqueeze(3).broadcast_to([128, 8, 16, 16]), op=ALU.is_equal),
                       R=[iotaB, rkB], W=scBs)
                    op(RG_ENG, lambda e, side=side: e.tensor_tensor(
                        out=T4, in0=T4, in1=ixf4[:, :, side, :].unsqueeze(2).broadcast_to([128, 8, 16, 16]), op=ALU.mult),
                       R=scBs + [ixfB], W=scBs)
                    op("dve", lambda e, dst=dst: e.tensor_reduce(out=dst[:], in_=T4, axis=AX.X, op=ALU.add),
                       R=scBs, W=[dstB])
                op("dve", lambda e: e.scalar_tensor_tensor(
                    out=idxf[:, :].rearrange("p (h k) -> p h k", h=8), in0=s0[:], scalar=128.0, in1=s1[:],
                    op0=ALU.mult, op1=ALU.add), R=[s0B, s1B], W=[idxfB])
                op("dve", lambda e: e.tensor_copy(out=idxu[par][:], in_=idxf[:]), R=[idxfB], W=[idxuB[par]])
                op("dve", lambda e: e.tensor_reduce(out=zz[:], in_=ee[:], axis=AX.X, op=ALU.add), R=[eeB], W=[zzB[0]])
                op("dve", lambda e: e.reciprocal(out=rz[:], in_=zz[:]), R=[zzB[0]], W=[rzB])
                op("dve", lambda e: e.tensor_tensor(
                    out=gg[par][:, :].rearrange("p (h k) -> p h k", h=8), in0=ee[:],
                    in1=rz[:, :].unsqueeze(2).broadcast_to([128, 8, 16]), op=ALU.mult),
                   R=[eeB, rzB], W=[ggB[par]])

            def back(ti, i, pending):
                par = ti % 2
                for j in range(128):
                    ndve = 2 if (j % 2 == 1 and j >= 24) else 1
                    nother = 12
                    while pending:
                        en_ = pending[0][0]
                        if en_ == "dve":
                            if ndve == 0:
                                break
                            ndve -= 1
                        else:
                            if nother == 0:
                                break
                            nother -= 1
                        pending.pop(0)[1]()
                    b = (ti * 128 + j) % NBUF_G
                    op("pool", lambda e, b=b, j=j: e.indirect_dma_start(
                        out=gbuf[b][:], out_offset=None, in_=uvb_d[layer][:, :],
                        in_offset=bass.IndirectOffsetOnAxis(ap=idxu[par][:, j:j + 1], axis=0)),
                       R=[idxuB[par], uvbB[layer]], W=[gbufB[b]], lane="g%d" % b)
                    routed = PE_DOT > 0 and (j % PE_DOT == 0) and (PE_DOT % 2 == 0)
                    if routed:
                        kk = (j // PE_DOT) % 2
                        for dc in range(8):
                            op("pe", lambda e, b=b, dc=dc: e.transpose(out=pv7[:, dc, :], in_=gbuf[b][:, dc * 128:(dc + 1) * 128],
                                                                       identity=ident[:]),
                               R=[gbufB[b], identB], W=[pbB[7]])
                        op("act", lambda e, kk=kk: e.copy(out=ugT[kk][:], in_=pv7), R=[pbB[7]], W=[ugTB[kk]])
                    else:
                        op("dve", lambda e, b=b, j=j: e.scalar_tensor_tensor(
                            out=junk2s[j % 2][:], in0=gbuf[b][:, 0:D], scalar=1.0, in1=xn[par][:], op0=ALU.mult,
                            op1=ALU.mult, accum_out=hh[:, par, j:j + 1]),
                           R=[gbufB[b], xnB[par]], W=[hhB[par][j], junk2B[j % 2]])
                    if PE_DOT > 0 and (PE_DOT % 2 == 0) and (j % PE_DOT == 1):
                        j0 = j - 1
                        kk = (j0 // PE_DOT) % 2
                        for dc in range(8):
                            op("pe", lambda e, kk=kk, dc=dc: e.matmul(
                                pbank[0][:, 0:128], lhsT=ugT[kk][:, dc, :], rhs=xnTp[par][:, dc, :],
                                start=(dc == 0), stop=(dc == 7)),
                               R=[ugTB[kk], xnTpB[par]], W=[pbB[0]])
                        op("dve", lambda e, j0=j0: e.scalar_tensor_tensor(
                            out=junk2s[0][:, 0:128], in0=pbank[0][:, 0:128], scalar=1.0, in1=identf[:], op0=ALU.mult,
                            op1=ALU.mult, accum_out=hh[:, par, j0:j0 + 1]),
                           R=[pbB[0], identfB], W=[hhB[par][j0], junk2B[0]])
                    if j % JG == JG - 1:
                        g0 = j - (JG - 1)
                        gi = g0 // JG
                        op("act", lambda e, g0=g0: e.activation(out=gl[:, par, g0:g0 + JG], in_=hh[:, par, g0:g0 + JG],
                                                                func=AF.Gelu),
                           R=[hhB[par][g0 + t] for t in range(JG)], W=[glB[par][gi]])
                        dpar = gi % 2
                        for t in range(JG):
                            jj = g0 + t
                            ds = dpar * JG + t
                            op("act", lambda e, jj=jj: e.activation(
                                out=aa[:, par, jj:jj + 1], in_=gl[:, par, jj:jj + 1], func=AF.Copy,
                                scale=gg[par][:, jj:jj + 1]),
                               R=[glB[par][gi], ggB[par]], W=[aaB[par][gi]] if t == 0 else [aaB2[par][gi]])
                            op("act", lambda e, jj=jj, ds=ds: e.activation(
                                out=dd[:, ds, :], in_=identf[:], func=AF.Copy, scale=aa[:, par, jj:jj + 1]),
                               R=[aaB[par][gi] if t == 0 else aaB2[par][gi], identfB], W=[ddB[ds]])
                        for t in range(JG):
                            jj = g0 + t
                            ds = dpar * JG + t
                            bb = (ti * 128 + jj) % NBUF_G
                            for half in range(2):
                                op("pe", lambda e, ds=ds, bb=bb, half=half, jj=jj: e.matmul(
                                    pbank[5 + half][:, :], lhsT=dd[:, ds, :],
                                    rhs=gbuf[bb][:, D + half * 512: D + (half + 1) * 512],
                                    start=(jj == 0), stop=(jj == 127)),
                                   R=[ddB[ds], gbufB[bb]], W=[pbB[5 + half]])
                while pending:
                    pending.pop(0)[1]()
                for half in range(2):
                    op("dve", lambda e, half=half: e.tensor_tensor(
                        out=xres[:, i, half * 512:(half + 1) * 512], in0=xres[:, i, half * 512:(half + 1) * 512],
                        in1=pbank[5 + half][:, :], op=ALU.add),
                       R=[pbB[5 + half], xresB[i]], W=[xresB[i]])
                if final:
                    xr = xres[:, i, :]
                    r, rB = emit_rstd(xr, xresB[i])
                    ytv = scflat[:, 0:D]
                    op("dve", lambda e: e.scalar_tensor_tensor(out=ytv, in0=xr, scalar=r, in1=gA[:],
                                                               op0=ALU.mult, op1=ALU.mult),
                       R=[xresB[i], rB, gAB], W=scBs)
                    o_ = i - 1
                    op("sp", lambda e: e.dma_start(out=y_d[o_ * 128:(o_ + 1) * 128, :], in_=ytv),
                       R=scBs, lane="yst%d" % (o_ % 2))

            n = len(tiles)
            front(0, tiles[0])
            for ti in range(n):
                pending = []
                if ti + 1 < n:
                    P.deferred = pending
                    front(ti + 1, tiles[ti + 1])
                    P.deferred = None
                back(ti, tiles[ti], pending)

        emit_convert(0)
        if upto >= 2:
            emit_peer(0, list(range(NT1)))

        if upto >= 3:
            fence = P.fence()
            A = AP0.fork()
            watt = A.alloc("watt", [128, 8, 1792], BF16)
            wattB = Buf()
            wo = A.alloc("wo", [128, 8, D], BF16)
            woB = Buf()
            kT = A.alloc("kT", [128, 4, NT1 * 128], BF16)
            kTB = [Buf() for _ in range(NT1)]
            vv = A.alloc("vv", [128, NT1, 256], BF16)
            vvB = [Buf() for _ in range(NT1)]
            biasm = A.alloc("biasm", [128, 16, 384], F32)
            biasmB = Buf()

            sinkbc = A.alloc("sinkbc", [128, 16], F32)
            sinkB = Buf()
            penb = A.alloc("penb", [1, NT1 * 128], BF16)
            penB = Buf()
            ones1 = A.alloc("ones1", [1, 128], BF16)
            ones1B = Buf()
            qTas = [A.alloc("qTa%d" % i, [128, 8, 128], BF16) for i in range(2)]
            qTaBss = [[Buf() for _ in range(2)] for _ in range(2)]
            LL = [A.alloc("LL%d" % i, [128, 384], F32) for i in range(2)]
            LLB = [Buf() for _ in range(2)]
            EE = [A.alloc("EE%d" % i, [128, 384], BF16) for i in range(2)]
            EEB = [Buf() for _ in range(2)]
            ET = [A.alloc("ET%d" % i, [128, 3, 128], BF16) for i in range(2)]
            ETB = [Buf() for _ in range(2)]
            mrow = A.alloc("mrow", [128, 16], F32)
            mrowB = [Buf() for _ in range(16)]
            nmrow = A.alloc("nmrow", [128, 16], F32)
            nmrowB = [Buf() for _ in range(16)]
            rsum = A.alloc("rsum", [128, 16], F32)
            rsumB = [Buf() for _ in range(16)]
            esink = A.alloc("esink", [128, 16], F32)
            esinkB = [Buf() for _ in range(16)]
            den = A.alloc("den", [128, 16], F32)
            denB = Buf()
            rden = A.alloc("rden", [128, 16], F32)
            rdenB = Buf()
            ao_off = (A.p + 31) // 32 * 32
            ao = A.alloc("ao", [128, D], BF16)
            aoBs = [Buf() for _ in range(2)]
            wmask = nc.alloc_sbuf_tensor_at("wmask", [128, 384], F32, offset=ao_off)
            wmaskB = aoBs[0]
            aoT = A.alloc("aoT", [128, 8, 128], BF16)
            aoTB = Buf()

            for k in range(4):
                lo, hi = k * 448, (k + 1) * 448
                op("pool", lambda e, lo=lo, hi=hi: e.dma_start(
                    out=watt[:, :, lo:hi], in_=w_att_d[:, lo:hi].rearrange("(dc dp) n -> dp dc n", dp=128)),
                   W=[wattB], lane="w%d" % k, extra=fence)
            for k in range(2):
                op("pool", lambda e, k=k: e.dma_start(
                    out=wo[:, :, k * 512:(k + 1) * 512],
                    in_=w_o_d[:, k * 512:(k + 1) * 512].rearrange("(dc dp) n -> dp dc n", dp=128)),
                   W=[woB], lane="w%d" % (2 + k), extra=fence)
            op("pool", lambda e: e.dma_start(out=penb[:], in_=pen_d[:, :]), W=[penB], lane="w1", extra=fence)
            op("sp", lambda e: e.dma_start(out=biasm[:], in_=bias_d.rearrange("p (h k) -> p h k", h=16)),
               W=[biasmB], lane="c_bias", extra=fence)
            op("sp", lambda e: e.dma_start(out=wmask[:], in_=wmask_d[:, :]), W=[wmaskB], lane="c_wmask", extra=fence)
            op("sp", lambda e: e.dma_start(out=sinkbc[:], in_=sink_d.partition_broadcast(128)), W=[sinkB], lane="c_sink",
               extra=fence)
            op("dve", lambda e: e.tensor_tensor(out=biasm[:], in0=biasm[:],
                                                in1=wmask[:, :].unsqueeze(1).broadcast_to([128, 16, 384]), op=ALU.add),
               R=[wmaskB, biasmB], W=[biasmB])
            op("dve", lambda e: e.tensor_scalar(out=biasm[:], in0=biasm[:], scalar1=-1.0, scalar2=None, op0=ALU.mult),
               R=[biasmB], W=[biasmB])
            op("dve", lambda e: e.memset(ones1[:], 1.0), W=[ones1B], extra=fence)
            load_gain(gA, gAB, 2)

            def norm_T(i, bank=0):
                xr = xres[:, i, :]
                r, rB = emit_rstd(xr, xresB[i])
                op("dve", lambda e: e.scalar_tensor_tensor(out=xnb[:], in0=xr, scalar=r, in1=gA[:],
                                                           op0=ALU.mult, op1=ALU.mult),
                   R=[xresB[i], rB, gAB], W=[xnbB])
                emit_transpose8(xnb, xnbB, xnT[:], xnTB, bank=bank)

            for i in range(NT1):
                emit_convert(1, 1)
                norm_T(i)
                for g in range(4):
                    for dc in range(8):
                        op("pe", lambda e, g=g, dc=dc: e.matmul(
                            pbank[1][:, g * 128:(g + 1) * 128], lhsT=watt[:, dc, 1024 + g * 128: 1024 + (g + 1) * 128],
                            rhs=xnT[:, dc, :], start=(dc == 0), stop=(dc == 7)),
                           R=[wattB, xnTB], W=[pbB[1]])
                op("act", lambda e, i=i: e.copy(out=kT[:, :, i * 128:(i + 1) * 128],
                                                in_=pbank[1][:, :].rearrange("p (g t) -> p g t", g=4)),
                   R=[pbB[1]], W=[kTB[i]])
                for dc in range(8):
                    op("pe", lambda e, dc=dc: e.matmul(pbank[2][:, 0:256], lhsT=xnT[:, dc, :], rhs=watt[:, dc, 1536:1792],
                                                       start=(dc == 0), stop=(dc == 7)),
                       R=[wattB, xnTB], W=[pbB[2]])
                op("act", lambda e, i=i: e.copy(out=vv[:, i, :], in_=pbank[2][:, 0:256]), R=[pbB[2]], W=[vvB[i]])

            def pre(o_):
                i = o_ + 1
                qq = qTas[o_ % 2]
                norm_T(i, bank=1)
                for cq in range(8):
                    bk = 1 + cq // 4
                    for dc in range(8):
                        op("pe", lambda e, cq=cq, dc=dc, bk=bk: e.matmul(
                            pbank[bk][:, (cq % 4) * 128:(cq % 4 + 1) * 128], lhsT=watt[:, dc, cq * 128:(cq + 1) * 128],
                            rhs=xnT[:, dc, :], start=(dc == 0), stop=(dc == 7)),
                           R=[wattB, xnTB], W=[pbB[bk]])
                for b4 in range(2):
                    op("act", lambda e, b4=b4: e.copy(out=qq[:, b4 * 4:(b4 + 1) * 4, :],
                                                      in_=pbank[1 + b4][:, :].rearrange("p (a b) -> p a b", a=4)),
                       R=[pbB[1 + b4]], W=[qTaBss[o_ % 2][b4]])

            def head(o_, pend):
                i = o_ + 1
                qTa = qTas[o_ % 2]
                qTaBs = qTaBss[o_ % 2]
                edge = (o_ == 0) or (o_ == NO - 1)
                npull = (len(pend) + 15) // 16

                def st_S(h):
                    cq, hf, g = h // 2, h % 2, h // 4
                    q = h % 2
                    bk = 3 + q
                    ps_s = pbank[bk][:, 0:384]
                    op("pe", lambda e: e.matmul(
                        ps_s, lhsT=qTa[64 * hf:64 * hf + 64, cq, :],
                        rhs=kT[64 * hf:64 * hf + 64, g, (i - 1) * 128:(i + 2) * 128], start=True, stop=not edge),
                       R=[qTaBs[cq // 4], kTB[i - 1], kTB[i], kTB[i + 1]], W=[pbB[bk]])
                    if edge:
                        op("pe", lambda e: e.matmul(
                            ps_s, lhsT=ones1[0:1, :], rhs=penb[0:1, (i - 1) * 128:(i + 2) * 128], start=False, stop=True),
                           R=[ones1B, penB], W=[pbB[bk]])
                    op("dve", lambda e: e.scalar_tensor_tensor(
                        out=LL[q][:], in0=ps_s, scalar=-0.125, in1=biasm[:, h, :], op0=ALU.mult, op1=ALU.add),
                       R=[pbB[bk], biasmB], W=[LLB[q]])
                    op("dve", lambda e: e.tensor_reduce(out=nmrow[:, h:h + 1], in_=LL[q][:], axis=AX.X, op=ALU.min),
                       R=[LLB[q]], W=[nmrowB[h]])
                    op("act", lambda e: e.activation(out=EE[q][:], in_=LL[q][:], func=AF.Exp, scale=-1.0,
                                                     bias=nmrow[:, h:h + 1], accum_out=rsum[:, h:h + 1]),
                       R=[LLB[q], nmrowB[h]], W=[EEB[q], rsumB[h]])
                    op("act", lambda e: e.activation(out=esink[:, h:h + 1], in_=nmrow[:, h:h + 1], func=AF.Exp,
                                                     bias=sinkbc[:, h:h + 1], scale=1.0),
                       R=[nmrowB[h], sinkB], W=[esinkB[h]])

                def st_T(h):
                    q = h % 2
                    ptbank = 5 if q == 0 else 0
                    pt = pbank_bf[ptbank][:, 0:384].rearrange("p (a b) -> p a b", a=3)
                    ptB = pbB[ptbank]
                    for kb in range(3):
                        op("pe", lambda e, kb=kb: e.transpose(out=pt[:, kb, :], in_=EE[q][:, kb * 128:(kb + 1) * 128],
                                                              identity=ident[:]),
                           R=[EEB[q], identB], W=[ptB])
                    op("act", lambda e: e.copy(out=ET[q][:], in_=pt), R=[ptB], W=[ETB[q]])

                def st_V(h):
                    g = h // 4
                    q = h % 2
                    bko = 6 + h // 8
                    for kb in range(3):
                        op("pe", lambda e, kb=kb: e.matmul(
                            pbank[bko][:, (h % 8) * 64:(h % 8 + 1) * 64], lhsT=ET[q][:, kb, :],
                            rhs=vv[:, i - 1 + kb, g * 64:(g + 1) * 64], start=(kb == 0), stop=(kb == 2)),
                           R=[ETB[q], vvB[i - 1 + kb]], W=[pbB[bko]])

                for k_ in range(16 + 2):
                    if k_ < 16:
                        st_S(k_)
                    if 0 <= k_ - 1 < 16:
                        st_T(k_ - 1)
                    if 0 <= k_ - 2 < 16:
                        st_V(k_ - 2)
                    for _ in range(npull):
                        if pend:
                            pend.pop(0)[1]()
                while pend:
                    pend.pop(0)[1]()

            def tail_now(o_):
                op("dve", lambda e: e.tensor_tensor(out=den[:], in0=rsum[:], in1=esink[:], op=ALU.add),
                   R=rsumB + esinkB, W=[denB])
                op("dve", lambda e: e.reciprocal(out=rden[:], in_=den[:]), R=[denB], W=[rdenB])
                for hb in range(2):
                    op("dve", lambda e, hb=hb: e.tensor_tensor(
                        out=ao[:, hb * 512:(hb + 1) * 512].rearrange("p (h d) -> p h d", h=8),
                        in0=pbank[6 + hb][:, :].rearrange("p (h d) -> p h d", h=8),
                        in1=rden[:, hb * 8:(hb + 1) * 8].unsqueeze(2).broadcast_to([128, 8, 64]), op=ALU.mult),
                       R=[pbB[6 + hb], rdenB], W=[aoBs[hb]])

            def tail_def(o_):
                i = o_ + 1
                emit_transpose8(ao, aoBs, aoT[:], aoTB, bank=2)
                for half in range(2):
                    bk = 1 + half
                    for cc in range(8):
                        op("pe", lambda e, half=half, cc=cc, bk=bk: e.matmul(
                            pbank[bk][:, :], lhsT=aoT[:, cc, :], rhs=wo[:, cc, half * 512:(half + 1) * 512],
                            start=(cc == 0), stop=(cc == 7)),
                           R=[aoTB, woB], W=[pbB[bk]])
                    op("dve", lambda e, half=half, bk=bk: e.tensor_tensor(
                        out=xres[:, i, half * 512:(half + 1) * 512], in0=xres[:, i, half * 512:(half + 1) * 512],
                        in1=pbank[bk][:, :], op=ALU.add),
                       R=[pbB[bk], xresB[i]], W=[xresB[i]])

            pre(0)
            pend_tail = []
            for o_ in range(NO):
                pend = pend_tail
                if o_ + 1 < NO:
                    P.deferred = []
                    pre(o_ + 1)
                    pend = pend + P.deferred
                    P.deferred = None
                head(o_, pend)
                tail_now(o_)
                P.deferred = pend_tail = []
                tail_def(o_)
                P.deferred = None
            for _, th in pend_tail:
                th()

        emit_convert(1)
        if upto >= 4:
            emit_peer(1, list(range(1, NO + 1)), final=True)

        if dbg:
            for i in range(NT1):
                op("sp", lambda e, i=i: e.dma_start(out=dbg_d[i * 128:(i + 1) * 128, :], in_=xres[:, i, :]),
                   R=[xresB[i]], lane="dbg")
        P.raw("sp", lambda e: e.nop(), deps=P.fence())
        P.build()
    return nc


def _t5_bucket_np(rel):
    half = 16
    max_exact = 8
    ret = np.where(rel > 0, half, 0)
    n = np.abs(rel)
    nf = np.maximum(n, 1).astype(np.float32)
    large = max_exact + (np.log(nf / max_exact) / np.float32(np.log(128 / max_exact)) * (half - max_exact)).astype(np.int32)
    large = np.minimum(large, half - 1)
    return ret + np.where(n < max_exact, n, large)


def prep_shared(inp):
    f = lambda a: np.ascontiguousarray(np.asarray(a, dtype=np.float32))
    sh = {}
    sh["w_in"] = f(inp["conv_w_in"][0])
    cwv = np.asarray(inp["conv_w"][0], np.float32)
    sh["cw"] = f(cwv.T.reshape(8, 128, 3).transpose(1, 0, 2).reshape(128, 24))
    sh["w_out"] = f(inp["conv_w_out"][0])
    wqkv = np.asarray(inp["attn_w_qkv"][0], np.float32)
    wq_, wk_, wv_ = wqkv[:, :1024], wqkv[:, 1024:1280], wqkv[:, 1280:1536]
    kd = []
    for g in range(4):
        kd += [wk_[:, g * 64:(g + 1) * 64], wk_[:, g * 64:(g + 1) * 64]]
    sh["w_att"] = f(np.concatenate([wq_] + kd + [wv_], axis=1))
    sh["w_o"] = f(inp["attn_w_o"][0])
    sh["sink"] = f(np.asarray(inp["attn_sink"][0]).reshape(1, 16))
    qi = np.arange(128)[:, None]
    kj = np.arange(384)[None, :]
    rel = kj - 128 - qi
    bk = _t5_bucket_np(rel)
    rb = np.asarray(inp["rel_bias"], np.float32)
    sh["bias_tab"] = f(rb[bk].transpose(0, 2, 1).reshape(128, 16 * 384))
    sh["wmask"] = f(np.where(np.abs(rel) <= 128, 0.0, -30000.0))
    sh["gains"] = f(np.stack([inp["conv_norm_g"][0], inp["ffn_norm_g"][0], inp["attn_norm_g"][0],
                              inp["ffn_norm_g"][1], inp["final_norm_g"]], axis=0))
    sh["w_pq"] = f(inp["peer_w_q"])
    sk = np.asarray(inp["peer_subkeys"], np.float32)
    sh["skT"] = f(sk.transpose(0, 4, 1, 2, 3).reshape(2, 128, 2048))
    for l in range(2):
        sh["uv%d" % l] = f(np.concatenate([np.asarray(inp["peer_u"][l], np.float32),
                                            np.asarray(inp["peer_v"][l], np.float32)], axis=1))
    sh["iota"] = f(np.broadcast_to(np.arange(16, dtype=np.float32), (128, 16)))
    return sh


def prep_core(x, b, k, NO, S):
    NT1, NX = NO + 2, NO + 4
    own0 = k * NO * 128
    lo = own0 - 256
    xe = np.zeros((NX * 128, D), np.float32)
    a, e = max(lo, 0), min(lo + NX * 128, S)
    xe[a - lo:e - lo] = x[b, a:e]
    pen = np.zeros((1, NT1 * 128), np.float32)
    t = own0 - 128 + np.arange(NT1 * 128)
    pen[0, (t < 0) | (t >= S)] = -240000.0
    return {"x_ext": xe, "pen": pen}


_NC_CACHE = {}


def kernel(**inputs):
    x = np.asarray(inputs["x"], np.float32)
    B, S, _ = x.shape
    NO = 16
    ncores = 8
    per_b = ncores // B
    sh = prep_shared(inputs)
    in_maps = []
    for c in range(ncores):
        b, k = c // per_b, c % per_b
        m = dict(sh)
        m.update(prep_core(x, b, k, NO, S))
        in_maps.append(m)
    nc = build(NO=NO)
    res = run_bass_kernel_spmd(nc, in_maps, core_ids=list(range(ncores)))
    out = np.zeros((B, S, D), np.float32)
    for c in range(ncores):
        b, k = c // per_b, c % per_b
        out[b, k * NO * 128:(k + 1) * NO * 128] = res.results[c]["y"]
    return out
```

```python
import contextlib
import os
import numpy as np
import concourse.bass as bass
import concourse.mybir as mybir
from concourse.bass_utils import run_bass_kernel_spmd

F32 = mybir.dt.float32
BF16 = mybir.dt.bfloat16
U32 = mybir.dt.uint32
ALU = mybir.AluOpType
AF = mybir.ActivationFunctionType
AX = mybir.AxisListType

D = 1024
EPS = 1e-6
SAME_ENGINE_SYNC = os.environ.get("K_SES", "1") == "1"
NBUF_G = int(os.environ.get("K_NB", "10"))
PE_DOT = int(os.environ.get("K_PEDOT", "0"))
JG = int(os.environ.get("K_JG", "1"))
RG_ENG = os.environ.get("K_RG", "dve")
FRONT_PULL = 3


class Op:
    __slots__ = ("eng", "fn", "deps", "lane", "count", "signaled")

    def __init__(self, eng, fn, deps, lane):
        self.eng = eng
        self.fn = fn
        self.deps = deps
        self.lane = lane
        self.count = None
        self.signaled = False


class Buf:
    __slots__ = ("w", "r")

    def __init__(self):
        self.w = {}
        self.r = {}


class Prog:
    ENGS = ("pe", "act", "dve", "pool", "sp")

    def __init__(self, nc):
        self.nc = nc
        self.ops = []
        self.lane_last = {}
        self.deferred = None

    def call(self, fn):
        if self.deferred is not None:
            self.deferred.append(("none", fn))
        else:
            fn()

    def raw(self, eng, fn, deps=(), lane=None):
        deps = [d for d in deps if d is not None]
        if lane is not None:
            prev = self.lane_last.get(lane)
            if prev is not None:
                deps.append(prev)
        o = Op(eng, fn, deps, lane)
        if lane is not None:
            self.lane_last[lane] = o
        self.ops.append(o)
        return o

    def op(self, eng, fn, R=(), W=(), lane=None, extra=()):
        if self.deferred is not None:
            self.deferred.append((eng, lambda: self._op(eng, fn, R, W, lane, extra)))
            return None
        return self._op(eng, fn, R, W, lane, extra)

    def _op(self, eng, fn, R=(), W=(), lane=None, extra=()):
        deps = list(extra)
        for b in R:
            deps.extend(b.w.values())
        for b in W:
            deps.extend(b.w.values())
            deps.extend(b.r.values())
        o = self.raw(eng, fn, deps, lane)
        key = ("l", lane) if lane is not None else ("e", eng)
        for b in R:
            b.r[key] = o
        for b in W:
            b.w = {key: o}
            b.r = {}
        return o

    def fence(self):
        last = {}
        for o in self.ops:
            key = ("l", o.lane) if o.lane is not None else ("e", o.eng)
            last[key] = o
        return list(last.values())

    def build(self):
        nc = self.nc
        for o in self.ops:
            nd = []
            seen = set()
            for d in o.deps:
                if id(d) in seen:
                    continue
                seen.add(id(d))
                if d.lane is None and d.eng == o.eng and o.lane is None:
                    if d.eng == "pe" or not SAME_ENGINE_SYNC:
                        continue
                nd.append(d)
            o.deps = nd
            for d in nd:
                d.signaled = True
        lanes = sorted({o.lane for o in self.ops if o.lane is not None})
        with contextlib.ExitStack() as es:
            esem = {e: es.enter_context(nc.semaphore("s_" + e)) for e in self.ENGS}
            lsem = {l: es.enter_context(nc.semaphore("l_" + str(l))) for l in lanes}
            ecount = {e: 0 for e in self.ENGS}
            lcount = {l: 0 for l in lanes}
            for o in self.ops:
                if o.lane is not None:
                    lcount[o.lane] += 16
                    o.count = lcount[o.lane]
                elif o.signaled:
                    ecount[o.eng] += 1
                    o.count = ecount[o.eng]
            self.final_counts = (dict(ecount), dict(lcount))
            block = es.enter_context(nc.Block())
            ops = self.ops

            def emit_for(engname):
                def body(eng):
                    waited = {}
                    for o in ops:
                        if o.eng != engname:
                            continue
                        need = {}
                        for d in o.deps:
                            key = ("l", d.lane) if d.lane is not None else ("e", d.eng)
                            if d.count > need.get(key, 0):
                                need[key] = d.count
                        for key, cnt in need.items():
                            if waited.get(key, 0) >= cnt:
                                continue
                            sem = lsem[key[1]] if key[0] == "l" else esem[key[1]]
                            eng.wait_ge(sem, cnt)
                            waited[key] = cnt
                        ins = o.fn(eng)
                        if o.lane is not None:
                            ins.then_inc(lsem[o.lane], 16)
                        elif o.signaled:
                            ins.then_inc(esem[o.eng], 1)
                return body

            block.tensor(emit_for("pe"))
            block.scalar(emit_for("act"))
            block.vector(emit_for("dve"))
            block.gpsimd(emit_for("pool"))
            block.sync(emit_for("sp"))


def _dsize(dt):
    return {F32: 4, BF16: 2, U32: 4}[dt]


class Arena:
    def __init__(self, nc, start, top):
        self.nc = nc
        self.p = start
        self.top = top
        self.n = 0

    def alloc(self, name, shape, dt):
        nbytes = int(np.prod(shape[1:])) * _dsize(dt)
        off = (self.p + 31) // 32 * 32
        self.p = off + nbytes
        assert self.p <= self.top, (name, self.p, self.top)
        self.n += 1
        return self.nc.alloc_sbuf_tensor_at(name, list(shape), dt, offset=off)

    def fork(self):
        return Arena(self.nc, self.p, self.top)


def build(NO=16, upto=99, dbg=False):
    NT1 = NO + 2
    NX = NO + 4
    nc = bass.Bass("TRN2", target_bir_lowering=False)

    def dr(name, shape, dt=F32, kind="ExternalInput"):
        return nc.dram_tensor(name, list(shape), dt, kind=kind).ap()

    x_ext = dr("x_ext", [NX * 128, D])
    pen_d = dr("pen", [1, NT1 * 128])
    w_in_d = dr("w_in", [D, 3 * D])
    cw_d = dr("cw", [128, 24])
    w_out_d = dr("w_out", [D, D])
    w_att_d = dr("w_att", [D, 1792])
    w_o_d = dr("w_o", [D, D])
    sink_d = dr("sink", [1, 16])
    bias_d = dr("bias_tab", [128, 16 * 384])
    wmask_d = dr("wmask", [128, 384])
    gains_d = dr("gains", [5, D])
    w_pq_d = dr("w_pq", [2, D, 2048])
    skT_d = dr("skT", [2, 128, 2048])
    uv_d = [dr("uv0", [16384, 2048]), dr("uv1", [16384, 2048])]
    iota_d = dr("iota", [128, 16])
    uvb_d = [nc.dram_tensor("uvb%d" % l, [16384, 2048], BF16, kind="Internal").ap() for l in range(2)]
    y_d = dr("y", [NO * 128, D], kind="ExternalOutput")
    if dbg:
        dbg_d = dr("dbg", [NT1 * 128, D], kind="ExternalOutput")

    P = Prog(nc)
    op = P.op

    with contextlib.ExitStack() as es:
        pbank = [es.enter_context(nc.psum_tensor("pb%d" % i, [128, 512], F32)) for i in range(8)]
        pbB = [Buf() for _ in range(8)]
        pbB5b = Buf()
        pbank_bf = [p.bitcast(BF16) for p in pbank]

        A0 = Arena(nc, (nc.sbuf_base + 63) // 64 * 64, nc.sbuf_top)
        xres = A0.alloc("xres", [128, NT1, D], F32)
        xresB = [Buf() for _ in range(NT1)]
        ident = A0.alloc("ident", [128, 128], BF16)
        identB = Buf()
        identf = A0.alloc("identf", [128, 128], F32)
        identfB = Buf()
        iota16 = A0.alloc("iota16", [128, 16], F32)
        iotaB = Buf()
        gA = A0.alloc("gA", [128, D], F32)
        gAB = Buf()
        gB = A0.alloc("gB", [128, D], F32)
        gBB = Buf()
        stat = A0.alloc("stat", [128, 96], F32)
        statB = [Buf() for _ in range(96)]
        junk = A0.alloc("junk", [128, D], BF16)
        junkB = Buf()
        junk2_off = (A0.p + 31) // 32 * 32
        junk2 = A0.alloc("junk2", [128, D], BF16)
        junk3 = A0.alloc("junk3", [128, D], BF16)
        junk2s = [junk2, junk3]
        junk2B = [Buf(), Buf()]
        xnb = A0.alloc("xnb", [128, D], BF16)
        xnbB = Buf()
        xnT = A0.alloc("xnT", [128, 8, 128], BF16)
        xnTB = Buf()

        stat_ctr = [0]

        def new_stat():
            k = stat_ctr[0] % 96
            stat_ctr[0] += 1
            return stat[:, k:k + 1], statB[k]

        op("pool", lambda e: e.memset(ident[:], 1.0), W=[identB])
        op("pool", lambda e: e.affine_select(out=ident[:], in_=ident[:], pattern=[[-1, 128]],
                                             compare_op=ALU.is_equal, fill=0.0, base=0,
                                             channel_multiplier=1), R=[identB], W=[identB])
        op("pool", lambda e: e.tensor_copy(out=identf[:], in_=ident[:]), R=[identB], W=[identfB])
        op("sp", lambda e: e.dma_start(out=iota16[:], in_=iota_d[:, :]), W=[iotaB], lane="c_iota")

        def load_gain(dst, dstB, row):
            return op("sp", lambda e: e.dma_start(out=dst[:], in_=gains_d[row:row + 1, :].partition_broadcast(128)),
                      W=[dstB], lane="gain")

        def emit_rstd(src_ap, srcB):
            ss, ssB = new_stat()
            op("act", lambda e: e.activation(out=junk[:], in_=src_ap, func=AF.Square, accum_out=ss),
               R=[srcB], W=[ssB, junkB])
            sd, sdB = new_stat()
            op("act", lambda e: e.activation(out=sd, in_=ss, func=AF.Sqrt, bias=EPS_AP[0], scale=1.0 / D),
               R=[ssB, epsB], W=[sdB])
            r, rB = new_stat()
            op("dve", lambda e: e.reciprocal(out=r, in_=sd), R=[sdB], W=[rB])
            return r, rB

        def emit_transpose8(src_bf, srcB_, dst_ap, dstB_, bank=0):
            pv = pbank_bf[bank][:, 0:1024].rearrange("p (a b) -> p a b", a=8)
            for dc in range(8):
                op("pe", lambda e, dc=dc: e.transpose(out=pv[:, dc, :], in_=src_bf[:, dc * 128:(dc + 1) * 128],
                                                      identity=ident[:]),
                   R=(list(srcB_) if isinstance(srcB_, (list, tuple)) else [srcB_]) + [identB], W=[pbB[bank]])
            op("act", lambda e: e.copy(out=dst_ap, in_=pv), R=[pbB[bank]], W=[dstB_])

        epsT = A0.alloc("epsT", [128, 1], F32)
        epsB = Buf()
        EPS_AP = [epsT[:, 0:1]]
        op("pool", lambda e: e.memset(epsT[:], EPS), W=[epsB])

        AP0 = A0

        NCV = 16
        RCV = 16384 // NCV
        uvbB = [Buf(), Buf()]

        CV_ROWS = {0: RCV, 1: RCV}
        CV_LANES = {0: NCV, 1: NCV}
        conv_left = {0: list(range(16384 // CV_ROWS[0])), 1: list(range(16384 // CV_ROWS[1]))}

        def emit_convert(layer, nchunks=10 ** 9):
            rows = CV_ROWS[layer]
            for _ in range(nchunks):
                if not conv_left[layer]:
                    break
                k = conv_left[layer].pop(0)
                op("pool", lambda e, k=k, rows=rows: e.dma_start(out=uvb_d[layer][k * rows:(k + 1) * rows, :],
                                                                 in_=uv_d[layer][k * rows:(k + 1) * rows, :]),
                   W=[], lane="cv%d_%d" % (layer, k % CV_LANES[layer]))
            if not conv_left[layer]:
                uvbB[layer].w = {("l", "cv%d_%d" % (layer, q)): P.lane_last["cv%d_%d" % (layer, q)]
                                 for q in range(CV_LANES[layer]) if ("cv%d_%d" % (layer, q)) in P.lane_last}

        if upto >= 1:
            A = AP0.fork()
            xtmp = nc.alloc_sbuf_tensor_at("xtmp", [128, D], F32, offset=junk2_off)
            xtmpB = Buf()
            xnb2 = A.alloc("xnb2", [128, D], BF16)
            xnb2B = Buf()
            win = A.alloc("win", [128, 8, 3 * D], BF16)
            winB = Buf()
            wout = A.alloc("wout", [128, 8, D], BF16)
            woutB = Buf()
            cw = A.alloc("cw", [128, 8, 3], F32)
            cwB = Buf()
            xnT_all = A.alloc("xnT_all", [128, 8, NX * 128], BF16)
            xnT_allB = [Buf() for _ in range(NX)]
            hsb = [A.alloc("hsb%d" % i, [128, 130], F32) for i in range(2)]
            hsbB = [Buf() for _ in range(2)]
            zsb = [A.alloc("zsb%d" % i, [128, 130], F32) for i in range(2)]
            zsbB = [Buf() for _ in range(2)]
            ysb = [A.alloc("ysb%d" % i, [128, 128], F32) for i in range(2)]
            ysbB = [Buf() for _ in range(2)]
            gT = [A.alloc("gT%d" % i, [128, 8, 128], BF16) for i in range(2)]
            gTB = [[Buf() for _ in range(8)] for _ in range(2)]

            for k in range(6):
                op("pool", lambda e, k=k: e.dma_start(
                    out=win[:, :, k * 512:(k + 1) * 512],
                    in_=w_in_d[:, k * 512:(k + 1) * 512].rearrange("(dc dp) n -> dp dc n", dp=128)),
                   W=[winB], lane="w%d" % (k % 4))
            op("pool", lambda e: e.dma_start(out=wout[:], in_=w_out_d.rearrange("(dc dp) n -> dp dc n", dp=128)),
               W=[woutB], lane="w1")
            op("sp", lambda e: e.dma_start(out=cw[:], in_=cw_d.rearrange("p (c k) -> p c k", k=3)), W=[cwB], lane="c_cw")
            load_gain(gA, gAB, 0)

            for e_ in range(NX):
                if e_ >= 4 and e_ % 2 == 0:
                    emit_convert(0, 1)
                if 1 <= e_ <= NT1:
                    dst, dB = xres[:, e_ - 1, :], xresB[e_ - 1]
                else:
                    dst, dB = xtmp[:], xtmpB
                op("sp", lambda e, e_=e_, dst=dst: e.dma_start(out=dst, in_=x_ext[e_ * 128:(e_ + 1) * 128, :]),
                   W=[dB], lane="xl%d" % (e_ % 4))
                r, rB = emit_rstd(dst, dB)
                xb_, xbB_ = (xnb, xnbB) if e_ % 2 == 0 else (xnb2, xnb2B)
                op("dve", lambda e, dst=dst, r=r, xb_=xb_: e.scalar_tensor_tensor(out=xb_[:], in0=dst, scalar=r, in1=gA[:],
                                                                                 op0=ALU.mult, op1=ALU.mult),
                   R=[dB, rB, gAB], W=[xbB_])
                emit_transpose8(xb_, xbB_, xnT_all[:, :, e_ * 128:(e_ + 1) * 128], xnT_allB[e_], bank=(0 if e_ % 2 == 0 else 7))

            for i in range(NT1):
                emit_convert(0, 1)
                e_ = i + 1
                c0 = 128 * e_ - 1
                par = i % 2
                for cc in range(8):
                    q = cc % 2
                    bk = 1 + q
                    psB = pbank[bk][:, 0:130]
                    psC = pbank[bk][:, 130:260]
                    psH = pbank[bk][:, 260:390]
                    for wi, pso in enumerate((psB, psC, psH)):
                        for dc in range(8):
                            op("pe", lambda e, wi=wi, pso=pso, dc=dc, cc=cc, c0=c0: e.matmul(
                                pso, lhsT=win[:, dc, wi * D + cc * 128: wi * D + (cc + 1) * 128],
                                rhs=xnT_all[:, dc, c0:c0 + 130], start=(dc == 0), stop=(dc == 7)),
                               R=[winB, xnT_allB[e_ - 1], xnT_allB[e_], xnT_allB[e_ + 1]], W=[pbB[bk]])
                    op("act", lambda e, q=q, psH=psH: e.copy(out=hsb[q][:], in_=psH), R=[pbB[bk]], W=[hsbB[q]])
                    op("dve", lambda e, q=q, psC=psC: e.tensor_tensor(out=zsb[q][:], in0=psC, in1=hsb[q][:], op=ALU.mult),
                       R=[pbB[bk], hsbB[q]], W=[zsbB[q]])
                    op("dve", lambda e, q=q, cc=cc: e.tensor_scalar(out=ysb[q][:], in0=zsb[q][:, 0:128],
                                                                   scalar1=cw[:, cc, 0:1], scalar2=None, op0=ALU.mult),
                       R=[zsbB[q], cwB], W=[ysbB[q]])
                    for kk in (1, 2):
                        op("dve", lambda e, q=q, cc=cc, kk=kk: e.scalar_tensor_tensor(
                            out=ysb[q][:], in0=zsb[q][:, kk:kk + 128], scalar=cw[:, cc, kk:kk + 1], in1=ysb[q][:],
                            op0=ALU.mult, op1=ALU.add),
                           R=[zsbB[q], cwB, ysbB[q]], W=[ysbB[q]])
                    op("dve", lambda e, q=q, cc=cc, par=par, psB=psB: e.tensor_tensor(
                        out=gT[par][:, cc, :], in0=psB[:, 1:129], in1=ysb[q][:], op=ALU.mult),
                       R=[pbB[bk], ysbB[q]], W=[gTB[par][cc]])
                for half in range(2):
                    bk = 3 + half
                    for cc in range(8):
                        op("pe", lambda e, half=half, cc=cc, par=par, bk=bk: e.matmul(
                            pbank[bk][:, :], lhsT=gT[par][:, cc, :], rhs=wout[:, cc, half * 512:(half + 1) * 512],
                            start=(cc == 0), stop=(cc == 7)),
                           R=[gTB[par][cc], woutB], W=[pbB[bk]])
                    op("dve", lambda e, half=half, i=i, bk=bk: e.tensor_tensor(
                        out=xres[:, i, half * 512:(half + 1) * 512], in0=xres[:, i, half * 512:(half + 1) * 512],
                        in1=pbank[bk][:, :], op=ALU.add),
                       R=[pbB[bk], xresB[i]], W=[xresB[i]])

        def emit_peer(layer, tiles, final=False):
            fence = P.fence()
            A = AP0.fork()
            wq = A.alloc("wq%d" % layer, [128, 8, 2048], BF16)
            wqB = Buf()
            sk = A.alloc("sk%d" % layer, [128, 16, 128], BF16)
            skB = Buf()
            xn = [A.alloc("xn%d_%d" % (layer, i), [128, D], F32) for i in range(2)]
            xnB = [Buf() for _ in range(2)]
            qT = A.alloc("qT%d" % layer, [128, 16, 128], BF16)
            qTBs = [Buf() for _ in range(4)]
            sc = A.alloc("sc%d" % layer, [128, 16, 128], F32)
            scBs = [Buf() for _ in range(4)]
            wk = A.alloc("wk%d" % layer, [128, 128], F32)
            wkB = Buf()
            mx = A.alloc("mx%d" % layer, [128, 16, 16], F32)
            mxB = Buf()
            ixu = A.alloc("ixu%d" % layer, [128, 16, 16], U32)
            ixuB = Buf()
            ixf = A.alloc("ixf%d" % layer, [128, 16, 16], F32)
            ixfB = Buf()
            cand = A.alloc("cand%d" % layer, [128, 256], F32)
            candB = Buf()
            cwk = A.alloc("cwk%d" % layer, [128, 256], F32)
            cwkB = Buf()
            tv = A.alloc("tv%d" % layer, [128, 8, 16], F32)
            tvB = Buf()
            pos = A.alloc("pos%d" % layer, [128, 8, 16], U32)
            posB = Buf()
            pa = A.alloc("pa%d" % layer, [128, 8, 16], U32)
            paB = Buf()
            pb_ = A.alloc("pb%d_" % layer, [128, 8, 16], U32)
            pbB_ = Buf()
            paf = A.alloc("paf%d" % layer, [128, 8, 16], F32)
            pafB = Buf()
            pbf = A.alloc("pbf%d" % layer, [128, 8, 16], F32)
            pbfB = Buf()
            s0 = A.alloc("s0%d" % layer, [128, 8, 16], F32)
            s0B = Buf()
            s1 = A.alloc("s1%d" % layer, [128, 8, 16], F32)
            s1B = Buf()
            idxf = A.alloc("idxf%d" % layer, [128, 128], F32)
            idxfB = Buf()
            idxu = [A.alloc("idxu%d_%d" % (layer, i), [128, 128], U32) for i in range(2)]
            idxuB = [Buf() for _ in range(2)]
            ntv = A.alloc("ntv%d" % layer, [128, 8], F32)
            ntvB = Buf()
            ee = A.alloc("ee%d" % layer, [128, 8, 16], F32)
            eeB = Buf()
            zz = A.alloc("zz%d" % layer, [128, 8], F32)
            zzB = [Buf() for _ in range(8)]
            rz = A.alloc("rz%d" % layer, [128, 8], F32)
            rzB = Buf()
            gg = [A.alloc("gg%d_%d" % (layer, i), [128, 128], F32) for i in range(2)]
            ggB = [Buf() for _ in range(2)]
            hh = A.alloc("hh%d" % layer, [128, 2, 128], F32)
            hhB = [[Buf() for _ in range(128)] for _ in range(2)]
            gl = A.alloc("gl%d" % layer, [128, 2, 128], F32)
            glB = [[Buf() for _ in range(128 // JG)] for _ in range(2)]
            aa = A.alloc("aa%d" % layer, [128, 2, 128], F32)
            aaB = [[Buf() for _ in range(128 // JG)] for _ in range(2)]
            aaB2 = [[Buf() for _ in range(128 // JG)] for _ in range(2)]
            dd = A.alloc("dd%d" % layer, [128, 2 * JG, 128], BF16)
            ddB = [Buf() for _ in range(2 * JG)]
            gbuf = [A.alloc("gb%d_%d" % (layer, i), [128, 2048], BF16) for i in range(NBUF_G)]
            gbufB = [Buf() for _ in range(NBUF_G)]
            yt = None
            xnT2 = A.alloc("xnT2_%d" % layer, [128, 8, 128], BF16)
            xnTp = [xnT, xnT2]
            xnTpB = [xnTB, Buf()]
            ugT = [A.alloc("ugT%d_%d" % (layer, i), [128, 8, 128], BF16) for i in range(2)] if PE_DOT > 0 else None
            ugTB = [Buf() for _ in range(2)]
            pv7 = pbank_bf[7][:, 0:1024].rearrange("p (a b) -> p a b", a=8)
            tb = sc

            for k in range(4):
                op("pool", lambda e, k=k: e.dma_start(
                    out=wq[:, :, k * 512:(k + 1) * 512],
                    in_=w_pq_d[layer, :, k * 512:(k + 1) * 512].rearrange("(dc dp) n -> dp dc n", dp=128)),
                   W=[wqB], lane="w%d" % k, extra=fence)
            op("pool", lambda e: e.dma_start(out=sk[:], in_=skT_d[layer].rearrange("p (c n) -> p c n", n=128)),
               W=[skB], lane="w1", extra=fence)
            load_gain(gB, gBB, 1 if layer == 0 else 3)
            if final:
                load_gain(gA, gAB, 4)

            scflat = sc[:, :, :].rearrange("p a b -> p (a b)")
            T4 = scflat.rearrange("p (h k a) -> p h k a", h=8, k=16)

            def front(ti, i):
                par = ti % 2
                xr = xres[:, i, :]
                r, rB = emit_rstd(xr, xresB[i])
                op("dve", lambda e: e.scalar_tensor_tensor(out=xn[par][:], in0=xr, scalar=r, in1=gB[:],
                                                           op0=ALU.mult, op1=ALU.mult),
                   R=[xresB[i], rB, gBB], W=[xnB[par]], extra=(fence if ti < 2 else ()))
                op("act", lambda e: e.copy(out=xnb[:], in_=xn[par][:]), R=[xnB[par]], W=[xnbB])
                emit_transpose8(xnb, xnbB, xnTp[par][:], xnTpB[par])
                for c in range(16):
                    bk = 1 + c // 4
                    for dc in range(8):
                        op("pe", lambda e, c=c, dc=dc, bk=bk: e.matmul(
                            pbank[bk][:, (c % 4) * 128:(c % 4 + 1) * 128], lhsT=wq[:, dc, c * 128:(c + 1) * 128],
                            rhs=xnTp[par][:, dc, :], start=(dc == 0), stop=(dc == 7)),
                           R=[wqB, xnTpB[par]], W=[pbB[bk]])
                for b4 in range(4):
                    op("act", lambda e, b4=b4: e.copy(out=qT[:, b4 * 4:(b4 + 1) * 4, :],
                                                      in_=pbank[1 + b4][:, :].rearrange("p (a b) -> p a b", a=4)),
                       R=[pbB[1 + b4]], W=[qTBs[b4]])
                for c in range(16):
                    bk = 1 + c // 4
                    op("pe", lambda e, c=c, bk=bk: e.matmul(
                        pbank[bk][:, (c % 4) * 128:(c % 4 + 1) * 128], lhsT=qT[:, c, :], rhs=sk[:, c, :],
                        start=True, stop=True),
                       R=[qTBs[c // 4], skB], W=[pbB[bk]])
                for b4 in range(4):
                    op("act", lambda e, b4=b4: e.copy(out=sc[:, b4 * 4:(b4 + 1) * 4, :],
                                                      in_=pbank[1 + b4][:, :].rearrange("p (a b) -> p a b", a=4)),
                       R=[pbB[1 + b4]], W=[scBs[b4]])
                for c in range(16):
                    op("dve", lambda e, c=c: e.max(out=mx[:, c, 0:8], in_=sc[:, c, :]), R=[scBs[c // 4]], W=[mxB])
                    op("dve", lambda e, c=c: e.max_index(out=ixu[:, c, 0:8], in_max=mx[:, c, 0:8], in_values=sc[:, c, :]),
                       R=[scBs[c // 4], mxB], W=[ixuB])
                    op("dve", lambda e, c=c: e.match_replace(out=wk[:], in_to_replace=mx[:, c, 0:8], in_values=sc[:, c, :],
                                                             imm_value=-1e30), R=[scBs[c // 4], mxB], W=[wkB])
                    op("dve", lambda e, c=c: e.max(out=mx[:, c, 8:16], in_=wk[:]), R=[wkB], W=[mxB])
                    op("dve", lambda e, c=c: e.max_index(out=ixu[:, c, 8:16], in_max=mx[:, c, 8:16], in_values=wk[:]),
                       R=[wkB, mxB], W=[ixuB])
                op("dve", lambda e: e.tensor_copy(out=ixf[:], in_=ixu[:]), R=[ixuB], W=[ixfB])
                mx4 = mx[:, :, :].rearrange("p (h t) k -> p h t k", t=2)
                ixf4 = ixf[:, :, :].rearrange("p (h t) k -> p h t k", t=2)
                op("dve", lambda e: e.tensor_tensor(
                    out=T4, in0=mx4[:, :, 0, :].unsqueeze(3).broadcast_to([128, 8, 16, 16]),
                    in1=mx4[:, :, 1, :].unsqueeze(2).broadcast_to([128, 8, 16, 16]), op=ALU.add),
                   R=[mxB], W=scBs)
                for h in range(8):
                    cand_h = scflat[:, h * 256:(h + 1) * 256]
                    op("dve", lambda e, h=h, cand_h=cand_h: e.max(out=tv[:, h, 0:8], in_=cand_h), R=scBs, W=[tvB])
                    op("dve", lambda e, h=h, cand_h=cand_h: e.max_index(out=pos[:, h, 0:8], in_max=tv[:, h, 0:8],
                                                                        in_values=cand_h),
                       R=scBs + [tvB], W=[posB])
                    op("dve", lambda e, h=h, cand_h=cand_h: e.match_replace(out=cwk[:], in_to_replace=tv[:, h, 0:8],
                                                                            in_values=cand_h, imm_value=-1e30),
                       R=scBs + [tvB], W=[cwkB])
                    op("dve", lambda e, h=h: e.max(out=tv[:, h, 8:16], in_=cwk[:]), R=[cwkB], W=[tvB])
                    op("dve", lambda e, h=h: e.max_index(out=pos[:, h, 8:16], in_max=tv[:, h, 8:16], in_values=cwk[:]),
                       R=[cwkB, tvB], W=[posB])
                op("dve", lambda e: e.tensor_tensor(out=ee[:], in0=tv[:],
                                                    in1=tv[:, :, 0].unsqueeze(2).broadcast_to([128, 8, 16]),
                                                    op=ALU.subtract), R=[tvB], W=[eeB])
                op("act", lambda e: e.activation(out=ee[:], in_=ee[:], func=AF.Exp), R=[eeB], W=[eeB])
                op("dve", lambda e: e.tensor_single_scalar(out=pa[:], in_=pos[:], scalar=4, op=ALU.logical_shift_right),
                   R=[posB], W=[paB])
                op("dve", lambda e: e.tensor_single_scalar(out=pb_[:], in_=pos[:], scalar=15, op=ALU.bitwise_and),
                   R=[posB], W=[pbB_])
                op("dve", lambda e: e.tensor_copy(out=paf[:], in_=pa[:]), R=[paB], W=[pafB])
                op("dve", lambda e: e.tensor_copy(out=pbf[:], in_=pb_[:]), R=[pbB_], W=[pbfB])
                io4 = iota16[:, :].unsqueeze(1).unsqueeze(1).broadcast_to([128, 8, 16, 16])
                for (rk, rkB, side, dst, dstB) in ((paf, pafB, 0, s0, s0B), (pbf, pbfB, 1, s1, s1B)):
                    op(RG_ENG, lambda e, rk=rk: e.tensor_tensor(
                        out=T4, in0=io4, in1=rk[:, :, :].unsqueeze(3).broadcast_to([128, 8, 16, 16]), op=ALU.is_equal),
                       R=[iotaB, rkB], W=scBs)
                    op(RG_ENG, lambda e, side=side: e.tensor_tensor(
                        out=T4, in0=T4, in1=ixf4[:, :, side, :].unsqueeze(2).broadcast_to([128, 8, 16, 16]), op=ALU.mult),
                       R=scBs + [ixfB], W=scBs)
                    op("dve", lambda e, dst=dst: e.tensor_reduce(out=dst[:], in_=T4, axis=AX.X, op=ALU.add),
                       R=scBs, W=[dstB])
                op("dve", lambda e: e.scalar_tensor_tensor(
                    out=idxf[:, :].rearrange("p (h k) -> p h k", h=8), in0=s0[:], scalar=128.0, in1=s1[:],
                    op0=ALU.mult, op1=ALU.add), R=[s0B, s1B], W=[idxfB])
                op("dve", lambda e: e.tensor_copy(out=idxu[par][:], in_=idxf[:]), R=[idxfB], W=[idxuB[par]])
                op("dve", lambda e: e.tensor_reduce(out=zz[:], in_=ee[:], axis=AX.X, op=ALU.add), R=[eeB], W=[zzB[0]])
                op("dve", lambda e: e.reciprocal(out=rz[:], in_=zz[:]), R=[zzB[0]], W=[rzB])
                op("dve", lambda e: e.tensor_tensor(
                    out=gg[par][:, :].rearrange("p (h k) -> p h k", h=8), in0=ee[:],
                    in1=rz[:, :].unsqueeze(2).broadcast_to([128, 8, 16]), op=ALU.mult),
                   R=[eeB, rzB], W=[ggB[par]])

            def back(ti, i, pending):
                par = ti % 2
                for j in range(128):
                    ndve = 2 if (j % 2 == 1 and j >= 24) else 1
                    nother = 12
                    while pending:
                        en_ = pending[0][0]
                        if en_ == "dve":
                            if ndve == 0:
                                break
                            ndve -= 1
                        else:
                            if nother == 0:
                                break
                            nother -= 1
                        pending.pop(0)[1]()
                    b = (ti * 128 + j) % NBUF_G
                    op("pool", lambda e, b=b, j=j: e.indirect_dma_start(
                        out=gbuf[b][:], out_offset=None, in_=uvb_d[layer][:, :],
                        in_offset=bass.IndirectOffsetOnAxis(ap=idxu[par][:, j:j + 1], axis=0)),
                       R=[idxuB[par], uvbB[layer]], W=[gbufB[b]], lane="g%d" % b)
                    routed = PE_DOT > 0 and (j % PE_DOT == 0) and (PE_DOT % 2 == 0)
                    if routed:
                        kk = (j // PE_DOT) % 2
                        for dc in range(8):
                            op("pe", lambda e, b=b, dc=dc: e.transpose(out=pv7[:, dc, :], in_=gbuf[b][:, dc * 128:(dc + 1) * 128],
                                                                       identity=ident[:]),
                               R=[gbufB[b], identB], W=[pbB[7]])
                        op("act", lambda e, kk=kk: e.copy(out=ugT[kk][:], in_=pv7), R=[pbB[7]], W=[ugTB[kk]])
                    else:
                        op("dve", lambda e, b=b, j=j: e.scalar_tensor_tensor(
                            out=junk2s[j % 2][:], in0=gbuf[b][:, 0:D], scalar=1.0, in1=xn[par][:], op0=ALU.mult,
                            op1=ALU.mult, accum_out=hh[:, par, j:j + 1]),
                           R=[gbufB[b], xnB[par]], W=[hhB[par][j], junk2B[j % 2]])
                    if PE_DOT > 0 and (PE_DOT % 2 == 0) and (j % PE_DOT == 1):
                        j0 = j - 1
                        kk = (j0 // PE_DOT) % 2
                        for dc in range(8):
                            op("pe", lambda e, kk=kk, dc=dc: e.matmul(
                                pbank[0][:, 0:128], lhsT=ugT[kk][:, dc, :], rhs=xnTp[par][:, dc, :],
                                start=(dc == 0), stop=(dc == 7)),
                               R=[ugTB[kk], xnTpB[par]], W=[pbB[0]])
                        op("dve", lambda e, j0=j0: e.scalar_tensor_tensor(
                            out=junk2s[0][:, 0:128], in0=pbank[0][:, 0:128], scalar=1.0, in1=identf[:], op0=ALU.mult,
                            op1=ALU.mult, accum_out=hh[:, par, j0:j0 + 1]),
                           R=[pbB[0], identfB], W=[hhB[par][j0], junk2B[0]])
                    if j % JG == JG - 1:
                        g0 = j - (JG - 1)
                        gi = g0 // JG
                        op("act", lambda e, g0=g0: e.activation(out=gl[:, par, g0:g0 + JG], in_=hh[:, par, g0:g0 + JG],
                                                                func=AF.Gelu),
                           R=[hhB[par][g0 + t] for t in range(JG)], W=[glB[par][gi]])
                        dpar = gi % 2
                        for t in range(JG):
                            jj = g0 + t
                            ds = dpar * JG + t
                            op("act", lambda e, jj=jj: e.activation(
                                out=aa[:, par, jj:jj + 1], in_=gl[:, par, jj:jj + 1], func=AF.Copy,
                                scale=gg[par][:, jj:jj + 1]),
                               R=[glB[par][gi], ggB[par]], W=[aaB[par][gi]] if t == 0 else [aaB2[par][gi]])
                            op("act", lambda e, jj=jj, ds=ds: e.activation(
                                out=dd[:, ds, :], in_=identf[:], func=AF.Copy, scale=aa[:, par, jj:jj + 1]),
                               R=[aaB[par][gi] if t == 0 else aaB2[par][gi], identfB], W=[ddB[ds]])
                        for t in range(JG):
                            jj = g0 + t
                            ds = dpar * JG + t
                            bb = (ti * 128 + jj) % NBUF_G
                            for half in range(2):
                                op("pe", lambda e, ds=ds, bb=bb, half=half, jj=jj: e.matmul(
                                    pbank[5 + half][:, :], lhsT=dd[:, ds, :],
                                    rhs=gbuf[bb][:, D + half * 512: D + (half + 1) * 512],
                                    start=(jj == 0), stop=(jj == 127)),
                                   R=[ddB[ds], gbufB[bb]], W=[pbB[5 + half]])
                while pending:
                    pending.pop(0)[1]()
                for half in range(2):
                    op("dve", lambda e, half=half: e.tensor_tensor(
                        out=xres[:, i, half * 512:(half + 1) * 512], in0=xres[:, i, half * 512:(half + 1) * 512],
                        in1=pbank[5 + half][:, :], op=ALU.add),
                       R=[pbB[5 + half], xresB[i]], W=[xresB[i]])
                if final:
                    xr = xres[:, i, :]
                    r, rB = emit_rstd(xr, xresB[i])
                    ytv = scflat[:, 0:D]
                    op("dve", lambda e: e.scalar_tensor_tensor(out=ytv, in0=xr, scalar=r, in1=gA[:],
                                                               op0=ALU.mult, op1=ALU.mult),
                       R=[xresB[i], rB, gAB], W=scBs)
                    o_ = i - 1
                    op("sp", lambda e: e.dma_start(out=y_d[o_ * 128:(o_ + 1) * 128, :], in_=ytv),
                       R=scBs, lane="yst%d" % (o_ % 2))

            n = len(tiles)
            front(0, tiles[0])
            for ti in range(n):
                pending = []
                if ti + 1 < n:
                    P.deferred = pending
                    front(ti + 1, tiles[ti + 1])
                    P.deferred = None
                back(ti, tiles[ti], pending)

        emit_convert(0)
        if upto >= 2:
            emit_peer(0, list(range(NT1)))

        if upto >= 3:
            fence = P.fence()
            A = AP0.fork()
            watt = A.alloc("watt", [128, 8, 1792], BF16)
            wattB = Buf()
            wo = A.alloc("wo", [128, 8, D], BF16)
            woB = Buf()
            kT = A.alloc("kT", [128, 4, NT1 * 128], BF16)
            kTB = [Buf() for _ in range(NT1)]
            vv = A.alloc("vv", [128, NT1, 256], BF16)
            vvB = [Buf() for _ in range(NT1)]
            biasm = A.alloc("biasm", [128, 16, 384], F32)
            biasmB = Buf()

            sinkbc = A.alloc("sinkbc", [128, 16], F32)
            sinkB = Buf()
            penb = A.alloc("penb", [1, NT1 * 128], BF16)
            penB = Buf()
            ones1 = A.alloc("ones1", [1, 128], BF16)
            ones1B = Buf()
            qTas = [A.alloc("qTa%d" % i, [128, 8, 128], BF16) for i in range(2)]
            qTaBss = [[Buf() for _ in range(2)] for _ in range(2)]
            LL = [A.alloc("LL%d" % i, [128, 384], F32) for i in range(2)]
            LLB = [Buf() for _ in range(2)]
            EE = [A.alloc("EE%d" % i, [128, 384], BF16) for i in range(2)]
            EEB = [Buf() for _ in range(2)]
            ET = [A.alloc("ET%d" % i, [128, 3, 128], BF16) for i in range(2)]
            ETB = [Buf() for _ in range(2)]
            mrow = A.alloc("mrow", [128, 16], F32)
            mrowB = [Buf() for _ in range(16)]
            nmrow = A.alloc("nmrow", [128, 16], F32)
            nmrowB = [Buf() for _ in range(16)]
            rsum = A.alloc("rsum", [128, 16], F32)
            rsumB = [Buf() for _ in range(16)]
            esink = A.alloc("esink", [128, 16], F32)
            esinkB = [Buf() for _ in range(16)]
            den = A.alloc("den", [128, 16], F32)
            denB = Buf()
            rden = A.alloc("rden", [128, 16], F32)
            rdenB = Buf()
            ao_off = (A.p + 31) // 32 * 32
            ao = A.alloc("ao", [128, D], BF16)
            aoBs = [Buf() for _ in range(2)]
            wmask = nc.alloc_sbuf_tensor_at("wmask", [128, 384], F32, offset=ao_off)
            wmaskB = aoBs[0]
            aoT = A.alloc("aoT", [128, 8, 128], BF16)
            aoTB = Buf()

            for k in range(4):
                lo, hi = k * 448, (k + 1) * 448
                op("pool", lambda e, lo=lo, hi=hi: e.dma_start(
                    out=watt[:, :, lo:hi], in_=w_att_d[:, lo:hi].rearrange("(dc dp) n -> dp dc n", dp=128)),
                   W=[wattB], lane="w%d" % k, extra=fence)
            for k in range(2):
                op("pool", lambda e, k=k: e.dma_start(
                    out=wo[:, :, k * 512:(k + 1) * 512],
                    in_=w_o_d[:, k * 512:(k + 1) * 512].rearrange("(dc dp) n -> dp dc n", dp=128)),
                   W=[woB], lane="w%d" % (2 + k), extra=fence)
            op("pool", lambda e: e.dma_start(out=penb[:], in_=pen_d[:, :]), W=[penB], lane="w1", extra=fence)
            op("sp", lambda e: e.dma_start(out=biasm[:], in_=bias_d.rearrange("p (h k) -> p h k", h=16)),
               W=[biasmB], lane="c_bias", extra=fence)
            op("sp", lambda e: e.dma_start(out=wmask[:], in_=wmask_d[:, :]), W=[wmaskB], lane="c_wmask", extra=fence)
            op("sp", lambda e: e.dma_start(out=sinkbc[:], in_=sink_d.partition_broadcast(128)), W=[sinkB], lane="c_sink",
               extra=fence)
            op("dve", lambda e: e.tensor_tensor(out=biasm[:], in0=biasm[:],
                                                in1=wmask[:, :].unsqueeze(1).broadcast_to([128, 16, 384]), op=ALU.add),
               R=[wmaskB, biasmB], W=[biasmB])
            op("dve", lambda e: e.tensor_scalar(out=biasm[:], in0=biasm[:], scalar1=-1.0, scalar2=None, op0=ALU.mult),
               R=[biasmB], W=[biasmB])
            op("dve", lambda e: e.memset(ones1[:], 1.0), W=[ones1B], extra=fence)
            load_gain(gA, gAB, 2)

            def norm_T(i, bank=0):
                xr = xres[:, i, :]
                r, rB = emit_rstd(xr, xresB[i])
                op("dve", lambda e: e.scalar_tensor_tensor(out=xnb[:], in0=xr, scalar=r, in1=gA[:],
                                                           op0=ALU.mult, op1=ALU.mult),
                   R=[xresB[i], rB, gAB], W=[xnbB])
                emit_transpose8(xnb, xnbB, xnT[:], xnTB, bank=bank)

            for i in range(NT1):
                if i >= 2:
                    emit_convert(1, 1)
                norm_T(i)
                for g in range(4):
                    for dc in range(8):
                        op("pe", lambda e, g=g, dc=dc: e.matmul(
                            pbank[1][:, g * 128:(g + 1) * 128], lhsT=watt[:, dc, 1024 + g * 128: 1024 + (g + 1) * 128],
                            rhs=xnT[:, dc, :], start=(dc == 0), stop=(dc == 7)),
                           R=[wattB, xnTB], W=[pbB[1]])
                op("act", lambda e, i=i: e.copy(out=kT[:, :, i * 128:(i + 1) * 128],
                                                in_=pbank[1][:, :].rearrange("p (g t) -> p g t", g=4)),
                   R=[pbB[1]], W=[kTB[i]])
                for dc in range(8):
                    op("pe", lambda e, dc=dc: e.matmul(pbank[2][:, 0:256], lhsT=xnT[:, dc, :], rhs=watt[:, dc, 1536:1792],
                                                       start=(dc == 0), stop=(dc == 7)),
                       R=[wattB, xnTB], W=[pbB[2]])
                op("act", lambda e, i=i: e.copy(out=vv[:, i, :], in_=pbank[2][:, 0:256]), R=[pbB[2]], W=[vvB[i]])

            def pre(o_):
                i = o_ + 1
                qq = qTas[o_ % 2]
                norm_T(i, bank=1)
                for cq in range(8):
                    bk = 1 + cq // 4
                    for dc in range(8):
                        op("pe", lambda e, cq=cq, dc=dc, bk=bk: e.matmul(
                            pbank[bk][:, (cq % 4) * 128:(cq % 4 + 1) * 128], lhsT=watt[:, dc, cq * 128:(cq + 1) * 128],
                            rhs=xnT[:, dc, :], start=(dc == 0), stop=(dc == 7)),
                           R=[wattB, xnTB], W=[pbB[bk]])
                for b4 in range(2):
                    op("act", lambda e, b4=b4: e.copy(out=qq[:, b4 * 4:(b4 + 1) * 4, :],
                                                      in_=pbank[1 + b4][:, :].rearrange("p (a b) -> p a b", a=4)),
                       R=[pbB[1 + b4]], W=[qTaBss[o_ % 2][b4]])

            def head(o_, pend):
                i = o_ + 1
                qTa = qTas[o_ % 2]
                qTaBs = qTaBss[o_ % 2]
                edge = (o_ == 0) or (o_ == NO - 1)
                npull = (len(pend) + 15) // 16

                def st_S(h):
                    cq, hf, g = h // 2, h % 2, h // 4
                    q = h % 2
                    bk = 3 + q
                    ps_s = pbank[bk][:, 0:384]
                    op("pe", lambda e: e.matmul(
                        ps_s, lhsT=qTa[64 * hf:64 * hf + 64, cq, :],
                        rhs=kT[64 * hf:64 * hf + 64, g, (i - 1) * 128:(i + 2) * 128], start=True, stop=not edge),
                       R=[qTaBs[cq // 4], kTB[i - 1], kTB[i], kTB[i + 1]], W=[pbB[bk]])
                    if edge:
                        op("pe", lambda e: e.matmul(
                            ps_s, lhsT=ones1[0:1, :], rhs=penb[0:1, (i - 1) * 128:(i + 2) * 128], start=False, stop=True),
                           R=[ones1B, penB], W=[pbB[bk]])
                    op("dve", lambda e: e.scalar_tensor_tensor(
                        out=LL[q][:], in0=ps_s, scalar=-0.125, in1=biasm[:, h, :], op0=ALU.mult, op1=ALU.add),
                       R=[pbB[bk], biasmB], W=[LLB[q]])
                    op("dve", lambda e: e.tensor_reduce(out=nmrow[:, h:h + 1], in_=LL[q][:], axis=AX.X, op=ALU.min),
                       R=[LLB[q]], W=[nmrowB[h]])
                    op("act", lambda e: e.activation(out=EE[q][:], in_=LL[q][:], func=AF.Exp, scale=-1.0,
                                                     bias=nmrow[:, h:h + 1], accum_out=rsum[:, h:h + 1]),
                       R=[LLB[q], nmrowB[h]], W=[EEB[q], rsumB[h]])
                    op("act", lambda e: e.activation(out=esink[:, h:h + 1], in_=nmrow[:, h:h + 1], func=AF.Exp,
                                                     bias=sinkbc[:, h:h + 1], scale=1.0),
                       R=[nmrowB[h], sinkB], W=[esinkB[h]])

                def st_T(h):
                    q = h % 2
                    ptbank = 5 if q == 0 else 0
                    pt = pbank_bf[ptbank][:, 0:384].rearrange("p (a b) -> p a b", a=3)
                    ptB = pbB[ptbank]
                    for kb in range(3):
                        op("pe", lambda e, kb=kb: e.transpose(out=pt[:, kb, :], in_=EE[q][:, kb * 128:(kb + 1) * 128],
                                                              identity=ident[:]),
                           R=[EEB[q], identB], W=[ptB])
                    op("act", lambda e: e.copy(out=ET[q][:], in_=pt), R=[ptB], W=[ETB[q]])

                def st_V(h):
                    g = h // 4
                    q = h % 2
                    bko = 6 + h // 8
                    for kb in range(3):
                        op("pe", lambda e, kb=kb: e.matmul(
                            pbank[bko][:, (h % 8) * 64:(h % 8 + 1) * 64], lhsT=ET[q][:, kb, :],
                            rhs=vv[:, i - 1 + kb, g * 64:(g + 1) * 64], start=(kb == 0), stop=(kb == 2)),
                           R=[ETB[q], vvB[i - 1 + kb]], W=[pbB[bko]])

                for k_ in range(16 + 2):
                    if k_ < 16:
                        st_S(k_)
                    if 0 <= k_ - 1 < 16:
                        st_T(k_ - 1)
                    if 0 <= k_ - 2 < 16:
                        st_V(k_ - 2)
                    for _ in range(npull):
                        if pend:
                            pend.pop(0)[1]()
                while pend:
                    pend.pop(0)[1]()

            def tail_now(o_):
                op("dve", lambda e: e.tensor_tensor(out=den[:], in0=rsum[:], in1=esink[:], op=ALU.add),
                   R=rsumB + esinkB, W=[denB])
                op("dve", lambda e: e.reciprocal(out=rden[:], in_=den[:]), R=[denB], W=[rdenB])
                for hb in range(2):
                    op("dve", lambda e, hb=hb: e.tensor_tensor(
                        out=ao[:, hb * 512:(hb + 1) * 512].rearrange("p (h d) -> p h d", h=8),
                        in0=pbank[6 + hb][:, :].rearrange("p (h d) -> p h d", h=8),
                        in1=rden[:, hb * 8:(hb + 1) * 8].unsqueeze(2).broadcast_to([128, 8, 64]), op=ALU.mult),
                       R=[pbB[6 + hb], rdenB], W=[aoBs[hb]])

            def tail_def(o_):
                i = o_ + 1
                emit_transpose8(ao, aoBs, aoT[:], aoTB, bank=2)
                for half in range(2):
                    bk = 1 + half
                    for cc in range(8):
                        op("pe", lambda e, half=half, cc=cc, bk=bk: e.matmul(
                            pbank[bk][:, :], lhsT=aoT[:, cc, :], rhs=wo[:, cc, half * 512:(half + 1) * 512],
                            start=(cc == 0), stop=(cc == 7)),
                           R=[aoTB, woB], W=[pbB[bk]])
                    op("dve", lambda e, half=half, bk=bk: e.tensor_tensor(
                        out=xres[:, i, half * 512:(half + 1) * 512], in0=xres[:, i, half * 512:(half + 1) * 512],
                        in1=pbank[bk][:, :], op=ALU.add),
                       R=[pbB[bk], xresB[i]], W=[xresB[i]])

            pre(0)
            pend_tail = []
            for o_ in range(NO):
                pend = pend_tail
                if o_ + 1 < NO:
                    P.deferred = []
                    pre(o_ + 1)
                    pend = pend + P.deferred
                    P.deferred = None
                head(o_, pend)
                tail_now(o_)
                P.deferred = pend_tail = []
                tail_def(o_)
                P.deferred = None
            for _, th in pend_tail:
                th()

        emit_convert(1)
        if upto >= 4:
            emit_peer(1, list(range(1, NO + 1)), final=True)

        if dbg:
            for i in range(NT1):
                op("sp", lambda e, i=i: e.dma_start(out=dbg_d[i * 128:(i + 1) * 128, :], in_=xres[:, i, :]),
                   R=[xresB[i]], lane="dbg")
        P.raw("sp", lambda e: e.nop(), deps=P.fence())
        P.build()
    return nc


def _t5_bucket_np(rel):
    half = 16
    max_exact = 8
    ret = np.where(rel > 0, half, 0)
    n = np.abs(rel)
    nf = np.maximum(n, 1).astype(np.float32)
    large = max_exact + (np.log(nf / max_exact) / np.float32(np.log(128 / max_exact)) * (half - max_exact)).astype(np.int32)
    large = np.minimum(large, half - 1)
    return ret + np.where(n < max_exact, n, large)


def prep_shared(inp):
    f = lambda a: np.ascontiguousarray(np.asarray(a, dtype=np.float32))
    sh = {}
    sh["w_in"] = f(inp["conv_w_in"][0])
    cwv = np.asarray(inp["conv_w"][0], np.float32)
    sh["cw"] = f(cwv.T.reshape(8, 128, 3).transpose(1, 0, 2).reshape(128, 24))
    sh["w_out"] = f(inp["conv_w_out"][0])
    wqkv = np.asarray(inp["attn_w_qkv"][0], np.float32)
    wq_, wk_, wv_ = wqkv[:, :1024], wqkv[:, 1024:1280], wqkv[:, 1280:1536]
    kd = []
    for g in range(4):
        kd += [wk_[:, g * 64:(g + 1) * 64], wk_[:, g * 64:(g + 1) * 64]]
    sh["w_att"] = f(np.concatenate([wq_] + kd + [wv_], axis=1))
    sh["w_o"] = f(inp["attn_w_o"][0])
    sh["sink"] = f(np.asarray(inp["attn_sink"][0]).reshape(1, 16))
    qi = np.arange(128)[:, None]
    kj = np.arange(384)[None, :]
    rel = kj - 128 - qi
    bk = _t5_bucket_np(rel)
    rb = np.asarray(inp["rel_bias"], np.float32)
    sh["bias_tab"] = f(rb[bk].transpose(0, 2, 1).reshape(128, 16 * 384))
    sh["wmask"] = f(np.where(np.abs(rel) <= 128, 0.0, -30000.0))
    sh["gains"] = f(np.stack([inp["conv_norm_g"][0], inp["ffn_norm_g"][0], inp["attn_norm_g"][0],
                              inp["ffn_norm_g"][1], inp["final_norm_g"]], axis=0))
    sh["w_pq"] = f(inp["peer_w_q"])
    sk = np.asarray(inp["peer_subkeys"], np.float32)
    sh["skT"] = f(sk.transpose(0, 4, 1, 2, 3).reshape(2, 128, 2048))
    for l in range(2):
        sh["uv%d" % l] = f(np.concatenate([np.asarray(inp["peer_u"][l], np.float32),
                                            np.asarray(inp["peer_v"][l], np.float32)], axis=1))
    sh["iota"] = f(np.broadcast_to(np.arange(16, dtype=np.float32), (128, 16)))
    return sh


def prep_core(x, b, k, NO, S):
    NT1, NX = NO + 2, NO + 4
    own0 = k * NO * 128
    lo = own0 - 256
    xe = np.zeros((NX * 128, D), np.float32)
    a, e = max(lo, 0), min(lo + NX * 128, S)
    xe[a - lo:e - lo] = x[b, a:e]
    pen = np.zeros((1, NT1 * 128), np.float32)
    t = own0 - 128 + np.arange(NT1 * 128)
    pen[0, (t < 0) | (t >= S)] = -240000.0
    return {"x_ext": xe, "pen": pen}


_NC_CACHE = {}


def kernel(**inputs):
    x = np.asarray(inputs["x"], np.float32)
    B, S, _ = x.shape
    NO = 16
    ncores = 8
    per_b = ncores // B
    sh = prep_shared(inputs)
    in_maps = []
    for c in range(ncores):
        b, k = c // per_b, c % per_b
        m = dict(sh)
        m.update(prep_core(x, b, k, NO, S))
        in_maps.append(m)
    nc = build(NO=NO)
    res = run_bass_kernel_spmd(nc, in_maps, core_ids=list(range(ncores)))
    out = np.zeros((B, S, D), np.float32)
    for c in range(ncores):
        b, k = c // per_b, c % per_b
        out[b, k * NO * 128:(k + 1) * NO * 128] = res.results[c]["y"]
    return out
```
